# Optimizing a Trainium2 kernel written in Bass

```python
import math
import jax, jax.numpy as jnp
from jax import lax
import numpy as np

D_MODEL = 1024
BATCH = 16
SEQ = 4096
DEPTH = 2
DEC_BATCH = 8
DEC_SEQ = 8192
PAST_LEN = 128

MLA_HEADS = 8
MLA_NOPE = 64
MLA_ROPE = 32
MLA_V = 64
MLA_QK = MLA_NOPE + MLA_ROPE
Q_LORA = 768
KV_LORA = 256
ROPE_THETA = 10000.0
Q_BLOCK = 128
RWKV_HEAD = 64
RWKV_HEADS = 8
RWKV_DIM = RWKV_HEADS * RWKV_HEAD
DECAY_LORA = 64
AAA_LORA = 64
MV_LORA = 32
GATE_LORA = 128
DECAY_SCALE = 0.606531
GN_EPS = 64e-5
D_FF = 2816
CONV_W = 3
P_DIM = 256
ALPHA = (2 * DEPTH) ** 0.25
BETA = (8 * DEPTH) ** -0.25
LN_EPS = 1e-5
RMS_EPS = 1e-6

RWKV_COLS = 3 * RWKV_DIM + 2 * DECAY_LORA + 2 * AAA_LORA + GATE_LORA
OFF_CKV = Q_LORA
OFF_KR = OFF_CKV + KV_LORA
OFF_RWKV = OFF_KR + MLA_ROPE
OFF_GA = OFF_RWKV + RWKV_COLS
OFF_GB = OFF_GA + D_MODEL
IN_COLS = OFF_GB + D_MODEL

kernel_name = 'hybrid_mla_rwkv7_deepnorm_encoder'


def _layernorm(x, g, b):
    xf = x.astype(jnp.float32)
    mu = jnp.mean(xf, -1, keepdims=True)
    var = jnp.mean(jnp.square(xf - mu), -1, keepdims=True)
    return ((xf - mu) * lax.rsqrt(var + LN_EPS) * g.astype(jnp.float32) + b.astype(jnp.float32)).astype(x.dtype)


def _rmsnorm(x, g):
    xf = x.astype(jnp.float32)
    return (xf * lax.rsqrt(jnp.mean(xf * xf, -1, keepdims=True) + RMS_EPS) * g.astype(jnp.float32)).astype(x.dtype)


def _shift_prev(z):
    return jnp.pad(z[:, :-1], ((0, 0), (1, 0), (0, 0)))


def _shift_next(z):
    return jnp.pad(z[:, 1:], ((0, 0), (0, 1), (0, 0)))


def _rope(x, cos, sin):
    half = x.shape[-1] // 2
    x1, x2 = x[..., :half], x[..., half:]
    return jnp.concatenate([x1 * cos - x2 * sin, x1 * sin + x2 * cos], axis=-1)


def _dwconv3(u, w, b):
    y = lax.conv_general_dilated(u, w[:, None, :].astype(u.dtype), window_strides=(1,), padding=((1, 1),),
                                 dimension_numbers=('NWC', 'WIO', 'NWC'), feature_group_count=u.shape[-1])
    return y + b.astype(u.dtype)


def _mla(c_q, c_kv, k_r, q_norm_g, kv_norm_g, w_uq, w_ukv):
    B, T, _ = c_q.shape
    dt = c_q.dtype
    pos = jnp.arange(T, dtype=jnp.float32)
    inv_freq = ROPE_THETA ** (-jnp.arange(0, MLA_ROPE, 2, dtype=jnp.float32) / MLA_ROPE)
    ang = pos[:, None] * inv_freq[None, :]
    cos, sin = jnp.cos(ang).astype(dt), jnp.sin(ang).astype(dt)
    q = (_rmsnorm(c_q, q_norm_g) @ w_uq).reshape(B, T, MLA_HEADS, MLA_QK)
    q_nope = q[..., :MLA_NOPE]
    q_rope = _rope(q[..., MLA_NOPE:], cos[:, None], sin[:, None])
    kv = (_rmsnorm(c_kv, kv_norm_g) @ w_ukv).reshape(B, T, MLA_HEADS, MLA_NOPE + MLA_V)
    k_nope, v = kv[..., :MLA_NOPE], kv[..., MLA_NOPE:]
    k_rope = _rope(k_r, cos, sin)
    scale = MLA_QK ** -0.5
    nblk = T // Q_BLOCK
    qn = q_nope.reshape(B, nblk, Q_BLOCK, MLA_HEADS, MLA_NOPE).swapaxes(0, 1)
    qr = q_rope.reshape(B, nblk, Q_BLOCK, MLA_HEADS, MLA_ROPE).swapaxes(0, 1)

    def block(args):
        qn_b, qr_b = args
        s = (jnp.einsum('bqhd,bkhd->bhqk', qn_b, k_nope)
             + jnp.einsum('bqhr,bkr->bhqk', qr_b, k_rope)).astype(jnp.float32) * scale
        p = jax.nn.softmax(s, axis=-1).astype(v.dtype)
        return jnp.einsum('bhqk,bkhd->bqhd', p, v)

    o = lax.map(block, (qn, qr))
    return o.swapaxes(0, 1).reshape(B, T, MLA_HEADS * MLA_V)


def _rwkv_scan(r, w, k, v, kk, kka):
    S0 = jnp.zeros(r.shape[1:] + (RWKV_HEAD,), jnp.float32)

    def step(S, inp):
        r_t, w_t, k_t, v_t, kk_t, kka_t = inp
        sa = jnp.einsum('dbhij,dbhj->dbhi', S, kk_t)
        S = S * w_t[..., None, :] - sa[..., :, None] * kka_t[..., None, :] + v_t[..., :, None] * k_t[..., None, :]
        y = jnp.einsum('dbhij,dbhj->dbhi', S, r_t)
        return S, y

    _, y = lax.scan(step, S0, (r, w, k, v, kk, kka))
    return y


def _rwkv_time_mix(z, x, v_first, mu, w0, w_up, a0, a_up, g_up, k_k, k_a, r_k, lnx_g, lnx_b, v_mix):
    B, T, _ = z.shape
    C, H, N = RWKV_DIM, RWKV_HEADS, RWKV_HEAD
    f32 = jnp.float32
    z = z + mu * (0.5 * (_shift_prev(z) + _shift_next(z)) - z)
    r = z[..., 0:C]
    k = z[..., C:2 * C]
    v = z[..., 2 * C:3 * C]
    o = 3 * C
    zw = z[..., o:o + 2 * DECAY_LORA].reshape(B, T, 2, DECAY_LORA)
    o += 2 * DECAY_LORA
    za = z[..., o:o + 2 * AAA_LORA].reshape(B, T, 2, AAA_LORA)
    o += 2 * AAA_LORA
    zg = z[..., o:o + GATE_LORA]
    decay = jnp.exp(-DECAY_SCALE * jax.nn.sigmoid(
        (w0 + jnp.einsum('btdl,dlc->btdc', jnp.tanh(zw), w_up)).astype(f32)))
    a = jax.nn.sigmoid(a0 + jnp.einsum('btdl,dlc->btdc', za, a_up)).astype(f32)
    g = jax.nn.sigmoid(zg) @ g_up
    if v_mix is None:
        v_first = v
    else:
        v0, v_down, v_upm = v_mix
        v = v + (v_first - v) * jax.nn.sigmoid(v0 + (x @ v_down) @ v_upm)
    kkh = (k * k_k).reshape(B, T, H, N).astype(f32)
    kkh = kkh * lax.rsqrt(jnp.sum(kkh * kkh, -1, keepdims=True) + 1e-12)
    kk = kkh.reshape(B, T, C)
    k_dir = k.astype(f32)[:, :, None, :] * (1.0 + (a - 1.0) * k_a.astype(f32))
    kka = kk[:, :, None, :] * a

    def dirs(u):
        u = jnp.stack([u[:, :, 0], jnp.flip(u[:, :, 1], axis=1)], axis=0).astype(f32)
        return u.reshape(2, B, T, H, N).transpose(2, 0, 1, 3, 4)

    def both(u):
        return dirs(jnp.broadcast_to(u[:, :, None, :], (B, T, 2, C)))

    y = _rwkv_scan(both(r), dirs(decay), dirs(k_dir), both(v), both(kk), dirs(kka))
    y = (y[:, 0] + jnp.flip(y[:, 1], axis=0)).transpose(1, 0, 2, 3)
    mean = jnp.mean(y, -1, keepdims=True)
    var = jnp.mean(jnp.square(y - mean), -1, keepdims=True)
    yn = ((y - mean) * lax.rsqrt(var + GN_EPS)).reshape(B, T, C) * lnx_g.astype(f32) + lnx_b.astype(f32)
    rh = r.reshape(B, T, H, N).astype(f32)
    ksh = jnp.sum(k_dir, axis=2).reshape(B, T, H, N)
    vh = v.reshape(B, T, H, N).astype(f32)
    bonus = jnp.sum(rh * ksh * r_k.reshape(H, N).astype(f32), -1, keepdims=True) * vh
    out = ((yn + bonus.reshape(B, T, C)) * g.astype(f32)).astype(x.dtype)
    return out, v_first


def setup_inputs(seed: int = 0) -> dict:
    key = jax.random.key(seed)
    ks = iter(jax.random.split(key, 48))
    f32 = jnp.float32

    def nrm(shape, scale):
        return jax.random.normal(next(ks), shape, f32) * scale

    def gain(shape):
        return 1.0 + nrm(shape, 0.02)

    L = DEPTH
    C = RWKV_DIM
    return {
        'x_prompt': nrm((BATCH, SEQ, D_MODEL), 1.0),
        'x_sample': nrm((DEC_BATCH, DEC_SEQ, D_MODEL), 1.0),
        'p_prompt': nrm((DEPTH, BATCH, SEQ, P_DIM), 1.0),
        'p_sample': nrm((DEPTH, DEC_BATCH, DEC_SEQ, P_DIM), 1.0),
        'w_in': nrm((L, D_MODEL, IN_COLS), D_MODEL ** -0.5),
        'q_norm_g': gain((L, Q_LORA)),
        'kv_norm_g': gain((L, KV_LORA)),
        'w_uq': nrm((L, Q_LORA, MLA_HEADS * MLA_QK), Q_LORA ** -0.5),
        'w_ukv': nrm((L, KV_LORA, MLA_HEADS * (MLA_NOPE + MLA_V)), KV_LORA ** -0.5),
        'tshift_mu': jax.random.uniform(next(ks), (L, RWKV_COLS), f32),
        'w0': -2.0 + nrm((L, 2, C), 0.5),
        'w_lora_up': nrm((L, 2, DECAY_LORA, C), 0.1),
        'a0': nrm((L, 2, C), 0.5),
        'a_lora_up': nrm((L, 2, AAA_LORA, C), 0.1),
        'g_lora_up': nrm((L, GATE_LORA, C), GATE_LORA ** -0.5),
        'k_k': 0.85 + nrm((L, C), 0.05),
        'k_a': gain((L, C)),
        'r_k': nrm((L, C), 0.1),
        'v0': 1.0 + nrm((L - 1, C), 0.1),
        'v_lora_down': nrm((L - 1, D_MODEL, MV_LORA), D_MODEL ** -0.5),
        'v_lora_up': nrm((L - 1, MV_LORA, C), MV_LORA ** -0.5),
        'lnx_g': gain((L, C)),
        'lnx_b': nrm((L, C), 0.02),
        'w_pa': nrm((L, MLA_HEADS * MLA_V, D_MODEL), (MLA_HEADS * MLA_V) ** -0.5),
        'w_pb': nrm((L, C, D_MODEL), C ** -0.5),
        'w_o': nrm((L, D_MODEL, D_MODEL), D_MODEL ** -0.5 * BETA),
        'ln1_g': gain((L, D_MODEL)),
        'ln1_b': nrm((L, D_MODEL), 0.02),
        'w_ffn_up': nrm((L, D_MODEL, 2 * D_FF), D_MODEL ** -0.5),
        'conv_w': nrm((L, CONV_W, 2 * D_FF), CONV_W ** -0.5),
        'conv_b': nrm((L, 2 * D_FF), 0.02),
        'w_ffn_down': nrm((L, D_FF, D_MODEL), D_FF ** -0.5 * BETA),
        'w_pe_gate': nrm((L, D_MODEL, D_MODEL), D_MODEL ** -0.5),
        'w_pe_proj': nrm((L, P_DIM, D_MODEL), P_DIM ** -0.5 * BETA),
        'ln2_g': gain((L, D_MODEL)),
        'ln2_b': nrm((L, D_MODEL), 0.02),
    }


def reference(x_prompt, x_sample, p_prompt, p_sample, w_in, q_norm_g, kv_norm_g, w_uq, w_ukv,
              tshift_mu, w0, w_lora_up, a0, a_lora_up, g_lora_up, k_k, k_a, r_k, v0, v_lora_down,
              v_lora_up, lnx_g, lnx_b, w_pa, w_pb, w_o, ln1_g, ln1_b, w_ffn_up, conv_w, conv_b,
              w_ffn_down, w_pe_gate, w_pe_proj, ln2_g, ln2_b):
    def trunk(x, p):
        v_first = None
        for i in range(DEPTH):
            h = x @ w_in[i]
            att = _mla(h[..., :OFF_CKV], h[..., OFF_CKV:OFF_KR], h[..., OFF_KR:OFF_RWKV],
                       q_norm_g[i], kv_norm_g[i], w_uq[i], w_ukv[i])
            v_mix = None if i == 0 else (v0[i - 1], v_lora_down[i - 1], v_lora_up[i - 1])
            rw, v_first = _rwkv_time_mix(h[..., OFF_RWKV:OFF_GA], x, v_first, tshift_mu[i], w0[i],
                                         w_lora_up[i], a0[i], a_lora_up[i], g_lora_up[i], k_k[i],
                                         k_a[i], r_k[i], lnx_g[i], lnx_b[i], v_mix)
            mixed = (jax.nn.sigmoid(h[..., OFF_GA:OFF_GB]) * (att @ w_pa[i])
                     + jax.nn.sigmoid(h[..., OFF_GB:]) * (rw @ w_pb[i]))
            x = _layernorm(ALPHA * x + mixed @ w_o[i], ln1_g[i], ln1_b[i])
            u = _dwconv3(x @ w_ffn_up[i], conv_w[i], conv_b[i])
            f = (jax.nn.gelu(u[..., :D_FF]) * u[..., D_FF:]) @ w_ffn_down[i]
            e = jax.nn.sigmoid(x @ w_pe_gate[i]) * (p[i] @ w_pe_proj[i])
            x = _layernorm(ALPHA * x + f + e, ln2_g[i], ln2_b[i])
        return x

    y_prompt = trunk(x_prompt, p_prompt)
    y_sample = trunk(x_sample, p_sample)
    return (y_prompt, y_sample)
```

```python
import numpy as np
import concourse.bass as bass
import concourse.mybir as mybir
from concourse.bass_utils import run_bass_kernel_spmd

F32 = mybir.dt.float32
BF16 = mybir.dt.bfloat16
AF = mybir.ActivationFunctionType
ALU = mybir.AluOpType


class Buf:
    __slots__ = ("name", "w", "r", "psum")

    def __init__(self, name):
        self.name = name
        self.psum = False
        self.w = {}
        self.r = {}


class TT:
    __slots__ = ("ap", "buf")

    def __init__(self, ap, buf):
        self.ap = ap
        self.buf = buf

    def __getitem__(self, idx):
        return TT(self.ap[idx], self.buf)

    def rr(self, pat, **kw):
        return TT(self.ap.rearrange(pat, **kw), self.buf)


class Pool:
    def __init__(self, tiles):
        self.tiles = tiles
        self.i = 0

    def next(self):
        t = self.tiles[self.i % len(self.tiles)]
        self.i += 1
        return t


class Prog:
    def __init__(self, nc, n_dma_sems=40):
        self.nc = nc
        self.E = {"pe": nc.tensor, "dve": nc.vector, "act": nc.scalar, "pool": nc.gpsimd, "sp": nc.sync}
        self.sems = []
        self.semval = []
        self.esem = {}
        for e in self.E:
            self.esem[e] = self._newsem("s_" + e)
        self.dsems = [self._newsem("d%d" % i) for i in range(n_dma_sems)]
        self.di = 0
        self.known = {e: {} for e in self.E}
        self.ninst = {e: 0 for e in self.E}
        self._stack = []

    def _newsem(self, name):
        h = self.nc.alloc_semaphore(name)
        self.sems.append(h)
        self.semval.append(0)
        return len(self.sems) - 1

    def sb(self, name, shape, dtype):
        self._uid = getattr(self, "_uid", 0) + 1
        name = "%s_u%d" % (name, self._uid)
        cm = self.nc.sbuf_tensor(name, list(shape), dtype)
        t = cm.__enter__()
        self._stack.append(cm)
        return TT(t[tuple(slice(None) for _ in shape)], Buf(name))

    def sbpool(self, name, shape, dtype, n):
        return Pool([self.sb("%s%d" % (name, i), shape, dtype) for i in range(n)])

    def psum(self, name, shape, dtype):
        cm = self.nc.psum_tensor(name, list(shape), dtype)
        t = cm.__enter__()
        self._stack.append(cm)
        b = Buf(name)
        b.psum = True
        return TT(t[tuple(slice(None) for _ in shape)], b)

    def mark(self):
        return len(self._stack)

    def barrier(self):
        for e in self.E:
            for s in range(len(self.sems)):
                if self.semval[s] > 0:
                    self._wait(e, s, self.semval[s])

    def release(self, mark):
        self.barrier()
        while len(self._stack) > mark:
            cm = self._stack.pop()
            cm.__exit__(None, None, None)

    def dram(self, name, shape, dtype, kind="Internal"):
        t = self.nc.dram_tensor(name, list(shape), dtype, kind=kind)
        return TT(t.ap(), Buf(name))

    def _wait(self, eng, sem, val):
        k = self.known[eng]
        if k.get(sem, 0) >= val:
            return
        self.E[eng].wait_ge(self.sems[sem], val)
        k[sem] = val

    def _pre(self, eng, reads, writes, acc=False):
        for t in reads:
            for s, v in t.buf.w.items():
                self._wait(eng, s, v)
            if t.buf.psum:
                for s, v in t.buf.r.items():
                    if s != self.esem[eng]:
                        self._wait(eng, s, v)
        for t in writes:
            b = t.buf
            for s, v in b.w.items():
                if eng == "pe" and s == self.esem["pe"]:
                    continue
                self._wait(eng, s, v)
            for s, v in b.r.items():
                if eng == "pe" and s == self.esem["pe"]:
                    continue
                self._wait(eng, s, v)

    def _post(self, eng, ins, reads, writes):
        s = self.esem[eng]
        self.semval[s] += 1
        v = self.semval[s]
        ins.then_inc(self.sems[s], 1)
        self.ninst[eng] += 1
        for t in reads:
            t.buf.r[s] = v
        for t in writes:
            t.buf.w[s] = v
            t.buf.r = {}

    @staticmethod
    def _ap(x):
        return x.ap if isinstance(x, TT) else x

    def _tts(self, *xs):
        return [x for x in xs if isinstance(x, TT)]

    def mm(self, out, lhsT, rhs, start=True, stop=True):
        self._pre("pe", [lhsT, rhs], [out])
        ins = self.nc.tensor.matmul(out.ap, lhsT.ap, rhs.ap, start=start, stop=stop)
        self._post("pe", ins, [lhsT, rhs], [out])

    def tr(self, out, in_, ident):
        self._pre("pe", [in_, ident], [out])
        ins = self.nc.tensor.transpose(out.ap, in_.ap, ident.ap)
        self._post("pe", ins, [in_, ident], [out])

    def act(self, out, in_, func=None, bias=None, scale=None):
        func = func if func is not None else AF.Copy
        rd = self._tts(in_, bias, scale)
        self._pre("act", rd, [out])
        kw = {}
        if bias is not None:
            kw["bias"] = self._ap(bias)
        if scale is not None:
            kw["scale"] = self._ap(scale)
        ins = self.nc.scalar.activation(out.ap, in_.ap, func, **kw)
        self._post("act", ins, rd, [out])

    def tt(self, eng, out, a, b, op):
        self._pre(eng, [a, b], [out])
        ins = self.E[eng].tensor_tensor(out.ap, a.ap, b.ap, op)
        self._post(eng, ins, [a, b], [out])

    def ts(self, eng, out, a, s1, op0, s2=None, op1=None):
        rd = self._tts(a, s1, s2)
        self._pre(eng, rd, [out])
        if op1 is None:
            ins = self.E[eng].tensor_scalar(out.ap, a.ap, self._ap(s1), None, op0)
        else:
            ins = self.E[eng].tensor_scalar(out.ap, a.ap, self._ap(s1), self._ap(s2), op0, op1)
        self._post(eng, ins, rd, [out])

    def stt(self, out, a, s, b, op0, op1):
        rd = self._tts(a, s, b)
        self._pre("dve", rd, [out])
        ins = self.nc.vector.scalar_tensor_tensor(out.ap, a.ap, self._ap(s), b.ap, op0, op1)
        self._post("dve", ins, rd, [out])

    def copy(self, eng, out, in_):
        if eng == "act":
            return self.act(out, in_, AF.Copy)
        self._pre(eng, [in_], [out])
        ins = self.E[eng].tensor_copy(out.ap, in_.ap)
        self._post(eng, ins, [in_], [out])

    def memset(self, eng, out, val):
        self._pre(eng, [], [out])
        ins = self.E[eng].memset(out.ap, val)
        self._post(eng, ins, [], [out])

    def recip(self, out, in_):
        self._pre("dve", [in_], [out])
        ins = self.nc.vector.reciprocal(out.ap, in_.ap)
        self._post("dve", ins, [in_], [out])

    def scan(self, out, d0, d1, init, op0, op1):
        rd = self._tts(d0, d1, init)
        self._pre("dve", rd, [out])
        ins = self.nc.vector.tensor_tensor_scan(out.ap, d0.ap, d1.ap, self._ap(init), op0, op1)
        self._post("dve", ins, rd, [out])

    def dma(self, out, in_, q="sp"):
        self._pre(q, [in_], [out])
        s = self.dsems[self.di % len(self.dsems)]
        self.di += 1
        self._wait(q, s, self.semval[s])
        self.semval[s] += 16
        v = self.semval[s]
        self.E[q].dma_start(out=out.ap, in_=in_.ap, allow_slow_non_contiguous=True).then_inc(self.sems[s], 16)
        in_.buf.r[s] = v
        out.buf.w[s] = v
        out.buf.r = {}

    def finish(self, outs):
        for s in self.dsems:
            if self.semval[s] > 0:
                self._wait("sp", s, self.semval[s])
        self.release(0)
D = 1024
DEPTH = 2
H = 8
NOPE, ROPE, VD = 64, 32, 64
QK = 96
QL, KVL = 768, 256
C = 512
DFF = 2816
PD = 256
ALPHA = (2 * DEPTH) ** 0.25
LN_EPS = 1e-5
RMS_EPS = 1e-6
GN_EPS = 64e-5
DECAY_SCALE = 0.606531
OFF_CKV, OFF_KR, OFF_RW = 768, 1024, 1056
OFF_GA = OFF_RW + 1920
OFF_GB = OFF_GA + 1024
INC = OFF_GB + 1024
TT_ = 512
CH = 128

VOFF = {}
_o = 0
for _n, _w in [("qg", 6), ("kvg", 2), ("mu", 15), ("w0", 8), ("a0", 8), ("kk", 4), ("ka", 4), ("rk", 4),
               ("v0", 4), ("lng", 4), ("lnb", 4), ("l1g", 8), ("l1b", 8), ("cw0", 44), ("cw1", 44),
               ("cw2", 44), ("cb", 44), ("l2g", 8), ("l2b", 8)]:
    VOFF[_n] = (_o, _w)
    _o += _w
NVEC = _o


def host_consts(Tmax):
    c = {}
    c["ident"] = np.eye(128, dtype=np.float32)
    pos = np.arange(Tmax, dtype=np.float32)
    inv = (np.float32(10000.0) ** (-np.arange(0, ROPE, 2, dtype=np.float32) / np.float32(ROPE))).astype(np.float32)
    ang = (pos[:, None] * inv[None, :]).astype(np.float32)
    cos = np.cos(ang).astype(np.float32).T
    sin = np.sin(ang).astype(np.float32).T
    c["cs"] = np.ascontiguousarray(np.tile(np.concatenate([cos, cos], 0), (4, 1)))
    c["sn"] = np.ascontiguousarray(np.tile(np.concatenate([sin, sin], 0), (4, 1)))
    s = np.arange(128)[:, None]
    t = np.arange(128)[None, :]
    m = np.zeros((128, 4, 128), np.float32)
    m[:, 0] = (s < t)
    m[:, 1] = (s <= t)
    m[:, 2] = (s > t)
    m[:, 3] = (s >= t)
    c["masks"] = m.reshape(128, 512)
    bd = np.zeros((128, 128), np.float32)
    bd[:64, :64] = 1
    bd[64:, 64:] = 1
    c["bd"] = bd
    rm = np.ones((128, TT_), np.float32)
    rm[:, ::CH] = 0
    c["rmask"] = rm
    return c


def blk(v, n):
    return np.ascontiguousarray(np.asarray(v, np.float32).reshape(n, 128).T)


def host_layer_params(w, l):
    o = {}
    vec = np.zeros((128, NVEC), np.float32)

    def put(name, arr):
        a, n = VOFF[name]
        assert arr.shape == (128, n), (name, arr.shape)
        vec[:, a:a + n] = arr
    put("qg", blk(w["q_norm_g"][l], 6))
    put("kvg", blk(w["kv_norm_g"][l], 2))
    put("mu", blk(w["tshift_mu"][l], 15))
    put("w0", blk(w["w0"][l].reshape(-1), 8))
    put("a0", blk(w["a0"][l].reshape(-1), 8))
    put("kk", blk(w["k_k"][l], 4))
    put("ka", blk(w["k_a"][l], 4))
    put("rk", blk(w["r_k"][l], 4))
    if l > 0:
        put("v0", blk(w["v0"][l - 1], 4))
    put("lng", blk(w["lnx_g"][l], 4))
    put("lnb", blk(w["lnx_b"][l], 4))
    put("l1g", blk(w["ln1_g"][l], 8))
    put("l1b", blk(w["ln1_b"][l], 8))
    for i in range(3):
        put("cw%d" % i, blk(w["conv_w"][l, i], 44))
    put("cb", blk(w["conv_b"][l], 44))
    put("l2g", blk(w["ln2_g"][l], 8))
    put("l2b", blk(w["ln2_b"][l], 8))
    o["vec"] = vec
    win = np.asarray(w["w_in"][l], np.float32)
    o["w_in"] = win
    o["w_krr"] = np.ascontiguousarray(np.concatenate([win[:, OFF_KR + 16:OFF_KR + 32], win[:, OFF_KR:OFF_KR + 16]], 1))
    wq = np.asarray(w["w_uq"][l], np.float32).reshape(QL, H, QK)
    o["wq_n"] = np.ascontiguousarray(wq[:, :, :NOPE].reshape(QL, H * NOPE))
    o["wq_r"] = np.ascontiguousarray(wq[:, :, NOPE:].reshape(QL, H * ROPE))
    o["wq_rr"] = np.ascontiguousarray(np.concatenate([wq[:, :, NOPE + 16:], wq[:, :, NOPE:NOPE + 16]], 2).reshape(QL, H * ROPE))
    wkv = np.asarray(w["w_ukv"][l], np.float32).reshape(KVL, H, NOPE + VD)
    o["wk"] = np.ascontiguousarray(wkv[:, :, :NOPE].reshape(KVL, H * NOPE))
    o["wv"] = np.ascontiguousarray(wkv[:, :, NOPE:].reshape(KVL, H * VD))
    o["w_lu"] = np.ascontiguousarray(np.asarray(w["w_lora_up"][l], np.float32).reshape(128, C))
    o["a_lu"] = np.ascontiguousarray(np.asarray(w["a_lora_up"][l], np.float32).reshape(128, C))
    o["g_lu"] = np.asarray(w["g_lora_up"][l], np.float32)
    if l > 0:
        o["v_ld"] = np.asarray(w["v_lora_down"][l - 1], np.float32)
        o["v_lu"] = np.asarray(w["v_lora_up"][l - 1], np.float32)
    for k in ["w_pa", "w_pb", "w_o", "w_ffn_up", "w_ffn_down", "w_pe_gate", "w_pe_proj"]:
        o[k] = np.asarray(w[k][l], np.float32)
    return o


class Cfg:
    def __init__(self, seqs, debug=False, phases="0ABCDE"):
        self.seqs = list(seqs)
        self.NT = sum(seqs)
        self.off = [sum(seqs[:i]) for i in range(len(seqs))]
        self.Tmax = max(seqs)
        self.debug = debug
        self.phases = phases

    def tiles(self):
        for si, T in enumerate(self.seqs):
            for j in range(T // TT_):
                yield si, j, self.off[si] + j * TT_, j * TT_
def load_w(P, dst, src, K, N, stage, engs=("dve", "pool"), scale_vec=None, neg=None):
    KC = (K + 127) // 128
    i = 0
    for kc in range(KC):
        rows = min(128, K - kc * 128)
        for c0 in range(0, N, 2048):
            cw = min(2048, N - c0)
            st = stage.next()
            P.dma(st[0:rows, 0:cw], src[kc * 128:kc * 128 + rows, c0:c0 + cw])
            eng = engs[i % len(engs)]
            i += 1
            if scale_vec is None:
                P.copy(eng, dst[0:rows, kc, c0:c0 + cw], st[0:rows, 0:cw])
            else:
                P.ts(eng, dst[0:rows, kc, c0:c0 + cw], st[0:rows, 0:cw], scale_vec[0:rows, kc:kc + 1], ALU.mult)
    if neg is not None:
        for (a, b) in neg:
            P.ts("pool", dst[:, :, a:b], dst[:, :, a:b], -1.0, ALU.mult)


def phase0(P, cfg, PS, x_in, p_in, XT, PTs, ident):
    m = P.mark()
    xin_pool = P.sbpool("p0x", [128, 4, 1024], F32, 2)
    xf_pool = P.sbpool("p0f", [128, 8, 512], F32, 2)
    pin_pool = P.sbpool("p0p", [128, 4, 256], F32, 2)
    pb_pool = P.sbpool("p0b", [128, 2, 512], BF16, 2)
    XTv = XT.rr("(kc p) t -> p kc t", p=128)
    k = 0
    for (si, j, g0, t0) in cfg.tiles():
        xin = xin_pool.next()
        P.dma(xin, x_in[g0:g0 + TT_, :].rr("(tb p) f -> p tb f", p=128))
        xf = xf_pool.next()
        for kc in range(8):
            ps = PS.next()
            for tb in range(4):
                P.tr(ps[:, tb * 128:(tb + 1) * 128], xin[:, tb, kc * 128:(kc + 1) * 128], ident)
            P.copy(("act", "dve")[k % 2], xf[:, kc, :], ps)
            k += 1
        P.dma(XTv[:, :, g0:g0 + TT_], xf)
        for l in range(DEPTH):
            pin = pin_pool.next()
            P.dma(pin, p_in[l, g0:g0 + TT_, :].rr("(tb p) f -> p tb f", p=128))
            pb = pb_pool.next()
            for kc in range(2):
                ps = PS.next()
                for tb in range(4):
                    P.tr(ps[:, tb * 128:(tb + 1) * 128], pin[:, tb, kc * 128:(kc + 1) * 128], ident)
                P.copy(("act", "dve")[k % 2], pb[:, kc, :], ps)
                k += 1
            P.dma(PTs[l].rr("(kc p) t -> p kc t", p=128)[:, :, g0:g0 + TT_], pb)
    P.release(m)


def phaseA(P, cfg, PS, l, W, vec, XT, S, consts):
    m = P.mark()
    Win = P.sb("Win", [128, 8, INC], BF16)
    Wkrr = P.sb("Wkrr", [128, 8, 128], BF16)
    P.memset("pool", Wkrr, 0.0)
    Wqn = P.sb("Wqn", [128, 6, 512], BF16)
    Wqr = P.sb("Wqr", [128, 6, 256], BF16)
    Wqrr = P.sb("Wqrr", [128, 6, 256], BF16)
    Wk = P.sb("Wk", [128, 2, 512], BF16)
    Wv = P.sb("Wv", [128, 2, 512], BF16)
    ones = P.sb("onesb", [128, 128], BF16)
    P.memset("pool", ones, 1.0)
    ms_ = P.mark()
    stage = P.sbpool("stg", [128, 2048], F32, 2)
    qg = vec[:, VOFF["qg"][0]:VOFF["qg"][0] + 6]
    kvg = vec[:, VOFF["kvg"][0]:VOFF["kvg"][0] + 2]
    load_w(P, Win, W["w_in"], D, INC, stage)
    load_w(P, Wkrr[:, :, 0:32], W["w_krr"], D, 32, stage)
    P.ts("pool", Wkrr[:, :, 0:16], Wkrr[:, :, 0:16], -1.0, ALU.mult)
    load_w(P, Wqn, W["wq_n"], QL, 512, stage, scale_vec=qg)
    load_w(P, Wqr, W["wq_r"], QL, 256, stage, scale_vec=qg)
    load_w(P, Wqrr, W["wq_rr"], QL, 256, stage, scale_vec=qg)
    P.ts("pool", Wqrr.rr("p k (h r) -> p k h r", r=32)[:, :, :, 0:16], Wqrr.rr("p k (h r) -> p k h r", r=32)[:, :, :, 0:16], -1.0, ALU.mult)
    load_w(P, Wk, W["wk"], KVL, 512, stage, scale_vec=kvg)
    load_w(P, Wv, W["wv"], KVL, 512, stage, scale_vec=kvg)
    P.release(ms_)

    xf_pool = P.sbpool("axf", [128, 8, TT_], F32, 1)
    xb_pool = P.sbpool("axb", [128, 8, TT_], BF16, 2)
    cs_pool = P.sbpool("acs", [128, 2, TT_], F32, 2)
    cq_pool = P.sbpool("acq", [128, 8, TT_], F32, 1)
    sq_pool = P.sbpool("asq", [128, TT_], BF16, 3)
    cn_pool = P.sbpool("acn", [128, 8, TT_], BF16, 1)
    rs_pool = P.sbpool("ars", [128, 2, TT_], F32, 1)
    zo_pool = P.sbpool("azo", [128, TT_], F32, 4)
    qo_pool = P.sbpool("aqo", [128, TT_], BF16, 6)
    t1_pool = P.sbpool("at1", [128, TT_], F32, 3)
    vo_pool = P.sbpool("avo", [128, 4, 512], BF16, 1)
    XTv = XT.rr("(kc p) t -> p kc t", p=128)
    Zv = S["Z"].rr("(b p) t -> p b t", p=128)
    Gv = S["G"].rr("(b p) t -> p b t", p=128)
    tiles = list(cfg.tiles())
    PSacc = Pool(PS.tiles[0:2])
    PS = Pool(PS.tiles[2:8])
    import os
    AT = float(os.environ.get("AT", "99"))

    def load(i):
        si, j, g0, t0 = tiles[i]
        xf = xf_pool.next()
        P.dma(xf, XTv[:, :, g0:g0 + TT_])
        cs = cs_pool.next()
        P.dma(cs[:, 0, :], consts["cs"][:, t0:t0 + TT_])
        P.dma(cs[:, 1, :], consts["sn"][:, t0:t0 + TT_])
        return xf, cs
    if AT <= 0:
        P.release(m)
        return
    nxt = load(0)
    ev = 0
    for i, (si, j, g0, t0) in enumerate(tiles):
        xf, cs = nxt
        xb = xb_pool.next()
        P.copy("act", xb[:, 0:4, :], xf[:, 0:4, :])
        P.copy("dve", xb[:, 4:8, :], xf[:, 4:8, :])
        if i + 1 < len(tiles):
            nxt = load(i + 1)
        if cfg.debug and i == 0 and l == 0:
            dbg1 = P.dram("dbg_xb", [128, 8, TT_], BF16, kind="ExternalOutput")
            P.dma(dbg1, xb)
            for ii, cc in enumerate([0, 1024, 2048, 4096]):
                dbg2 = P.dram("dbg_win%d" % ii, [128, 8, 512], BF16, kind="ExternalOutput")
                P.dma(dbg2, Win[:, :, cc:cc + 512])
            dbg3 = P.dram("dbg_xf", [128, 8, TT_], F32, kind="ExternalOutput")
            P.dma(dbg3, xf)

        def proj(c0, mw, Wt=Win, KC=8, rhs=xb):
            ps = PS.next()
            for kc in range(KC):
                P.mm(ps[0:mw, :], Wt[:, kc, c0:c0 + mw], rhs[:, kc, :], start=(kc == 0), stop=(kc == KC - 1))
            return ps
        if AT <= 0.4:
            continue
        for b in range(15):
            ps = proj(OFF_RW + b * 128, 128)
            zo = zo_pool.next()
            P.copy(("act", "dve")[ev % 2], zo, ps)
            ev += 1
            P.dma(Zv[:, b, g0:g0 + TT_], zo)
            if cfg.debug and i == 0 and l == 0 and b == 0:
                dz = P.sb("dbgz", [128, TT_], F32)
                P.copy("dve", dz, ps)
                P.dma(P.dram("dbg_z0", [128, TT_], F32, kind="ExternalOutput"), dz)
                P.dma(P.dram("dbg_z1", [128, TT_], F32, kind="ExternalOutput"), zo)
        if AT <= 0.45:
            continue
        for b in range(16):
            ps = proj(OFF_GA + b * 128, 128)
            zo = zo_pool.next()
            P.act(zo, ps, AF.Sigmoid)
            P.dma(Gv[:, b, g0:g0 + TT_], zo)
        if AT <= 0.5:
            continue
        cq = cq_pool.next()
        cn = cn_pool.next()
        rs = rs_pool.next()
        ssq = [PSacc.next(), PSacc.next()]
        sqs = []
        for b in range(8):
            ps = proj(b * 128, 128)
            P.copy("dve", cq[:, b, :], ps)
            sq = sq_pool.next()
            P.act(sq, ps, AF.Square)
            sqs.append(sq)
            which = 0 if b < 6 else 1
            first = b in (0, 6)
            last = b in (5, 7)
            P.mm(ssq[which], ones, sq, start=first, stop=last)
        if cfg.debug and i == 0 and l == 0:
            dbg4 = P.dram("dbg_cq", [128, 8, TT_], F32, kind="ExternalOutput")
            P.dma(dbg4, cq)
        if AT <= 0.6:
            continue
        for which, (n, b0, b1) in enumerate([(QL, 0, 6), (KVL, 6, 8)]):
            P.act(rs[:, which, :], ssq[which], AF.Sqrt, bias=consts["eps_rms"], scale=1.0 / n)
            if AT <= 0.7:
                continue
            P.recip(rs[:, which, :], rs[:, which, :])
            if AT <= 0.8:
                continue
            for b in range(b0, b1):
                P.tt("dve", cn[:, b, :], cq[:, b, :], rs[:, which, :], ALU.mult)
        if AT <= 1:
            continue
        ps = proj(OFF_KR, 128)
        ps2 = proj(0, 128, Wt=Wkrr)
        t1 = t1_pool.next()
        t2 = t1_pool.next()
        P.tt("dve", t1[0:32, :], ps[0:32, :], cs[0:32, 0, :], ALU.mult)
        P.tt("dve", t2[0:32, :], ps2[0:32, :], cs[0:32, 1, :], ALU.mult)
        kr = qo_pool.next()
        P.tt("dve", kr[0:32, :], t1[0:32, :], t2[0:32, :], ALU.add)
        for h in range(H):
            P.dma(S["KT"][h, 64:96, g0:g0 + TT_], kr[0:32, :])
        if AT <= 4:
            continue
        scale = QK ** -0.5
        for b in range(4):
            ps = proj(b * 128, 128, Wt=Wqn, KC=6, rhs=cn)
            qo = qo_pool.next()
            P.act(qo, ps, AF.Copy, scale=scale)
            for hh in range(2):
                P.dma(S["QT"][2 * b + hh, 0:64, g0:g0 + TT_], qo[hh * 64:(hh + 1) * 64, :])
        for b in range(2):
            ps = proj(b * 128, 128, Wt=Wqr, KC=6, rhs=cn)
            ps2 = proj(b * 128, 128, Wt=Wqrr, KC=6, rhs=cn)
            t1 = t1_pool.next()
            t2 = t1_pool.next()
            P.tt("dve", t1, ps, cs[:, 0, :], ALU.mult)
            P.tt("dve", t2, ps2, cs[:, 1, :], ALU.mult)
            qo = qo_pool.next()
            P.tt("dve", t1, t1, t2, ALU.add)
            P.act(qo, t1, AF.Copy, scale=scale)
            for hh in range(4):
                P.dma(S["QT"][4 * b + hh, 64:96, g0:g0 + TT_], qo[hh * 32:(hh + 1) * 32, :])
        if AT <= 5:
            continue
        for b in range(4):
            ps = proj(b * 128, 128, Wt=Wk, KC=2, rhs=cn[:, 6:8, :])
            qo = qo_pool.next()
            P.copy(("act", "dve")[b % 2], qo, ps)
            for hh in range(2):
                P.dma(S["KT"][2 * b + hh, 0:64, g0:g0 + TT_], qo[hh * 64:(hh + 1) * 64, :])
        if AT <= 6:
            continue
        vo = vo_pool.next()
        for tb in range(4):
            ps = PS.next()
            for kc in range(2):
                P.mm(ps, cn[:, 6 + kc, tb * 128:(tb + 1) * 128], Wv[:, kc, :], start=(kc == 0), stop=(kc == 1))
            P.copy(("act", "dve")[tb % 2], vo[:, tb, :], ps)
        P.dma(S["V"][g0:g0 + TT_, :].rr("(tb p) c -> p tb c", p=128), vo)
    P.release(m)
def phaseB(P, cfg, PSall, l, S, consts):
    m = P.mark()
    PSs = Pool(PSall.tiles[0:5])
    PSo = Pool(PSall.tiles[5:7])
    PSb = Pool(PSall.tiles[7:8])
    Tm = cfg.Tmax
    kt_pool = P.sbpool("bkt", [96, Tm], BF16, 2)
    vh_pool = P.sbpool("bvh", [128, Tm // 128, 65], BF16, 2)
    for t in vh_pool.tiles:
        P.memset("pool", t[:, :, 64:65], 1.0)
    q_pool = P.sbpool("bq", [96, TT_], BF16, 3)
    pt_pool = P.sbpool("bpt", [128, TT_], BF16, 4)
    lrow_pool = P.sbpool("blr", [65, TT_], F32, 2)
    rec_pool = P.sbpool("brc", [64, TT_], F32, 2)
    ao_pool = P.sbpool("bao", [64, TT_], BF16, 3)
    ones32 = P.sb("bones", [65, 64], F32)
    P.memset("pool", ones32, 1.0)
    for si, T in enumerate(cfg.seqs):
        off = cfg.off[si]
        nk = T // 128
        for h in range(H):
            kt = kt_pool.next()
            vh = vh_pool.next()
            P.dma(kt[:, 0:T], S["KT"][h, :, off:off + T])
            for c0 in range(0, nk, 8):
                P.dma(vh[:, c0:c0 + 8, 0:64],
                      S["V"][off + c0 * 128:off + (c0 + 8) * 128, h * 64:(h + 1) * 64].rr("(c p) v -> p c v", p=128))
            for qt in range(T // TT_):
                g0 = off + qt * TT_
                q = q_pool.next()
                P.dma(q, S["QT"][h, :, g0:g0 + TT_])
                pso = PSo.next()
                pss = {}
                LA = 2
                for kc in range(min(LA, nk)):
                    pss[kc] = PSs.next()
                    P.mm(pss[kc], kt[:, kc * 128:(kc + 1) * 128], q)
                for kc in range(nk):
                    if kc + LA < nk:
                        pss[kc + LA] = PSs.next()
                        P.mm(pss[kc + LA], kt[:, (kc + LA) * 128:(kc + LA + 1) * 128], q)
                    pt = pt_pool.next()
                    P.act(pt, pss.pop(kc), AF.Exp)
                    P.mm(pso[0:65, :], vh[:, kc, :], pt, start=(kc == 0), stop=(kc == nk - 1))
                lrow = lrow_pool.next()
                P.copy("dve", lrow[64:65, :], pso[64:65, :])
                psb = PSb.next()
                P.mm(psb[0:64, :], ones32[64:65, :], lrow[64:65, :])
                rec = rec_pool.next()
                P.recip(rec, psb[0:64, :])
                ao = ao_pool.next()
                P.tt("dve", ao, pso[0:64, :], rec, ALU.mult)
                P.dma(S["ATT"][h * 64:(h + 1) * 64, g0:g0 + TT_], ao)
    P.release(m)
def run_rr(gens):
    gens = list(gens)
    while gens:
        nxt = []
        for g in gens:
            try:
                next(g)
                nxt.append(g)
            except StopIteration:
                pass
        gens = nxt


def phaseC(P, cfg, PS, l, W, vec, XT, S, consts):
    import os
    CT = float(os.environ.get("CT", "99"))
    TC = 256
    NCH = TC // CH
    m = P.mark()
    Wlu = P.sb("Wlu", [128, 1, C], BF16)
    Alu = P.sb("Alu", [128, 1, C], BF16)
    Glu = P.sb("Glu", [128, 1, C], BF16)
    if l > 0:
        Vld = P.sb("Vld", [128, 8, 128], BF16)
        Vlu = P.sb("Vlu", [128, 1, C], BF16)
    ms_ = P.mark()
    stage = P.sbpool("stg", [128, 2048], F32, 2)
    load_w(P, Wlu, W["w_lu"], 128, C, stage)
    load_w(P, Alu, W["a_lu"], 128, C, stage)
    load_w(P, Glu, W["g_lu"], 128, C, stage)
    if l > 0:
        P.memset("pool", Vld, 0.0)
        P.memset("pool", Vlu, 0.0)
        load_w(P, Vld[:, :, 0:32], W["v_ld"], D, 32, stage)
        load_w(P, Vlu, W["v_lu"], 32, C, stage)
    P.release(ms_)
    masks = P.sb("cmask", [128, 4, 2, 128], F32)
    P.dma(masks, consts["masks_d"])
    bdf = P.sb("cbdf", [128, 128], F32)
    P.dma(bdf, consts["bd_d"])
    bdb = P.sb("cbdb", [128, 128], BF16)
    P.copy("pool", bdb, bdf)
    bd64 = P.sb("cbd64", [128, 128], F32)
    P.ts("dve", bd64, bdf, 1.0 / 64, ALU.mult)
    id2 = P.sb("cid2", [128, 2, 128], F32)
    P.copy("pool", id2[:, 0, :], consts["ident"])
    P.copy("pool", id2[:, 1, :], consts["ident"])
    idb = P.sb("cidb", [128, 128], BF16)
    P.copy("pool", idb, consts["ident"])
    rmask = P.sb("crm", [128, TC], F32)
    P.dma(rmask, consts["rmask_d"][:, 0:TC])
    mo, _ = VOFF["mu"]
    om = P.sb("com", [128, 15], F32)
    hm = P.sb("chm", [128, 15], F32)
    P.ts("dve", om, vec[:, mo:mo + 15], -1.0, ALU.mult, 1.0, ALU.add)
    P.ts("dve", hm, vec[:, mo:mo + 15], 0.5, ALU.mult)
    eps12 = consts["eps_12"]
    epsg = consts["eps_gn"]

    zt_pool = P.sbpool("czt", [128, 3, TC + 2], F32, 2)
    zs_pool = P.sbpool("czs", [128, 15, TC], F32, 1)
    f_pool = P.sbpool("cf", [128, TC], F32, 4)
    fcb = [P.sbpool("cfc%d" % i, [128, TC], F32, 11) for i in range(4)]
    nt_pool = P.sbpool("cnt", [128, 4], F32, 8)
    bcb = [P.sbpool("cbc%d" % i, [128, TC], BF16, 2) for i in range(4)]
    ops_pool = P.sbpool("cops", [128, 4, 6, TC], BF16, 2)
    vb_pool = P.sbpool("cvb", [128, 4, TC], BF16, 2)
    pl_pool = P.sbpool("cpl", [128, 4, 4], F32, 3)
    yt_pool = P.sbpool("cyt", [128, 4, TC], F32, 2)
    sg_pool = P.sbpool("csg", [128, TC], BF16, 2)
    bon_pool = P.sbpool("cbon", [128, 4, TC], F32, 2)
    if l > 0:
        xh_pool = P.sbpool("cxh", [128, 2, TC], F32, 1)
        xb_pool = P.sbpool("cxb", [128, 8, TC], BF16, 1)
        vf_pool = P.sbpool("cvf", [128, 4, TC], F32, 1)
        xd_pool = P.sbpool("cxd", [128, TC], BF16, 1)
    tok_pool = P.sbpool("utok", [128, 4, 128], BF16, 8)
    mt_pool = P.sbpool("umt", [128, 6, 256], BF16, 4)
    mk_pool = P.sbpool("umk", [128, 256], BF16, 5)
    s_pool = P.sbpool("us", [128, 256], BF16, 5)
    nk_pool = P.sbpool("unk", [128, 256], BF16, 4)
    mr_pool = P.sbpool("umr", [128, 2, 256], BF16, 8)
    x1_pool = P.sbpool("ux1", [128, 128], BF16, 4)
    ut_pool = P.sbpool("uut", [128, 128], F32, 8)
    wt_pool = P.sbpool("uwt", [128, 128], BF16, 8)
    u_pool = P.sbpool("uu", [128, 128], BF16, 4)
    th_pool = P.sbpool("uth", [128, 128], F32, 4)
    pad_pools = []
    for nm in ("upB", "upA", "upR"):
        pp_ = P.sbpool(nm, [128, 2, 128], BF16, 4)
        for t_ in pp_.tiles:
            P.memset("pool", t_, 0.0)
        pad_pools.append(pp_)
    tzp = P.sb("ctzp", [128, TC], BF16)
    zap = P.sb("czap", [128, TC], BF16)
    f2_pool = P.sbpool("cf2", [128, TC], F32, 4)
    o_pool = P.sbpool("co", [128, TC], BF16, 2)
    Hf = [P.sb("Hf%d" % i, [128, 128], F32) for i in range(4)]
    Hb = [P.sb("Hb%d" % i, [128, 128], BF16) for i in range(4)]

    print("phaseC layer", l, "sbuf remaining", P.nc.sbuf_bytes_remaining)
    Zv = S["Z"].rr("(b p) t -> p b t", p=128)
    XTv = XT.rr("(kc p) t -> p kc t", p=128)
    YFv = S["YF"].rr("(b p) t -> p b t", p=128)
    BONv = S["BON"].rr("(b p) t -> p b t", p=128)
    VFv = S["VF"].rr("(b p) t -> p b t", p=128)
    RWv = S["RW"].rr("(b p) t -> p b t", p=128)
    V = VOFF
    mul, add, sub = ALU.mult, ALU.add, ALU.subtract

    def c1(si, g0, t0, d, res):
        T = cfg.seqs[si]
        zs = zs_pool.next()
        for gb in range(5):
            zt = zt_pool.next()
            P.dma(zt[:, :, 1:TC + 1], Zv[:, 3 * gb:3 * gb + 3, g0:g0 + TC])
            if t0 == 0:
                P.memset("pool", zt[:, :, 0:1], 0.0)
            else:
                P.dma(zt[:, :, 0:1], Zv[:, 3 * gb:3 * gb + 3, g0 - 1:g0])
            if t0 + TC >= T:
                P.memset("pool", zt[:, :, TC + 1:TC + 2], 0.0)
            else:
                P.dma(zt[:, :, TC + 1:TC + 2], Zv[:, 3 * gb:3 * gb + 3, g0 + TC:g0 + TC + 1])
            for q in range(3):
                b = 3 * gb + q
                t = f_pool.next()
                P.tt("pool", t, zt[:, q, 0:TC], zt[:, q, 2:TC + 2], add)
                P.act(zs[:, b, :], zt[:, q, 1:TC + 1], AF.Identity, scale=om[:, b:b + 1])
                P.stt(zs[:, b, :], t, hm[:, b:b + 1], zs[:, b, :], mul, add)
            yield
        hs = slice(64 * d, 64 * d + 64)
        tz = tzp
        P.act(tz[hs, :], zs[hs, 12, :], AF.Tanh)
        zab = zap
        P.copy("act", zab[hs, :], zs[hs, 13, :])
        if l > 0:
            xb = xb_pool.next()
            for hh in range(4):
                xh = xh_pool.next()
                P.dma(xh, XTv[:, 2 * hh:2 * hh + 2, g0:g0 + TC])
                P.copy(("dve", "act")[hh % 2], xb[:, 2 * hh:2 * hh + 2, :], xh)
            ps = PS.next()[:, 0:TC]
            for kc in range(8):
                P.mm(ps, Vld[:, kc, :], xb[:, kc, :], start=(kc == 0), stop=(kc == 7))
            xd = xd_pool.next()
            P.copy("act", xd, ps)
            vf = vf_pool.next()
            P.dma(vf, VFv[:, :, g0:g0 + TC])
            for cb in range(4):
                ps = PS.next()[:, 0:TC]
                P.mm(ps, Vlu[:, 0, cb * 128:(cb + 1) * 128], xd)
                vm = f_pool.next()
                P.act(vm, ps, AF.Sigmoid, bias=vec[:, V["v0"][0] + cb:V["v0"][0] + cb + 1])
                t = f_pool.next()
                P.tt("dve", t, vf[:, cb, :], zs[:, 8 + cb, :], sub)
                P.tt("dve", t, t, vm, mul)
                P.tt("dve", zs[:, 8 + cb, :], zs[:, 8 + cb, :], t, add)
        elif d == 0:
            P.dma(VFv[:, :, g0:g0 + TC], zs[:, 8:12, :])
        vb = vb_pool.next()
        P.copy("act", vb, zs[:, 8:12, :])
        yield
        ops = ops_pool.next()
        pl = pl_pool.next()
        gt = None
        bon = bon_pool.next()
        if d == 1:
            gt = sg_pool.next()
            P.act(gt, zs[:, 14, :], AF.Sigmoid)
        def chain(cb, f_pool, b_pool):
            r = zs[:, cb, :]
            k = zs[:, 4 + cb, :]
            v = zs[:, 8 + cb, :]
            col = lambda n: vec[:, V[n][0] + cb:V[n][0] + cb + 1]
            cold = lambda n: vec[:, V[n][0] + 4 * d + cb:V[n][0] + 4 * d + cb + 1]
            ps = PS.next()[:, 0:TC]
            P.mm(ps, Wlu[:, 0, cb * 128:(cb + 1) * 128], tz)
            lw = f_pool.next()
            P.act(lw, ps, AF.Sigmoid, bias=cold("w0"))
            yield
            P.act(lw, lw, AF.Identity, scale=-DECAY_SCALE)
            ps = PS.next()[:, 0:TC]
            P.mm(ps, Alu[:, 0, cb * 128:(cb + 1) * 128], zab)
            a = f_pool.next()
            P.act(a, ps, AF.Sigmoid, bias=cold("a0"))
            yield
            kkr = f_pool.next()
            P.act(kkr, k, AF.Identity, scale=col("kk"))
            sq = b_pool.next()
            P.act(sq, kkr, AF.Square)
            yield
            ps = PS.next()[:, 0:TC]
            P.mm(ps, bdb, sq)
            rs = f_pool.next()
            P.act(rs, ps, AF.Sqrt, bias=eps12, scale=1.0)
            yield
            P.recip(rs, rs)
            kk = kkr
            P.tt("dve", kk, kkr, rs, mul)
            kd = f_pool.next()
            P.ts("dve", kd, a, -1.0, add, col("ka"), mul)
            P.stt(kd, kd, 1.0, k, add, mul)
            yield
            kka = rs
            P.tt("dve", kka, kk, a, mul)
            yield
            t = f_pool.next()
            P.stt(t, r, col("rk"), kd, mul, mul)
            tb16 = b_pool.next()
            P.copy("act", tb16, t)
            yield
            ps = PS.next()[:, 0:TC]
            P.mm(ps, bdb, tb16)
            P.tt("dve", bon[:, cb, :], ps, v, mul)
            yield
            cum = f_pool.next()
            P.scan(cum, rmask, lw, 0.0, mul, add)
            yield
            E = lw
            P.tt("dve", E, cum, lw, sub)
            tot = cum.rr("p (c q) -> p c q", q=CH)[:, :, CH - 1]
            P.act(pl[:, cb, 0:NCH], tot, AF.Exp)
            ntot = nt_pool.next()
            P.ts("dve", ntot[:, 0:NCH], tot, -1.0, mul)
            yield
            pincl = f_pool.next()
            pexcl = f_pool.next()
            pinv = f_pool.next()
            pinv2 = a
            pinv2 = f_pool.next()
            if d == 0:
                P.act(pincl, cum, AF.Exp)
                P.act(pexcl, E, AF.Exp)
                P.act(pinv, cum, AF.Exp, scale=-1.0)
                for c in range(NCH):
                    cs = slice(c * CH, (c + 1) * CH)
                    P.act(pinv2[:, cs], cum[:, cs], AF.Exp, bias=cum[:, c * CH + CH - 1:c * CH + CH], scale=-1.0)
            else:
                P.act(pinv2, E, AF.Exp)
                for c in range(NCH):
                    cs = slice(c * CH, (c + 1) * CH)
                    tc_ = cum[:, c * CH + CH - 1:c * CH + CH]
                    P.act(pincl[:, cs], E[:, cs], AF.Exp, bias=tc_, scale=-1.0)
                    P.act(pexcl[:, cs], cum[:, cs], AF.Exp, bias=tc_, scale=-1.0)
                    P.act(pinv[:, cs], E[:, cs], AF.Exp, bias=ntot[:, c:c + 1], scale=1.0)
            yield
            P.tt("dve", ops[:, cb, 0, :], kk, pexcl, mul)
            P.tt("dve", ops[:, cb, 1, :], kka, pinv, mul)
            P.tt("dve", ops[:, cb, 2, :], kd, pinv, mul)
            P.tt("dve", ops[:, cb, 3, :], r, pincl, mul)
            P.tt("pool", ops[:, cb, 4, :], kka, pinv2, mul)
            P.tt("pool", ops[:, cb, 5, :], kd, pinv2, mul)
            yield
        yield from rr_gen([chain(cb, fcb[cb], bcb[cb]) for cb in range(4)])
        res.update(zs=zs, vb=vb, ops=ops, pl=pl, gt=gt, bon=bon)

    def unit_pre(d, c, cb, ops, vb, res):
        ms, msT, mi = (0, 2, 1) if d == 0 else (2, 0, 3)
        cs = slice(c * CH, (c + 1) * CH)
        Bt, At, Kt, Rt, At2, Kt2 = [ops[:, cb, i, cs] for i in range(6)]
        hp = [slice(0, 64), slice(64, 128)]
        m2 = lambda i: masks[:, i, :, :].rr("p a b -> p (a b)")
        pst = PS.next()
        pstb = TT(pst.ap.bitcast(BF16), pst.buf)
        for i, src in enumerate([Bt, At2, Kt2, vb[:, cb, cs]]):
            P.tr(pstb[:, i * 128:(i + 1) * 128], src, idb)
        tok = tok_pool.next()
        P.copy("act", tok.rr("p a b -> p (a b)"), pstb[:, 0:512])
        yield
        if CT <= 1.1:
            return
        pB, pA, pR = [pp_.next() for pp_ in pad_pools]
        for h in range(2):
            P.copy("pool", pB[hp[h], h, :], Bt[hp[h], :])
            P.copy("pool", pA[hp[h], h, :], At[hp[h], :])
            P.copy("pool", pR[hp[h], h, :], Rt[hp[h], :])
        f2 = lambda t_: t_.rr("p a b -> p (a b)")
        psN = PS.next()
        psT = PS.next()
        P.mm(psN[:, 0:256], At, f2(pB))
        P.mm(psT[:, 0:256], Bt, f2(pA))
        mt = mt_pool.next()
        mk = mk_pool.next()
        P.stt(mk, psN[:, 0:256], -1.0, m2(ms), mul, mul)
        P.stt(mt[:, 0, :], psT[:, 0:256], -1.0, m2(msT), mul, mul)
        yield
        if CT <= 1.2:
            return
        psK = PS.next()
        psA = PS.next()
        P.mm(psK[:, 0:256], Kt, f2(pB))
        P.mm(psA[:, 0:256], At, f2(pR))
        P.mm(psA[:, 256:512], Kt, f2(pR))
        nk = nk_pool.next()
        P.tt("dve", nk, psK[:, 0:256], m2(ms), mul)
        mr = mr_pool.next()
        P.tt("dve", mr[:, 0, :], psA[:, 0:256], m2(mi), mul)
        P.tt("dve", mr[:, 1, :], psA[:, 256:512], m2(mi), mul)
        yield
        if CT <= 1.3:
            return
        psX = PS.next()
        for h in range(2):
            P.mm(psX[:, h * 64:(h + 1) * 64], nk[:, h * 128:(h + 1) * 128], tok[:, 3, h * 64:(h + 1) * 64])
        x1 = x1_pool.next()
        P.copy("act", x1, psX[:, 0:128])
        yield
        if CT <= 1.4:
            return
        sb_ = None
        for kk_ in range(1, 7):
            psM = PS.next()
            for h in range(2):
                hs_ = slice(h * 128, (h + 1) * 128)
                P.mm(psM[:, hs_], mt[:, kk_ - 1, hs_], mk[:, hs_])
            if kk_ <= 5:
                psMT = PS.next()
                for h in range(2):
                    hs_ = slice(h * 128, (h + 1) * 128)
                    P.mm(psMT[:, hs_], mk[:, hs_], mt[:, kk_ - 1, hs_])
                mk = mk_pool.next()
                P.copy("act", mk, psM[:, 0:256])
                P.copy("dve", mt[:, kk_, :], psMT[:, 0:256])
            else:
                sb_ = s_pool.next()
                P.tt("dve", sb_, psM[:, 0:256], id2.rr("p a b -> p (a b)"), add)
            yield
        if CT <= 1.5:
            return
        for kk_ in range(5, -1, -1):
            psS = PS.next()
            for h in range(2):
                hs_ = slice(h * 128, (h + 1) * 128)
                P.mm(psS[:, hs_], mt[:, kk_, hs_], sb_[:, hs_])
            s2 = s_pool.next()
            P.tt("dve", s2, psS[:, 0:256], sb_, add)
            sb_ = s2
            yield
        psU = PS.next()
        for h in range(2):
            P.mm(psU[:, h * 64:(h + 1) * 64], sb_[:, h * 128:(h + 1) * 128], x1[:, h * 64:(h + 1) * 64])
        P.mm(psU[:, 128:384], tok[:, 0, :], sb_)
        ut = ut_pool.next()
        P.copy("act", ut, psU[:, 0:128])
        wt = wt_pool.next()
        P.copy("act", wt[0:64, :], psU[0:64, 128:256])
        P.copy("act", wt[64:128, :], psU[64:128, 256:384])
        res.update(tok=tok, mr=mr, ut=ut, wt=wt, Rt=Rt)
        yield

    def unit_seq(d, c, cb, pre, pl, yt):
        tok, mr, ut, wt, Rt = pre["tok"], pre["mr"], pre["ut"], pre["wt"], pre["Rt"]
        cs = slice(c * CH, (c + 1) * CH)
        psU = PS.next()
        P.mm(psU[:, 0:128], wt, Hb[cb])
        u = u_pool.next()
        P.stt(u, psU[:, 0:128], -1.0, ut, mul, sub)
        yield
        psO = PS.next()
        P.mm(psO[:, 0:256], tok[:, 3, :], mr[:, 1, :], start=True, stop=False)
        P.mm(psO[:, 0:256], u, mr[:, 0, :], start=False, stop=False)
        P.mm(psO[:, 0:128], Hb[cb], Rt, start=False, stop=False)
        P.mm(psO[:, 128:256], Hb[cb], Rt, start=False, stop=True)
        psH = PS.next()
        P.mm(psH[:, 0:128], tok[:, 2, :], tok[:, 3, :], start=True, stop=False)
        P.mm(psH[:, 0:128], tok[:, 1, :], u, start=False, stop=True)
        P.copy("act", yt[0:64, cb, cs], psO[0:64, 0:128])
        P.copy("act", yt[64:128, cb, cs], psO[64:128, 128:256])
        th = th_pool.next()
        P.tt("dve", th, psH[:, 0:128], bdf, mul)
        P.stt(Hf[cb], Hf[cb], pl[:, cb, c:c + 1], th, mul, add)
        P.copy("act", Hb[cb], Hf[cb])
        yield

    import os
    def rr_gen(gens):
        gens = list(gens)
        while gens:
            nxt = []
            for g in gens:
                try:
                    next(g)
                    nxt.append(g)
                except StopIteration:
                    pass
            gens = nxt
            yield

    def tile_units(d, g0, R):
        ops, vb, pl, gt, bon = R["ops"], R["vb"], R["pl"], R["gt"], R["bon"]
        yt = yt_pool.next()
        corder = list(range(NCH)) if d == 0 else list(range(NCH - 1, -1, -1))
        pres = {c: [dict() for _ in range(4)] for c in corder}
        yield from rr_gen([unit_pre(d, corder[0], cb, ops, vb, pres[corder[0]][cb]) for cb in range(4)])
        for idx, c in enumerate(corder):
            gl = [unit_seq(d, c, cb, pres[c][cb], pl, yt) for cb in range(4)]
            if idx + 1 < len(corder):
                cn = corder[idx + 1]
                gl += [unit_pre(d, cn, cb, ops, vb, pres[cn][cb]) for cb in range(4)]
            yield from rr_gen(gl)
        if d == 0:
            P.dma(YFv[:, :, g0:g0 + TC], yt)
            P.dma(BONv[:, :, g0:g0 + TC], bon)
        else:
            for cb in range(4):
                y0 = f2_pool.next()
                P.dma(y0, YFv[:, cb, g0:g0 + TC])
                b0 = f2_pool.next()
                P.dma(b0, BONv[:, cb, g0:g0 + TC])
                y = yt[:, cb, :]
                P.tt("pool", y, y, y0, add)
                P.tt("pool", b0, b0, bon[:, cb, :], add)
                pm = PS.next()
                P.mm(pm[:, 0:TC], bd64, y)
                sq = f2_pool.next()
                P.act(sq, y, AF.Square)
                pq = PS.next()
                P.mm(pq[:, 0:TC], bd64, sq)
                msq = f2_pool.next()
                P.act(msq, pm[:, 0:TC], AF.Square)
                var = sq
                P.tt("dve", var, pq[:, 0:TC], msq, sub)
                P.act(var, var, AF.Sqrt, bias=epsg, scale=1.0)
                P.recip(var, var)
                P.tt("dve", y, y, pm[:, 0:TC], sub)
                P.tt("dve", y, y, var, mul)
                P.ts("dve", y, y, vec[:, V["lng"][0] + cb:V["lng"][0] + cb + 1], mul,
                     vec[:, V["lnb"][0] + cb:V["lnb"][0] + cb + 1], add)
                P.tt("dve", y, y, b0, add)
                o = o_pool.next()
                pg = PS.next()
                P.mm(pg[:, 0:TC], Glu[:, 0, cb * 128:(cb + 1) * 128], gt)
                P.tt("dve", o, pg[:, 0:TC], y, mul)
                P.dma(RWv[:, cb, g0:g0 + TC], o)
                yield

    for d in range(2):
        P.memset("pool", tzp, 0.0)
        P.memset("pool", zap, 0.0)
        for si, T in enumerate(cfg.seqs):
            for cb in range(4):
                P.memset("pool", Hf[cb], 0.0)
                P.memset("pool", Hb[cb], 0.0)
            nt = T // TC
            order = list(range(nt)) if d == 0 else list(range(nt - 1, -1, -1))
            prev = None
            for j in order + [None]:
                gens = []
                R = None
                if j is not None:
                    t0 = j * TC
                    g0 = cfg.off[si] + t0
                    R = {"g0": g0}
                    gens.append(c1(si, g0, t0, d, R))
                if prev is not None:
                    gens.append(tile_units(d, prev["g0"], prev))
                run_rr(gens)
                prev = R
    P.release(m)
def layernorm(P, PS, y, gname, vec, ones32s, epst, sqp, stp, tp, emit):
    pm = PS.next()
    pq = PS.next()
    for b in range(8):
        P.mm(pm, ones32s, y[:, b, :], start=(b == 0), stop=(b == 7))
    for b in range(8):
        sq = sqp.next()
        P.act(sq, y[:, b, :], AF.Square)
        P.mm(pq, ones32s, sq, start=(b == 0), stop=(b == 7))
    mean = stp.next()
    P.copy("act", mean, pm)
    msq = stp.next()
    P.act(msq, pm, AF.Square)
    var = stp.next()
    P.tt("dve", var, pq, msq, ALU.subtract)
    P.act(var, var, AF.Sqrt, bias=epst, scale=1.0)
    P.recip(var, var)
    g0 = VOFF[gname + "g"][0]
    b0 = VOFF[gname + "b"][0]
    for b in range(8):
        t = tp.next()
        P.tt("dve", t, y[:, b, :], mean, ALU.subtract)
        P.tt("dve", t, t, var, ALU.mult)
        P.act(t, t, AF.Identity, bias=vec[:, b0 + b:b0 + b + 1], scale=vec[:, g0 + b:g0 + b + 1])
        emit(b, t)


def phaseD1(P, cfg, PS, l, W, vec, XT, S, consts):
    m = P.mark()
    Wpa = P.sb("Wpa", [128, 4, D], BF16)
    Wpb = P.sb("Wpb", [128, 4, D], BF16)
    Wo = P.sb("Wo", [128, 8, D], BF16)
    ms_ = P.mark()
    stage = P.sbpool("stg", [128, 2048], F32, 2)
    load_w(P, Wpa, W["w_pa"], C, D, stage)
    load_w(P, Wpb, W["w_pb"], C, D, stage)
    load_w(P, Wo, W["w_o"], D, D, stage)
    P.release(ms_)
    at_pool = P.sbpool("dat", [128, 8, TT_], BF16, 2)
    g_pool = P.sbpool("dg", [128, 16, TT_], F32, 1)
    xf_pool = P.sbpool("dxf", [128, 8, TT_], F32, 2)
    mx_pool = P.sbpool("dmx", [128, 8, TT_], BF16, 1)
    y_pool = P.sbpool("dy", [128, 8, TT_], F32, 1)
    t_pool = P.sbpool("dt", [128, TT_], F32, 4)
    sq_pool = P.sbpool("dsq", [128, TT_], F32, 2)
    st_pool = P.sbpool("dst", [128, TT_], F32, 3)
    XTv = XT.rr("(kc p) t -> p kc t", p=128)
    X1v = S["X1"].rr("(kc p) t -> p kc t", p=128)
    Gv = S["G"].rr("(b p) t -> p b t", p=128)
    ATv = S["ATT"].rr("(b p) t -> p b t", p=128)
    RWv = S["RW"].rr("(b p) t -> p b t", p=128)
    for (si, j, g0, t0) in cfg.tiles():
        at = at_pool.next()
        P.dma(at[:, 0:4, :], ATv[:, :, g0:g0 + TT_])
        P.dma(at[:, 4:8, :], RWv[:, :, g0:g0 + TT_])
        g = g_pool.next()
        P.dma(g[:, 0:8, :], Gv[:, 0:8, g0:g0 + TT_])
        P.dma(g[:, 8:16, :], Gv[:, 8:16, g0:g0 + TT_])
        xf = xf_pool.next()
        P.dma(xf, XTv[:, :, g0:g0 + TT_])
        mx = mx_pool.next()
        for mb in range(8):
            pa = PS.next()
            for kc in range(4):
                P.mm(pa, Wpa[:, kc, mb * 128:(mb + 1) * 128], at[:, kc, :], start=(kc == 0), stop=(kc == 3))
            pb = PS.next()
            for kc in range(4):
                P.mm(pb, Wpb[:, kc, mb * 128:(mb + 1) * 128], at[:, 4 + kc, :], start=(kc == 0), stop=(kc == 3))
            t1 = t_pool.next()
            t2 = t_pool.next()
            P.tt("dve", t1, pa, g[:, mb, :], ALU.mult)
            P.tt("dve", t2, pb, g[:, 8 + mb, :], ALU.mult)
            P.tt("dve", mx[:, mb, :], t1, t2, ALU.add)
        y = y_pool.next()
        for mb in range(8):
            po = PS.next()
            for kc in range(8):
                P.mm(po, Wo[:, kc, mb * 128:(mb + 1) * 128], mx[:, kc, :], start=(kc == 0), stop=(kc == 7))
            P.stt(y[:, mb, :], xf[:, mb, :], ALPHA, po, ALU.mult, ALU.add)

        def emit(b, t):
            P.dma(X1v[:, b, g0:g0 + TT_], t)
        layernorm(P, PS, y, "l1", vec, consts["ones32s"], consts["eps_ln"], sq_pool, st_pool, t_pool, emit)
    P.release(m)


def phaseD2a(P, cfg, PS, l, W, vec, S, consts):
    m = P.mark()
    Wup = P.sb("Wup", [128, 8, 2 * DFF], BF16)
    Wdn = P.sb("Wdn", [128, 22, D], BF16)
    ms_ = P.mark()
    stage = P.sbpool("stg", [128, 2048], F32, 2)
    load_w(P, Wup, W["w_ffn_up"], D, 2 * DFF, stage)
    load_w(P, Wdn, W["w_ffn_down"], DFF, D, stage)
    P.release(ms_)
    xf_pool = P.sbpool("exf", [128, 8, TT_], F32, 1)
    xb_pool = P.sbpool("exb", [128, 8, TT_], BF16, 1)
    xh_pool = P.sbpool("exh", [128, 8, 2], F32, 2)
    xhb_pool = P.sbpool("exhb", [128, 8, 2], BF16, 2)
    uh_pool = P.sbpool("euh", [128, 44, 2], F32, 2)
    ue_pool = P.sbpool("eue", [128, TT_ + 2], F32, 3)
    yc_pool = P.sbpool("eyc", [128, TT_], F32, 3)
    s_pool = P.sbpool("es", [128, TT_], F32, 3)
    gb_pool = P.sbpool("egb", [128, 22, TT_], BF16, 1)
    o_pool = P.sbpool("eo", [128, TT_], F32, 2)
    X1v = S["X1"].rr("(kc p) t -> p kc t", p=128)
    Y2v = S["Y2"].rr("(kc p) t -> p kc t", p=128)
    cw = [VOFF["cw0"][0], VOFF["cw1"][0], VOFF["cw2"][0]]
    cb = VOFF["cb"][0]
    for (si, j, g0, t0) in cfg.tiles():
        T = cfg.seqs[si]
        xf = xf_pool.next()
        P.dma(xf, X1v[:, :, g0:g0 + TT_])
        xh = xh_pool.next()
        if t0 == 0:
            P.memset("pool", xh[:, :, 0:1], 0.0)
        else:
            P.dma(xh[:, :, 0:1], X1v[:, :, g0 - 1:g0])
        if t0 + TT_ >= T:
            P.memset("pool", xh[:, :, 1:2], 0.0)
        else:
            P.dma(xh[:, :, 1:2], X1v[:, :, g0 + TT_:g0 + TT_ + 1])
        xb = xb_pool.next()
        P.copy("dve", xb[:, 0:4, :], xf[:, 0:4, :])
        P.copy("act", xb[:, 4:8, :], xf[:, 4:8, :])
        xhb = xhb_pool.next()
        P.copy("pool", xhb, xh)
        psh = PS.next()
        for b in range(44):
            for kc in range(8):
                P.mm(psh[:, 2 * b:2 * b + 2], Wup[:, kc, b * 128:(b + 1) * 128], xhb[:, kc, :], start=(kc == 0), stop=(kc == 7))
        uh = uh_pool.next()
        P.copy("dve", uh.rr("p b c -> p (b c)"), psh[:, 0:88])
        gb = gb_pool.next()

        def conv_block(b):
            ps = PS.next()
            for kc in range(8):
                P.mm(ps, Wup[:, kc, b * 128:(b + 1) * 128], xb[:, kc, :], start=(kc == 0), stop=(kc == 7))
            ue = ue_pool.next()
            P.act(ue[:, 1:TT_ + 1], ps, AF.Copy)
            P.copy("pool", ue[:, 0:1], uh[:, b, 0:1])
            P.copy("pool", ue[:, TT_ + 1:TT_ + 2], uh[:, b, 1:2])
            yc = yc_pool.next()
            P.act(yc, ue[:, 1:TT_ + 1], AF.Identity, bias=vec[:, cb + b:cb + b + 1], scale=vec[:, cw[1] + b:cw[1] + b + 1])
            P.stt(yc, ue[:, 0:TT_], vec[:, cw[0] + b:cw[0] + b + 1], yc, ALU.mult, ALU.add)
            P.stt(yc, ue[:, 2:TT_ + 2], vec[:, cw[2] + b:cw[2] + b + 1], yc, ALU.mult, ALU.add)
            return yc
        for i in range(22):
            ya = conv_block(i)
            yb = conv_block(22 + i)
            s = s_pool.next()
            P.act(s, ya, AF.Square)
            P.ts("dve", s, s, 0.044715, ALU.mult, 1.0, ALU.add)
            P.tt("pool", s, s, ya, ALU.mult)
            P.act(s, s, AF.Sigmoid, scale=1.5957691216057308)
            P.tt("dve", s, s, ya, ALU.mult)
            P.tt("dve", gb[:, i, :], s, yb, ALU.mult)
        for mb in range(8):
            ps = PS.next()
            for kc in range(22):
                P.mm(ps, Wdn[:, kc, mb * 128:(mb + 1) * 128], gb[:, kc, :], start=(kc == 0), stop=(kc == 21))
            o = o_pool.next()
            P.stt(o, xf[:, mb, :], ALPHA, ps, ALU.mult, ALU.add)
            P.dma(Y2v[:, mb, g0:g0 + TT_], o)
    P.release(m)


def phaseD2b(P, cfg, PS, l, W, vec, XTn, y_out, S, PTl, consts, last):
    m = P.mark()
    Wpg = P.sb("Wpg", [128, 8, D], BF16)
    Wpp = P.sb("Wpp", [128, 2, D], BF16)
    ms_ = P.mark()
    stage = P.sbpool("stg", [128, 2048], F32, 2)
    load_w(P, Wpg, W["w_pe_gate"], D, D, stage)
    load_w(P, Wpp, W["w_pe_proj"], PD, D, stage)
    P.release(ms_)
    xf_pool = P.sbpool("fxf", [128, 8, TT_], F32, 1)
    xb_pool = P.sbpool("fxb", [128, 8, TT_], BF16, 2)
    y_pool = P.sbpool("fy", [128, 8, TT_], F32, 2)
    pt_pool = P.sbpool("fpt", [128, 2, TT_], BF16, 2)
    t_pool = P.sbpool("ft", [128, TT_], F32, 4)
    sq_pool = P.sbpool("fsq", [128, TT_], F32, 2)
    st_pool = P.sbpool("fst", [128, TT_], F32, 3)
    x2_pool = P.sbpool("fx2", [128, 8, TT_], F32, 1)
    yo_pool = P.sbpool("fyo", [128, D], F32, 2)
    X1v = S["X1"].rr("(kc p) t -> p kc t", p=128)
    Y2v = S["Y2"].rr("(kc p) t -> p kc t", p=128)
    PTv = PTl.rr("(kc p) t -> p kc t", p=128)
    if not last:
        XTv = XTn.rr("(kc p) t -> p kc t", p=128)
    k = 0
    for (si, j, g0, t0) in cfg.tiles():
        xf = xf_pool.next()
        P.dma(xf, X1v[:, :, g0:g0 + TT_])
        y = y_pool.next()
        P.dma(y, Y2v[:, :, g0:g0 + TT_])
        pt = pt_pool.next()
        P.dma(pt, PTv[:, :, g0:g0 + TT_])
        xb = xb_pool.next()
        P.copy("dve", xb[:, 0:4, :], xf[:, 0:4, :])
        P.copy("act", xb[:, 4:8, :], xf[:, 4:8, :])
        for mb in range(8):
            pg = PS.next()
            for kc in range(8):
                P.mm(pg, Wpg[:, kc, mb * 128:(mb + 1) * 128], xb[:, kc, :], start=(kc == 0), stop=(kc == 7))
            pp = PS.next()
            for kc in range(2):
                P.mm(pp, Wpp[:, kc, mb * 128:(mb + 1) * 128], pt[:, kc, :], start=(kc == 0), stop=(kc == 1))
            sg = t_pool.next()
            P.act(sg, pg, AF.Sigmoid)
            P.tt("dve", sg, pp, sg, ALU.mult)
            P.tt("dve", y[:, mb, :], y[:, mb, :], sg, ALU.add)
        if not last:
            def emit(b, t):
                P.dma(XTv[:, b, g0:g0 + TT_], t)
            layernorm(P, PS, y, "l2", vec, consts["ones32s"], consts["eps_ln"], sq_pool, st_pool, t_pool, emit)
        else:
            x2 = x2_pool.next()

            def emit(b, t):
                P.copy("act", x2[:, b, :], t)
            layernorm(P, PS, y, "l2", vec, consts["ones32s"], consts["eps_ln"], sq_pool, st_pool, t_pool, emit)
            for tb in range(4):
                yo = yo_pool.next()
                for half in range(2):
                    ps = PS.next()
                    for q in range(4):
                        kc = half * 4 + q
                        P.tr(ps[:, q * 128:(q + 1) * 128], x2[:, kc, tb * 128:(tb + 1) * 128], consts["ident"])
                    P.copy(("act", "dve")[k % 2], yo[:, half * 512:(half + 1) * 512], ps)
                    k += 1
                P.dma(y_out[g0 + tb * 128:g0 + (tb + 1) * 128, :], yo)
    P.release(m)
WKEYS0 = ["w_in", "w_krr", "wq_n", "wq_r", "wq_rr", "wk", "wv", "w_lu", "a_lu", "g_lu", "w_pa", "w_pb", "w_o",
          "w_ffn_up", "w_ffn_down", "w_pe_gate", "w_pe_proj"]
WSHAPES = {"w_in": (D, INC), "w_krr": (D, 32), "wq_n": (QL, 512), "wq_r": (QL, 256), "wq_rr": (QL, 256),
           "wk": (KVL, 512), "wv": (KVL, 512), "w_lu": (128, C), "a_lu": (128, C), "g_lu": (128, C),
           "v_ld": (D, 32), "v_lu": (32, C), "w_pa": (C, D), "w_pb": (C, D), "w_o": (D, D),
           "w_ffn_up": (D, 2 * DFF), "w_ffn_down": (DFF, D), "w_pe_gate": (D, D), "w_pe_proj": (PD, D)}


def wkeys(l):
    return WKEYS0 + (["v_ld", "v_lu"] if l > 0 else [])


def build(cfg):
    nc = bass.Bass("TRN2", target_bir_lowering=False)
    P = Prog(nc)
    NT = cfg.NT
    dbg = cfg.debug

    def din(name, shape, dt=F32):
        return TT(nc.dram_tensor(name, list(shape), dt, kind="ExternalInput").ap(), Buf(name))
    x_in = din("x", [NT, D])
    p_in = din("p", [DEPTH, NT, PD])
    cd = {"cs": din("cs", [128, cfg.Tmax]), "sn": din("sn", [128, cfg.Tmax]), "ident_d": din("ident", [128, 128]),
          "masks_d": din("masks", [128, 4, 2, 128]), "bd_d": din("bd", [128, 128]), "rmask_d": din("rmask", [128, TT_])}
    Wd = []
    vecd = []
    for l in range(DEPTH):
        vecd.append(din("vec_%d" % l, [128, NVEC]))
        Wd.append({k: din("%s_%d" % (k, l), WSHAPES[k]) for k in wkeys(l)})
    y_out = TT(nc.dram_tensor("y", [NT, D], F32, kind="ExternalOutput").ap(), Buf("y"))
    kind = "ExternalOutput" if dbg else "Internal"
    S = {}
    for name, shape, dt in [("XT0", [D, NT], F32), ("XT1", [D, NT], F32), ("PT0", [PD, NT], BF16), ("PT1", [PD, NT], BF16),
                            ("QT", [H, QK, NT], BF16), ("KT", [H, QK, NT], BF16), ("V", [NT, 512], BF16),
                            ("Z", [1920, NT], F32), ("G", [2048, NT], F32), ("ATT", [512, NT], BF16),
                            ("RW", [512, NT], BF16), ("VF", [512, NT], F32), ("YF", [512, NT], F32),
                            ("BON", [512, NT], F32), ("X1", [D, NT], F32), ("Y2", [D, NT], F32)]:
        S[name] = P.dram(name, shape, dt, kind=kind)
    PSall = Pool([P.psum("ps%d" % i, [128, 512], F32) for i in range(8)])
    ident = P.sb("ident_sb", [128, 128], F32)
    P.dma(ident, cd["ident_d"])
    cd["ident"] = ident
    ones32s = P.sb("ones32s", [128, 128], F32)
    P.memset("pool", ones32s, 1.0 / D)
    cd["ones32s"] = ones32s
    for nm, val in [("eps_rms", RMS_EPS), ("eps_ln", LN_EPS), ("eps_12", 1e-12), ("eps_gn", GN_EPS)]:
        t = P.sb(nm, [128, 1], F32)
        P.memset("pool", t, val)
        cd[nm] = t
    vecs = []
    for l in range(DEPTH):
        v = P.sb("vec%d" % l, [128, NVEC], F32)
        P.dma(v, vecd[l])
        vecs.append(v)
    ph = cfg.phases
    XTs = [S["XT0"], S["XT1"]]
    PTs = [S["PT0"], S["PT1"]]
    if "0" in ph:
        phase0(P, cfg, PSall, x_in, p_in, XTs[0], PTs, ident)
    for l in range(cfg.nlayers if hasattr(cfg, "nlayers") else DEPTH):
        XT = XTs[l % 2]
        XTn = XTs[(l + 1) % 2]
        last = (l == DEPTH - 1)
        if "A" in ph:
            phaseA(P, cfg, PSall, l, Wd[l], vecs[l], XT, S, cd)
        if "B" in ph:
            phaseB(P, cfg, PSall, l, S, cd)
        if "C" in ph:
            phaseC(P, cfg, PSall, l, Wd[l], vecs[l], XT, S, cd)
        if "D" in ph:
            phaseD1(P, cfg, PSall, l, Wd[l], vecs[l], XT, S, cd)
        if "E" in ph:
            phaseD2a(P, cfg, PSall, l, Wd[l], vecs[l], S, cd)
            phaseD2b(P, cfg, PSall, l, Wd[l], vecs[l], XTn, y_out, S, PTs[l], cd, last)
    P.finish([y_out])
    return nc, P


def host_inputs(cfg, w, x_cores, p_cores):
    c = host_consts(cfg.Tmax)
    base = {"cs": c["cs"], "sn": c["sn"], "ident": c["ident"],
            "masks": np.ascontiguousarray(np.repeat(c["masks"].reshape(128, 4, 1, 128), 2, axis=2)),
            "bd": c["bd"], "rmask": c["rmask"]}
    for l in range(DEPTH):
        hp = host_layer_params(w, l)
        base["vec_%d" % l] = hp["vec"]
        for k in wkeys(l):
            base["%s_%d" % (k, l)] = np.ascontiguousarray(hp[k], dtype=np.float32)
    maps = []
    for xc, pc in zip(x_cores, p_cores):
        m = dict(base)
        m["x"] = np.ascontiguousarray(xc, dtype=np.float32)
        m["p"] = np.ascontiguousarray(pc, dtype=np.float32)
        maps.append(m)
    return maps


def kernel(**inputs):
    w = {k: np.asarray(v) for k, v in inputs.items()}
    xp, xs, pp, ps_ = w["x_prompt"], w["x_sample"], w["p_prompt"], w["p_sample"]
    n = 8
    Bp, Tp = xp.shape[0], xp.shape[1]
    Bs, Ts = xs.shape[0], xs.shape[1]
    npc = Bp // n
    nsc = Bs // n
    cfg = Cfg([Tp] * npc + [Ts] * nsc)
    x_cores, p_cores = [], []
    for c in range(n):
        xl = [xp[c * npc + i] for i in range(npc)] + [xs[c * nsc + i] for i in range(nsc)]
        pl = [pp[:, c * npc + i] for i in range(npc)] + [ps_[:, c * nsc + i] for i in range(nsc)]
        x_cores.append(np.concatenate(xl, 0))
        p_cores.append(np.concatenate(pl, 1))
    nc, P = build(cfg)
    maps = host_inputs(cfg, w, x_cores, p_cores)
    res = run_bass_kernel_spmd(nc, maps, core_ids=list(range(n)))
    yp = np.zeros(xp.shape, np.float32)
    ys = np.zeros(xs.shape, np.float32)
    for c in range(n):
        y = np.asarray(res.results[c]["y"], np.float32)
        o = 0
        for i in range(npc):
            yp[c * npc + i] = y[o:o + Tp]
            o += Tp
        for i in range(nsc):
            ys[c * nsc + i] = y[o:o + Ts]
            o += Ts
    return (yp, ys)
```

```python
import numpy as np
import concourse.bass as bass
import concourse.mybir as mybir
from concourse.bass_utils import run_bass_kernel_spmd

F32 = mybir.dt.float32
BF16 = mybir.dt.bfloat16
AF = mybir.ActivationFunctionType
ALU = mybir.AluOpType


class Buf:
    __slots__ = ("name", "w", "r", "psum")

    def __init__(self, name):
        self.name = name
        self.psum = False
        self.w = {}
        self.r = {}


class TT:
    __slots__ = ("ap", "buf")

    def __init__(self, ap, buf):
        self.ap = ap
        self.buf = buf

    def __getitem__(self, idx):
        return TT(self.ap[idx], self.buf)

    def rr(self, pat, **kw):
        return TT(self.ap.rearrange(pat, **kw), self.buf)


class Pool:
    def __init__(self, tiles):
        self.tiles = tiles
        self.i = 0

    def next(self):
        t = self.tiles[self.i % len(self.tiles)]
        self.i += 1
        return t


class Prog:
    def __init__(self, nc, n_dma_sems=40):
        self.nc = nc
        self.E = {"pe": nc.tensor, "dve": nc.vector, "act": nc.scalar, "pool": nc.gpsimd, "sp": nc.sync}
        self.sems = []
        self.semval = []
        self.esem = {}
        for e in self.E:
            self.esem[e] = self._newsem("s_" + e)
        self.dsems = [self._newsem("d%d" % i) for i in range(n_dma_sems)]
        self.di = 0
        self.known = {e: {} for e in self.E}
        self.ninst = {e: 0 for e in self.E}
        self._stack = []

    def _newsem(self, name):
        h = self.nc.alloc_semaphore(name)
        self.sems.append(h)
        self.semval.append(0)
        return len(self.sems) - 1

    def sb(self, name, shape, dtype):
        self._uid = getattr(self, "_uid", 0) + 1
        name = "%s_u%d" % (name, self._uid)
        cm = self.nc.sbuf_tensor(name, list(shape), dtype)
        t = cm.__enter__()
        self._stack.append(cm)
        return TT(t[tuple(slice(None) for _ in shape)], Buf(name))

    def sbpool(self, name, shape, dtype, n):
        return Pool([self.sb("%s%d" % (name, i), shape, dtype) for i in range(n)])

    def psum(self, name, shape, dtype):
        cm = self.nc.psum_tensor(name, list(shape), dtype)
        t = cm.__enter__()
        self._stack.append(cm)
        b = Buf(name)
        b.psum = True
        return TT(t[tuple(slice(None) for _ in shape)], b)

    def mark(self):
        return len(self._stack)

    def barrier(self):
        for e in self.E:
            for s in range(len(self.sems)):
                if self.semval[s] > 0:
                    self._wait(e, s, self.semval[s])

    def release(self, mark):
        self.barrier()
        while len(self._stack) > mark:
            cm = self._stack.pop()
            cm.__exit__(None, None, None)

    def dram(self, name, shape, dtype, kind="Internal"):
        t = self.nc.dram_tensor(name, list(shape), dtype, kind=kind)
        return TT(t.ap(), Buf(name))

    def _wait(self, eng, sem, val):
        k = self.known[eng]
        if k.get(sem, 0) >= val:
            return
        self.E[eng].wait_ge(self.sems[sem], val)
        k[sem] = val

    def _pre(self, eng, reads, writes, acc=False):
        for t in reads:
            for s, v in t.buf.w.items():
                self._wait(eng, s, v)
            if t.buf.psum:
                for s, v in t.buf.r.items():
                    if s != self.esem[eng]:
                        self._wait(eng, s, v)
        for t in writes:
            b = t.buf
            for s, v in b.w.items():
                if eng == "pe" and s == self.esem["pe"]:
                    continue
                self._wait(eng, s, v)
            for s, v in b.r.items():
                if eng == "pe" and s == self.esem["pe"]:
                    continue
                self._wait(eng, s, v)

    def _post(self, eng, ins, reads, writes):
        s = self.esem[eng]
        self.semval[s] += 1
        v = self.semval[s]
        ins.then_inc(self.sems[s], 1)
        self.ninst[eng] += 1
        for t in reads:
            t.buf.r[s] = v
        for t in writes:
            t.buf.w[s] = v
            t.buf.r = {}

    @staticmethod
    def _ap(x):
        return x.ap if isinstance(x, TT) else x

    def _tts(self, *xs):
        return [x for x in xs if isinstance(x, TT)]

    def mm(self, out, lhsT, rhs, start=True, stop=True):
        self._pre("pe", [lhsT, rhs], [out])
        ins = self.nc.tensor.matmul(out.ap, lhsT.ap, rhs.ap, start=start, stop=stop)
        self._post("pe", ins, [lhsT, rhs], [out])

    def tr(self, out, in_, ident):
        self._pre("pe", [in_, ident], [out])
        ins = self.nc.tensor.transpose(out.ap, in_.ap, ident.ap)
        self._post("pe", ins, [in_, ident], [out])

    def act(self, out, in_, func=None, bias=None, scale=None):
        func = func if func is not None else AF.Copy
        rd = self._tts(in_, bias, scale)
        self._pre("act", rd, [out])
        kw = {}
        if bias is not None:
            kw["bias"] = self._ap(bias)
        if scale is not None:
            kw["scale"] = self._ap(scale)
        ins = self.nc.scalar.activation(out.ap, in_.ap, func, **kw)
        self._post("act", ins, rd, [out])

    def tt(self, eng, out, a, b, op):
        self._pre(eng, [a, b], [out])
        ins = self.E[eng].tensor_tensor(out.ap, a.ap, b.ap, op)
        self._post(eng, ins, [a, b], [out])

    def ts(self, eng, out, a, s1, op0, s2=None, op1=None):
        rd = self._tts(a, s1, s2)
        self._pre(eng, rd, [out])
        if op1 is None:
            ins = self.E[eng].tensor_scalar(out.ap, a.ap, self._ap(s1), None, op0)
        else:
            ins = self.E[eng].tensor_scalar(out.ap, a.ap, self._ap(s1), self._ap(s2), op0, op1)
        self._post(eng, ins, rd, [out])

    def stt(self, out, a, s, b, op0, op1):
        rd = self._tts(a, s, b)
        self._pre("dve", rd, [out])
        ins = self.nc.vector.scalar_tensor_tensor(out.ap, a.ap, self._ap(s), b.ap, op0, op1)
        self._post("dve", ins, rd, [out])

    def copy(self, eng, out, in_):
        if eng == "act":
            return self.act(out, in_, AF.Copy)
        self._pre(eng, [in_], [out])
        ins = self.E[eng].tensor_copy(out.ap, in_.ap)
        self._post(eng, ins, [in_], [out])

    def memset(self, eng, out, val):
        self._pre(eng, [], [out])
        ins = self.E[eng].memset(out.ap, val)
        self._post(eng, ins, [], [out])

    def recip(self, out, in_):
        self._pre("dve", [in_], [out])
        ins = self.nc.vector.reciprocal(out.ap, in_.ap)
        self._post("dve", ins, [in_], [out])

    def scan(self, out, d0, d1, init, op0, op1):
        rd = self._tts(d0, d1, init)
        self._pre("dve", rd, [out])
        ins = self.nc.vector.tensor_tensor_scan(out.ap, d0.ap, d1.ap, self._ap(init), op0, op1)
        self._post("dve", ins, rd, [out])

    def dma(self, out, in_, q="sp"):
        self._pre(q, [in_], [out])
        s = self.dsems[self.di % len(self.dsems)]
        self.di += 1
        self._wait(q, s, self.semval[s])
        self.semval[s] += 16
        v = self.semval[s]
        self.E[q].dma_start(out=out.ap, in_=in_.ap, allow_slow_non_contiguous=True).then_inc(self.sems[s], 16)
        in_.buf.r[s] = v
        out.buf.w[s] = v
        out.buf.r = {}

    def finish(self, outs):
        for s in self.dsems:
            if self.semval[s] > 0:
                self._wait("sp", s, self.semval[s])
        self.release(0)
D = 1024
DEPTH = 2
H = 8
NOPE, ROPE, VD = 64, 32, 64
QK = 96
QL, KVL = 768, 256
C = 512
DFF = 2816
PD = 256
ALPHA = (2 * DEPTH) ** 0.25
LN_EPS = 1e-5
RMS_EPS = 1e-6
GN_EPS = 64e-5
DECAY_SCALE = 0.606531
OFF_CKV, OFF_KR, OFF_RW = 768, 1024, 1056
OFF_GA = OFF_RW + 1920
OFF_GB = OFF_GA + 1024
INC = OFF_GB + 1024
TT_ = 512
CH = 128

VOFF = {}
_o = 0
for _n, _w in [("qg", 6), ("kvg", 2), ("mu", 15), ("w0", 8), ("a0", 8), ("kk", 4), ("ka", 4), ("rk", 4),
               ("v0", 4), ("lng", 4), ("lnb", 4), ("l1g", 8), ("l1b", 8), ("cw0", 44), ("cw1", 44),
               ("cw2", 44), ("cb", 44), ("l2g", 8), ("l2b", 8)]:
    VOFF[_n] = (_o, _w)
    _o += _w
NVEC = _o


def host_consts(Tmax):
    c = {}
    c["ident"] = np.eye(128, dtype=np.float32)
    pos = np.arange(Tmax, dtype=np.float32)
    inv = (np.float32(10000.0) ** (-np.arange(0, ROPE, 2, dtype=np.float32) / np.float32(ROPE))).astype(np.float32)
    ang = (pos[:, None] * inv[None, :]).astype(np.float32)
    cos = np.cos(ang).astype(np.float32).T
    sin = np.sin(ang).astype(np.float32).T
    c["cs"] = np.ascontiguousarray(np.tile(np.concatenate([cos, cos], 0), (4, 1)))
    c["sn"] = np.ascontiguousarray(np.tile(np.concatenate([sin, sin], 0), (4, 1)))
    s = np.arange(128)[:, None]
    t = np.arange(128)[None, :]
    m = np.zeros((128, 4, 128), np.float32)
    m[:, 0] = (s < t)
    m[:, 1] = (s <= t)
    m[:, 2] = (s > t)
    m[:, 3] = (s >= t)
    c["masks"] = m.reshape(128, 512)
    bd = np.zeros((128, 128), np.float32)
    bd[:64, :64] = 1
    bd[64:, 64:] = 1
    c["bd"] = bd
    rm = np.ones((128, TT_), np.float32)
    rm[:, ::CH] = 0
    c["rmask"] = rm
    return c


def blk(v, n):
    return np.ascontiguousarray(np.asarray(v, np.float32).reshape(n, 128).T)


def host_layer_params(w, l):
    o = {}
    vec = np.zeros((128, NVEC), np.float32)

    def put(name, arr):
        a, n = VOFF[name]
        assert arr.shape == (128, n), (name, arr.shape)
        vec[:, a:a + n] = arr
    put("qg", blk(w["q_norm_g"][l], 6))
    put("kvg", blk(w["kv_norm_g"][l], 2))
    put("mu", blk(w["tshift_mu"][l], 15))
    put("w0", blk(w["w0"][l].reshape(-1), 8))
    put("a0", blk(w["a0"][l].reshape(-1), 8))
    put("kk", blk(w["k_k"][l], 4))
    put("ka", blk(w["k_a"][l], 4))
    put("rk", blk(w["r_k"][l], 4))
    if l > 0:
        put("v0", blk(w["v0"][l - 1], 4))
    put("lng", blk(w["lnx_g"][l], 4))
    put("lnb", blk(w["lnx_b"][l], 4))
    put("l1g", blk(w["ln1_g"][l], 8))
    put("l1b", blk(w["ln1_b"][l], 8))
    for i in range(3):
        put("cw%d" % i, blk(w["conv_w"][l, i], 44))
    put("cb", blk(w["conv_b"][l], 44))
    put("l2g", blk(w["ln2_g"][l], 8))
    put("l2b", blk(w["ln2_b"][l], 8))
    o["vec"] = vec
    win = np.asarray(w["w_in"][l], np.float32)
    o["w_in"] = win
    o["w_krr"] = np.ascontiguousarray(np.concatenate([win[:, OFF_KR + 16:OFF_KR + 32], win[:, OFF_KR:OFF_KR + 16]], 1))
    wq = np.asarray(w["w_uq"][l], np.float32).reshape(QL, H, QK)
    o["wq_n"] = np.ascontiguousarray(wq[:, :, :NOPE].reshape(QL, H * NOPE))
    o["wq_r"] = np.ascontiguousarray(wq[:, :, NOPE:].reshape(QL, H * ROPE))
    o["wq_rr"] = np.ascontiguousarray(np.concatenate([wq[:, :, NOPE + 16:], wq[:, :, NOPE:NOPE + 16]], 2).reshape(QL, H * ROPE))
    wkv = np.asarray(w["w_ukv"][l], np.float32).reshape(KVL, H, NOPE + VD)
    o["wk"] = np.ascontiguousarray(wkv[:, :, :NOPE].reshape(KVL, H * NOPE))
    o["wv"] = np.ascontiguousarray(wkv[:, :, NOPE:].reshape(KVL, H * VD))
    o["w_lu"] = np.ascontiguousarray(np.asarray(w["w_lora_up"][l], np.float32).reshape(128, C))
    o["a_lu"] = np.ascontiguousarray(np.asarray(w["a_lora_up"][l], np.float32).reshape(128, C))
    o["g_lu"] = np.asarray(w["g_lora_up"][l], np.float32)
    if l > 0:
        o["v_ld"] = np.asarray(w["v_lora_down"][l - 1], np.float32)
        o["v_lu"] = np.asarray(w["v_lora_up"][l - 1], np.float32)
    for k in ["w_pa", "w_pb", "w_o", "w_ffn_up", "w_ffn_down", "w_pe_gate", "w_pe_proj"]:
        o[k] = np.asarray(w[k][l], np.float32)
    return o


class Cfg:
    def __init__(self, seqs, debug=False, phases="0ABCDE"):
        self.seqs = list(seqs)
        self.NT = sum(seqs)
        self.off = [sum(seqs[:i]) for i in range(len(seqs))]
        self.Tmax = max(seqs)
        self.debug = debug
        self.phases = phases

    def tiles(self):
        for si, T in enumerate(self.seqs):
            for j in range(T // TT_):
                yield si, j, self.off[si] + j * TT_, j * TT_
def load_w(P, dst, src, K, N, stage, engs=("dve", "pool"), scale_vec=None, neg=None):
    KC = (K + 127) // 128
    i = 0
    for kc in range(KC):
        rows = min(128, K - kc * 128)
        for c0 in range(0, N, 2048):
            cw = min(2048, N - c0)
            st = stage.next()
            P.dma(st[0:rows, 0:cw], src[kc * 128:kc * 128 + rows, c0:c0 + cw])
            eng = engs[i % len(engs)]
            i += 1
            if scale_vec is None:
                P.copy(eng, dst[0:rows, kc, c0:c0 + cw], st[0:rows, 0:cw])
            else:
                P.ts(eng, dst[0:rows, kc, c0:c0 + cw], st[0:rows, 0:cw], scale_vec[0:rows, kc:kc + 1], ALU.mult)
    if neg is not None:
        for (a, b) in neg:
            P.ts("pool", dst[:, :, a:b], dst[:, :, a:b], -1.0, ALU.mult)


def phase0(P, cfg, PS, x_in, p_in, XT, PTs, ident):
    m = P.mark()
    xin_pool = P.sbpool("p0x", [128, 4, 1024], F32, 2)
    xf_pool = P.sbpool("p0f", [128, 8, 512], F32, 2)
    pin_pool = P.sbpool("p0p", [128, 4, 256], F32, 2)
    pb_pool = P.sbpool("p0b", [128, 2, 512], BF16, 2)
    XTv = XT.rr("(kc p) t -> p kc t", p=128)
    k = 0
    for (si, j, g0, t0) in cfg.tiles():
        xin = xin_pool.next()
        P.dma(xin, x_in[g0:g0 + TT_, :].rr("(tb p) f -> p tb f", p=128))
        xf = xf_pool.next()
        for kc in range(8):
            ps = PS.next()
            for tb in range(4):
                P.tr(ps[:, tb * 128:(tb + 1) * 128], xin[:, tb, kc * 128:(kc + 1) * 128], ident)
            P.copy(("act", "dve")[k % 2], xf[:, kc, :], ps)
            k += 1
        P.dma(XTv[:, :, g0:g0 + TT_], xf)
        for l in range(DEPTH):
            pin = pin_pool.next()
            P.dma(pin, p_in[l, g0:g0 + TT_, :].rr("(tb p) f -> p tb f", p=128))
            pb = pb_pool.next()
            for kc in range(2):
                ps = PS.next()
                for tb in range(4):
                    P.tr(ps[:, tb * 128:(tb + 1) * 128], pin[:, tb, kc * 128:(kc + 1) * 128], ident)
                P.copy(("act", "dve")[k % 2], pb[:, kc, :], ps)
                k += 1
            P.dma(PTs[l].rr("(kc p) t -> p kc t", p=128)[:, :, g0:g0 + TT_], pb)
    P.release(m)


def phaseA(P, cfg, PS, l, W, vec, XT, S, consts):
    m = P.mark()
    Win = P.sb("Win", [128, 8, INC], BF16)
    Wkrr = P.sb("Wkrr", [128, 8, 128], BF16)
    P.memset("pool", Wkrr, 0.0)
    Wqn = P.sb("Wqn", [128, 6, 512], BF16)
    Wqr = P.sb("Wqr", [128, 6, 256], BF16)
    Wqrr = P.sb("Wqrr", [128, 6, 256], BF16)
    Wk = P.sb("Wk", [128, 2, 512], BF16)
    Wv = P.sb("Wv", [128, 2, 512], BF16)
    ones = P.sb("onesb", [128, 128], BF16)
    P.memset("pool", ones, 1.0)
    ms_ = P.mark()
    stage = P.sbpool("stg", [128, 2048], F32, 2)
    qg = vec[:, VOFF["qg"][0]:VOFF["qg"][0] + 6]
    kvg = vec[:, VOFF["kvg"][0]:VOFF["kvg"][0] + 2]
    load_w(P, Win, W["w_in"], D, INC, stage)
    load_w(P, Wkrr[:, :, 0:32], W["w_krr"], D, 32, stage)
    P.ts("pool", Wkrr[:, :, 0:16], Wkrr[:, :, 0:16], -1.0, ALU.mult)
    load_w(P, Wqn, W["wq_n"], QL, 512, stage, scale_vec=qg)
    load_w(P, Wqr, W["wq_r"], QL, 256, stage, scale_vec=qg)
    load_w(P, Wqrr, W["wq_rr"], QL, 256, stage, scale_vec=qg)
    P.ts("pool", Wqrr.rr("p k (h r) -> p k h r", r=32)[:, :, :, 0:16], Wqrr.rr("p k (h r) -> p k h r", r=32)[:, :, :, 0:16], -1.0, ALU.mult)
    load_w(P, Wk, W["wk"], KVL, 512, stage, scale_vec=kvg)
    load_w(P, Wv, W["wv"], KVL, 512, stage, scale_vec=kvg)
    P.release(ms_)

    xf_pool = P.sbpool("axf", [128, 8, TT_], F32, 1)
    xb_pool = P.sbpool("axb", [128, 8, TT_], BF16, 2)
    cs_pool = P.sbpool("acs", [128, 2, TT_], F32, 2)
    cq_pool = P.sbpool("acq", [128, 8, TT_], F32, 1)
    sq_pool = P.sbpool("asq", [128, TT_], BF16, 3)
    cn_pool = P.sbpool("acn", [128, 8, TT_], BF16, 1)
    rs_pool = P.sbpool("ars", [128, 2, TT_], F32, 1)
    zo_pool = P.sbpool("azo", [128, TT_], F32, 4)
    qo_pool = P.sbpool("aqo", [128, TT_], BF16, 6)
    t1_pool = P.sbpool("at1", [128, TT_], F32, 3)
    vo_pool = P.sbpool("avo", [128, 4, 512], BF16, 1)
    XTv = XT.rr("(kc p) t -> p kc t", p=128)
    Zv = S["Z"].rr("(b p) t -> p b t", p=128)
    Gv = S["G"].rr("(b p) t -> p b t", p=128)
    tiles = list(cfg.tiles())
    PSacc = Pool(PS.tiles[0:2])
    PS = Pool(PS.tiles[2:8])
    import os
    AT = float(os.environ.get("AT", "99"))

    def load(i):
        si, j, g0, t0 = tiles[i]
        xf = xf_pool.next()
        P.dma(xf, XTv[:, :, g0:g0 + TT_])
        cs = cs_pool.next()
        P.dma(cs[:, 0, :], consts["cs"][:, t0:t0 + TT_])
        P.dma(cs[:, 1, :], consts["sn"][:, t0:t0 + TT_])
        return xf, cs
    if AT <= 0:
        P.release(m)
        return
    nxt = load(0)
    ev = 0
    for i, (si, j, g0, t0) in enumerate(tiles):
        xf, cs = nxt
        xb = xb_pool.next()
        P.copy("act", xb[:, 0:4, :], xf[:, 0:4, :])
        P.copy("dve", xb[:, 4:8, :], xf[:, 4:8, :])
        if i + 1 < len(tiles):
            nxt = load(i + 1)
        if cfg.debug and i == 0 and l == 0:
            dbg1 = P.dram("dbg_xb", [128, 8, TT_], BF16, kind="ExternalOutput")
            P.dma(dbg1, xb)
            for ii, cc in enumerate([0, 1024, 2048, 4096]):
                dbg2 = P.dram("dbg_win%d" % ii, [128, 8, 512], BF16, kind="ExternalOutput")
                P.dma(dbg2, Win[:, :, cc:cc + 512])
            dbg3 = P.dram("dbg_xf", [128, 8, TT_], F32, kind="ExternalOutput")
            P.dma(dbg3, xf)

        def proj(c0, mw, Wt=Win, KC=8, rhs=xb):
            ps = PS.next()
            for kc in range(KC):
                P.mm(ps[0:mw, :], Wt[:, kc, c0:c0 + mw], rhs[:, kc, :], start=(kc == 0), stop=(kc == KC - 1))
            return ps
        if AT <= 0.4:
            continue
        for b in range(15):
            ps = proj(OFF_RW + b * 128, 128)
            zo = zo_pool.next()
            P.copy(("act", "dve")[ev % 2], zo, ps)
            ev += 1
            P.dma(Zv[:, b, g0:g0 + TT_], zo)
            if cfg.debug and i == 0 and l == 0 and b == 0:
                dz = P.sb("dbgz", [128, TT_], F32)
                P.copy("dve", dz, ps)
                P.dma(P.dram("dbg_z0", [128, TT_], F32, kind="ExternalOutput"), dz)
                P.dma(P.dram("dbg_z1", [128, TT_], F32, kind="ExternalOutput"), zo)
        if AT <= 0.45:
            continue
        for b in range(16):
            ps = proj(OFF_GA + b * 128, 128)
            zo = zo_pool.next()
            P.act(zo, ps, AF.Sigmoid)
            P.dma(Gv[:, b, g0:g0 + TT_], zo)
        if AT <= 0.5:
            continue
        cq = cq_pool.next()
        cn = cn_pool.next()
        rs = rs_pool.next()
        ssq = [PSacc.next(), PSacc.next()]
        sqs = []
        for b in range(8):
            ps = proj(b * 128, 128)
            P.copy("dve", cq[:, b, :], ps)
            sq = sq_pool.next()
            P.act(sq, ps, AF.Square)
            sqs.append(sq)
            which = 0 if b < 6 else 1
            first = b in (0, 6)
            last = b in (5, 7)
            P.mm(ssq[which], ones, sq, start=first, stop=last)
        if cfg.debug and i == 0 and l == 0:
            dbg4 = P.dram("dbg_cq", [128, 8, TT_], F32, kind="ExternalOutput")
            P.dma(dbg4, cq)
        if AT <= 0.6:
            continue
        for which, (n, b0, b1) in enumerate([(QL, 0, 6), (KVL, 6, 8)]):
            P.act(rs[:, which, :], ssq[which], AF.Sqrt, bias=consts["eps_rms"], scale=1.0 / n)
            if AT <= 0.7:
                continue
            P.recip(rs[:, which, :], rs[:, which, :])
            if AT <= 0.8:
                continue
            for b in range(b0, b1):
                P.tt("dve", cn[:, b, :], cq[:, b, :], rs[:, which, :], ALU.mult)
        if AT <= 1:
            continue
        ps = proj(OFF_KR, 128)
        ps2 = proj(0, 128, Wt=Wkrr)
        t1 = t1_pool.next()
        t2 = t1_pool.next()
        P.tt("dve", t1[0:32, :], ps[0:32, :], cs[0:32, 0, :], ALU.mult)
        P.tt("dve", t2[0:32, :], ps2[0:32, :], cs[0:32, 1, :], ALU.mult)
        kr = qo_pool.next()
        P.tt("dve", kr[0:32, :], t1[0:32, :], t2[0:32, :], ALU.add)
        for h in range(H):
            P.dma(S["KT"][h, 64:96, g0:g0 + TT_], kr[0:32, :])
        if AT <= 4:
            continue
        scale = QK ** -0.5
        for b in range(4):
            ps = proj(b * 128, 128, Wt=Wqn, KC=6, rhs=cn)
            qo = qo_pool.next()
            P.act(qo, ps, AF.Copy, scale=scale)
            for hh in range(2):
                P.dma(S["QT"][2 * b + hh, 0:64, g0:g0 + TT_], qo[hh * 64:(hh + 1) * 64, :])
        for b in range(2):
            ps = proj(b * 128, 128, Wt=Wqr, KC=6, rhs=cn)
            ps2 = proj(b * 128, 128, Wt=Wqrr, KC=6, rhs=cn)
            t1 = t1_pool.next()
            t2 = t1_pool.next()
            P.tt("dve", t1, ps, cs[:, 0, :], ALU.mult)
            P.tt("dve", t2, ps2, cs[:, 1, :], ALU.mult)
            qo = qo_pool.next()
            P.tt("dve", t1, t1, t2, ALU.add)
            P.act(qo, t1, AF.Copy, scale=scale)
            for hh in range(4):
                P.dma(S["QT"][4 * b + hh, 64:96, g0:g0 + TT_], qo[hh * 32:(hh + 1) * 32, :])
        if AT <= 5:
            continue
        for b in range(4):
            ps = proj(b * 128, 128, Wt=Wk, KC=2, rhs=cn[:, 6:8, :])
            qo = qo_pool.next()
            P.copy(("act", "dve")[b % 2], qo, ps)
            for hh in range(2):
                P.dma(S["KT"][2 * b + hh, 0:64, g0:g0 + TT_], qo[hh * 64:(hh + 1) * 64, :])
        if AT <= 6:
            continue
        vo = vo_pool.next()
        for tb in range(4):
            ps = PS.next()
            for kc in range(2):
                P.mm(ps, cn[:, 6 + kc, tb * 128:(tb + 1) * 128], Wv[:, kc, :], start=(kc == 0), stop=(kc == 1))
            P.copy(("act", "dve")[tb % 2], vo[:, tb, :], ps)
        P.dma(S["V"][g0:g0 + TT_, :].rr("(tb p) c -> p tb c", p=128), vo)
    P.release(m)
def phaseB(P, cfg, PSall, l, S, consts):
    m = P.mark()
    PSs = Pool(PSall.tiles[0:5])
    PSo = Pool(PSall.tiles[5:7])
    PSb = Pool(PSall.tiles[7:8])
    Tm = cfg.Tmax
    kt_pool = P.sbpool("bkt", [96, Tm], BF16, 2)
    vh_pool = P.sbpool("bvh", [128, Tm // 128, 65], BF16, 2)
    for t in vh_pool.tiles:
        P.memset("pool", t[:, :, 64:65], 1.0)
    q_pool = P.sbpool("bq", [96, TT_], BF16, 3)
    pt_pool = P.sbpool("bpt", [128, TT_], BF16, 4)
    lrow_pool = P.sbpool("blr", [65, TT_], F32, 2)
    rec_pool = P.sbpool("brc", [64, TT_], F32, 2)
    ao_pool = P.sbpool("bao", [64, TT_], BF16, 3)
    ones32 = P.sb("bones", [65, 64], F32)
    P.memset("pool", ones32, 1.0)
    for si, T in enumerate(cfg.seqs):
        off = cfg.off[si]
        nk = T // 128
        for h in range(H):
            kt = kt_pool.next()
            vh = vh_pool.next()
            P.dma(kt[:, 0:T], S["KT"][h, :, off:off + T])
            for c0 in range(0, nk, 8):
                P.dma(vh[:, c0:c0 + 8, 0:64],
                      S["V"][off + c0 * 128:off + (c0 + 8) * 128, h * 64:(h + 1) * 64].rr("(c p) v -> p c v", p=128))
            for qt in range(T // TT_):
                g0 = off + qt * TT_
                q = q_pool.next()
                P.dma(q, S["QT"][h, :, g0:g0 + TT_])
                pso = PSo.next()
                pss = {}
                LA = 2
                for kc in range(min(LA, nk)):
                    pss[kc] = PSs.next()
                    P.mm(pss[kc], kt[:, kc * 128:(kc + 1) * 128], q)
                for kc in range(nk):
                    if kc + LA < nk:
                        pss[kc + LA] = PSs.next()
                        P.mm(pss[kc + LA], kt[:, (kc + LA) * 128:(kc + LA + 1) * 128], q)
                    pt = pt_pool.next()
                    P.act(pt, pss.pop(kc), AF.Exp)
                    P.mm(pso[0:65, :], vh[:, kc, :], pt, start=(kc == 0), stop=(kc == nk - 1))
                lrow = lrow_pool.next()
                P.copy("dve", lrow[64:65, :], pso[64:65, :])
                psb = PSb.next()
                P.mm(psb[0:64, :], ones32[64:65, :], lrow[64:65, :])
                rec = rec_pool.next()
                P.recip(rec, psb[0:64, :])
                ao = ao_pool.next()
                P.tt("dve", ao, pso[0:64, :], rec, ALU.mult)
                P.dma(S["ATT"][h * 64:(h + 1) * 64, g0:g0 + TT_], ao)
    P.release(m)
def run_rr(gens):
    gens = list(gens)
    while gens:
        nxt = []
        for g in gens:
            try:
                next(g)
                nxt.append(g)
            except StopIteration:
                pass
        gens = nxt


def phaseC(P, cfg, PS, l, W, vec, XT, S, consts):
    import os
    CT = float(os.environ.get("CT", "99"))
    TC = 256
    NCH = TC // CH
    m = P.mark()
    Wlu = P.sb("Wlu", [128, 1, C], BF16)
    Alu = P.sb("Alu", [128, 1, C], BF16)
    Glu = P.sb("Glu", [128, 1, C], BF16)
    if l > 0:
        Vld = P.sb("Vld", [128, 8, 128], BF16)
        Vlu = P.sb("Vlu", [128, 1, C], BF16)
    ms_ = P.mark()
    stage = P.sbpool("stg", [128, 2048], F32, 2)
    load_w(P, Wlu, W["w_lu"], 128, C, stage)
    load_w(P, Alu, W["a_lu"], 128, C, stage)
    load_w(P, Glu, W["g_lu"], 128, C, stage)
    if l > 0:
        P.memset("pool", Vld, 0.0)
        P.memset("pool", Vlu, 0.0)
        load_w(P, Vld[:, :, 0:32], W["v_ld"], D, 32, stage)
        load_w(P, Vlu, W["v_lu"], 32, C, stage)
    P.release(ms_)
    masks = P.sb("cmask", [128, 4, 2, 128], F32)
    P.dma(masks, consts["masks_d"])
    bdf = P.sb("cbdf", [128, 128], F32)
    P.dma(bdf, consts["bd_d"])
    bdb = P.sb("cbdb", [128, 128], BF16)
    P.copy("pool", bdb, bdf)
    bd64 = P.sb("cbd64", [128, 128], F32)
    P.ts("dve", bd64, bdf, 1.0 / 64, ALU.mult)
    id2 = P.sb("cid2", [128, 2, 128], F32)
    P.copy("pool", id2[:, 0, :], consts["ident"])
    P.copy("pool", id2[:, 1, :], consts["ident"])
    idb = P.sb("cidb", [128, 128], BF16)
    P.copy("pool", idb, consts["ident"])
    rmask = P.sb("crm", [128, TC], F32)
    P.dma(rmask, consts["rmask_d"][:, 0:TC])
    mo, _ = VOFF["mu"]
    om = P.sb("com", [128, 15], F32)
    hm = P.sb("chm", [128, 15], F32)
    P.ts("dve", om, vec[:, mo:mo + 15], -1.0, ALU.mult, 1.0, ALU.add)
    P.ts("dve", hm, vec[:, mo:mo + 15], 0.5, ALU.mult)
    eps12 = consts["eps_12"]
    epsg = consts["eps_gn"]

    zt_pool = P.sbpool("czt", [128, 3, TC + 2], F32, 2)
    zs_pool = P.sbpool("czs", [128, 15, TC], F32, 1)
    f_pool = P.sbpool("cf", [128, TC], F32, 4)
    fcb = [P.sbpool("cfc%d" % i, [128, TC], F32, 11) for i in range(4)]
    nt_pool = P.sbpool("cnt", [128, 4], F32, 8)
    bcb = [P.sbpool("cbc%d" % i, [128, TC], BF16, 2) for i in range(4)]
    ops_pool = P.sbpool("cops", [128, 4, 6, TC], BF16, 2)
    vb_pool = P.sbpool("cvb", [128, 4, TC], BF16, 2)
    pl_pool = P.sbpool("cpl", [128, 4, 4], F32, 3)
    yt_pool = P.sbpool("cyt", [128, 4, TC], F32, 2)
    sg_pool = P.sbpool("csg", [128, TC], BF16, 2)
    bon_pool = P.sbpool("cbon", [128, 4, TC], F32, 2)
    if l > 0:
        xh_pool = P.sbpool("cxh", [128, 2, TC], F32, 1)
        xb_pool = P.sbpool("cxb", [128, 8, TC], BF16, 1)
        vf_pool = P.sbpool("cvf", [128, 4, TC], F32, 1)
        xd_pool = P.sbpool("cxd", [128, TC], BF16, 1)
    tok_pool = P.sbpool("utok", [128, 4, 128], BF16, 8)
    mt_pool = P.sbpool("umt", [128, 6, 256], BF16, 4)
    mk_pool = P.sbpool("umk", [128, 256], BF16, 5)
    s_pool = P.sbpool("us", [128, 256], BF16, 5)
    nk_pool = P.sbpool("unk", [128, 256], BF16, 4)
    mr_pool = P.sbpool("umr", [128, 2, 256], BF16, 8)
    x1_pool = P.sbpool("ux1", [128, 128], BF16, 4)
    ut_pool = P.sbpool("uut", [128, 128], F32, 8)
    wt_pool = P.sbpool("uwt", [128, 128], BF16, 8)
    u_pool = P.sbpool("uu", [128, 128], BF16, 4)
    th_pool = P.sbpool("uth", [128, 128], F32, 4)
    pad_pools = []
    for nm in ("upB", "upA", "upR"):
        pp_ = P.sbpool(nm, [128, 2, 128], BF16, 4)
        for t_ in pp_.tiles:
            P.memset("pool", t_, 0.0)
        pad_pools.append(pp_)
    tzp = P.sb("ctzp", [128, TC], BF16)
    zap = P.sb("czap", [128, TC], BF16)
    f2_pool = P.sbpool("cf2", [128, TC], F32, 4)
    o_pool = P.sbpool("co", [128, TC], BF16, 2)
    Hf = [P.sb("Hf%d" % i, [128, 128], F32) for i in range(4)]
    Hb = [P.sb("Hb%d" % i, [128, 128], BF16) for i in range(4)]

    print("phaseC layer", l, "sbuf remaining", P.nc.sbuf_bytes_remaining)
    Zv = S["Z"].rr("(b p) t -> p b t", p=128)
    XTv = XT.rr("(kc p) t -> p kc t", p=128)
    YFv = S["YF"].rr("(b p) t -> p b t", p=128)
    BONv = S["BON"].rr("(b p) t -> p b t", p=128)
    VFv = S["VF"].rr("(b p) t -> p b t", p=128)
    RWv = S["RW"].rr("(b p) t -> p b t", p=128)
    V = VOFF
    mul, add, sub = ALU.mult, ALU.add, ALU.subtract

    def c1(si, g0, t0, d, res):
        T = cfg.seqs[si]
        zs = zs_pool.next()
        for gb in range(5):
            zt = zt_pool.next()
            P.dma(zt[:, :, 1:TC + 1], Zv[:, 3 * gb:3 * gb + 3, g0:g0 + TC])
            if t0 == 0:
                P.memset("pool", zt[:, :, 0:1], 0.0)
            else:
                P.dma(zt[:, :, 0:1], Zv[:, 3 * gb:3 * gb + 3, g0 - 1:g0])
            if t0 + TC >= T:
                P.memset("pool", zt[:, :, TC + 1:TC + 2], 0.0)
            else:
                P.dma(zt[:, :, TC + 1:TC + 2], Zv[:, 3 * gb:3 * gb + 3, g0 + TC:g0 + TC + 1])
            for q in range(3):
                b = 3 * gb + q
                t = f_pool.next()
                P.tt("dve", t, zt[:, q, 0:TC], zt[:, q, 2:TC + 2], add)
                P.act(zs[:, b, :], zt[:, q, 1:TC + 1], AF.Identity, scale=om[:, b:b + 1])
                P.stt(zs[:, b, :], t, hm[:, b:b + 1], zs[:, b, :], mul, add)
            yield
        hs = slice(64 * d, 64 * d + 64)
        tz = tzp
        P.act(tz[hs, :], zs[hs, 12, :], AF.Tanh)
        zab = zap
        P.copy("act", zab[hs, :], zs[hs, 13, :])
        if l > 0:
            xb = xb_pool.next()
            for hh in range(4):
                xh = xh_pool.next()
                P.dma(xh, XTv[:, 2 * hh:2 * hh + 2, g0:g0 + TC])
                P.copy(("dve", "act")[hh % 2], xb[:, 2 * hh:2 * hh + 2, :], xh)
            ps = PS.next()[:, 0:TC]
            for kc in range(8):
                P.mm(ps, Vld[:, kc, :], xb[:, kc, :], start=(kc == 0), stop=(kc == 7))
            xd = xd_pool.next()
            P.copy("act", xd, ps)
            vf = vf_pool.next()
            P.dma(vf, VFv[:, :, g0:g0 + TC])
            for cb in range(4):
                ps = PS.next()[:, 0:TC]
                P.mm(ps, Vlu[:, 0, cb * 128:(cb + 1) * 128], xd)
                vm = f_pool.next()
                P.act(vm, ps, AF.Sigmoid, bias=vec[:, V["v0"][0] + cb:V["v0"][0] + cb + 1])
                t = f_pool.next()
                P.tt("dve", t, vf[:, cb, :], zs[:, 8 + cb, :], sub)
                P.tt("dve", t, t, vm, mul)
                P.tt("dve", zs[:, 8 + cb, :], zs[:, 8 + cb, :], t, add)
        elif d == 0:
            P.dma(VFv[:, :, g0:g0 + TC], zs[:, 8:12, :])
        vb = vb_pool.next()
        P.copy("act", vb, zs[:, 8:12, :])
        yield
        ops = ops_pool.next()
        pl = pl_pool.next()
        gt = None
        bon = bon_pool.next()
        if d == 1:
            gt = sg_pool.next()
            P.act(gt, zs[:, 14, :], AF.Sigmoid)
        def chain(cb, f_pool, b_pool):
            r = zs[:, cb, :]
            k = zs[:, 4 + cb, :]
            v = zs[:, 8 + cb, :]
            col = lambda n: vec[:, V[n][0] + cb:V[n][0] + cb + 1]
            cold = lambda n: vec[:, V[n][0] + 4 * d + cb:V[n][0] + 4 * d + cb + 1]
            ps = PS.next()[:, 0:TC]
            P.mm(ps, Wlu[:, 0, cb * 128:(cb + 1) * 128], tz)
            lw = f_pool.next()
            P.act(lw, ps, AF.Sigmoid, bias=cold("w0"))
            yield
            P.act(lw, lw, AF.Identity, scale=-DECAY_SCALE)
            ps = PS.next()[:, 0:TC]
            P.mm(ps, Alu[:, 0, cb * 128:(cb + 1) * 128], zab)
            a = f_pool.next()
            P.act(a, ps, AF.Sigmoid, bias=cold("a0"))
            yield
            kkr = f_pool.next()
            P.act(kkr, k, AF.Identity, scale=col("kk"))
            sq = b_pool.next()
            P.act(sq, kkr, AF.Square)
            yield
            ps = PS.next()[:, 0:TC]
            P.mm(ps, bdb, sq)
            rs = f_pool.next()
            P.act(rs, ps, AF.Sqrt, bias=eps12, scale=1.0)
            yield
            P.recip(rs, rs)
            kk = kkr
            P.tt("dve", kk, kkr, rs, mul)
            kd = f_pool.next()
            P.ts("dve", kd, a, -1.0, add, col("ka"), mul)
            P.stt(kd, kd, 1.0, k, add, mul)
            yield
            kka = rs
            P.tt("dve", kka, kk, a, mul)
            yield
            t = f_pool.next()
            P.stt(t, r, col("rk"), kd, mul, mul)
            tb16 = b_pool.next()
            P.copy("act", tb16, t)
            yield
            ps = PS.next()[:, 0:TC]
            P.mm(ps, bdb, tb16)
            P.tt("dve", bon[:, cb, :], ps, v, mul)
            yield
            cum = f_pool.next()
            P.scan(cum, rmask, lw, 0.0, mul, add)
            yield
            E = lw
            P.tt("dve", E, cum, lw, sub)
            tot = cum.rr("p (c q) -> p c q", q=CH)[:, :, CH - 1]
            P.act(pl[:, cb, 0:NCH], tot, AF.Exp)
            ntot = nt_pool.next()
            P.ts("dve", ntot[:, 0:NCH], tot, -1.0, mul)
            yield
            pincl = f_pool.next()
            pexcl = f_pool.next()
            pinv = f_pool.next()
            pinv2 = a
            pinv2 = f_pool.next()
            if d == 0:
                P.act(pincl, cum, AF.Exp)
                P.act(pexcl, E, AF.Exp)
                P.act(pinv, cum, AF.Exp, scale=-1.0)
                for c in range(NCH):
                    cs = slice(c * CH, (c + 1) * CH)
                    P.act(pinv2[:, cs], cum[:, cs], AF.Exp, bias=cum[:, c * CH + CH - 1:c * CH + CH], scale=-1.0)
            else:
                P.act(pinv2, E, AF.Exp)
                for c in range(NCH):
                    cs = slice(c * CH, (c + 1) * CH)
                    tc_ = cum[:, c * CH + CH - 1:c * CH + CH]
                    P.act(pincl[:, cs], E[:, cs], AF.Exp, bias=tc_, scale=-1.0)
                    P.act(pexcl[:, cs], cum[:, cs], AF.Exp, bias=tc_, scale=-1.0)
                    P.act(pinv[:, cs], E[:, cs], AF.Exp, bias=ntot[:, c:c + 1], scale=1.0)
            yield
            P.tt("dve", ops[:, cb, 0, :], kk, pexcl, mul)
            P.tt("dve", ops[:, cb, 1, :], kka, pinv, mul)
            P.tt("dve", ops[:, cb, 2, :], kd, pinv, mul)
            P.tt("dve", ops[:, cb, 3, :], r, pincl, mul)
            P.tt("dve", ops[:, cb, 4, :], kka, pinv2, mul)
            P.tt("dve", ops[:, cb, 5, :], kd, pinv2, mul)
            yield
        yield from rr_gen([chain(cb, fcb[cb], bcb[cb]) for cb in range(4)])
        res.update(zs=zs, vb=vb, ops=ops, pl=pl, gt=gt, bon=bon)

    def unit_pre(d, c, cb, ops, vb, res):
        ms, msT, mi = (0, 2, 1) if d == 0 else (2, 0, 3)
        cs = slice(c * CH, (c + 1) * CH)
        Bt, At, Kt, Rt, At2, Kt2 = [ops[:, cb, i, cs] for i in range(6)]
        hp = [slice(0, 64), slice(64, 128)]
        m2 = lambda i: masks[:, i, :, :].rr("p a b -> p (a b)")
        pst = PS.next()
        pstb = TT(pst.ap.bitcast(BF16), pst.buf)
        for i, src in enumerate([Bt, At2, Kt2, vb[:, cb, cs]]):
            P.tr(pstb[:, i * 128:(i + 1) * 128], src, idb)
        tok = tok_pool.next()
        P.copy("act", tok.rr("p a b -> p (a b)"), pstb[:, 0:512])
        yield
        if CT <= 1.1:
            return
        pB, pA, pR = [pp_.next() for pp_ in pad_pools]
        for h in range(2):
            P.copy("act", pB[hp[h], h, :], Bt[hp[h], :])
            P.copy("dve", pA[hp[h], h, :], At[hp[h], :])
            P.copy(("act", "dve")[h], pR[hp[h], h, :], Rt[hp[h], :])
        f2 = lambda t_: t_.rr("p a b -> p (a b)")
        psN = PS.next()
        psT = PS.next()
        P.mm(psN[:, 0:256], At, f2(pB))
        P.mm(psT[:, 0:256], Bt, f2(pA))
        mt = mt_pool.next()
        mk = mk_pool.next()
        P.stt(mk, psN[:, 0:256], -1.0, m2(ms), mul, mul)
        P.stt(mt[:, 0, :], psT[:, 0:256], -1.0, m2(msT), mul, mul)
        yield
        if CT <= 1.2:
            return
        psK = PS.next()
        psA = PS.next()
        P.mm(psK[:, 0:256], Kt, f2(pB))
        P.mm(psA[:, 0:256], At, f2(pR))
        P.mm(psA[:, 256:512], Kt, f2(pR))
        nk = nk_pool.next()
        P.tt("dve", nk, psK[:, 0:256], m2(ms), mul)
        mr = mr_pool.next()
        P.tt("dve", mr[:, 0, :], psA[:, 0:256], m2(mi), mul)
        P.tt("dve", mr[:, 1, :], psA[:, 256:512], m2(mi), mul)
        yield
        if CT <= 1.3:
            return
        psX = PS.next()
        for h in range(2):
            P.mm(psX[:, h * 64:(h + 1) * 64], nk[:, h * 128:(h + 1) * 128], tok[:, 3, h * 64:(h + 1) * 64])
        x1 = x1_pool.next()
        P.copy("act", x1, psX[:, 0:128])
        yield
        if CT <= 1.4:
            return
        sb_ = None
        for kk_ in range(1, 7):
            psM = PS.next()
            for h in range(2):
                hs_ = slice(h * 128, (h + 1) * 128)
                P.mm(psM[:, hs_], mt[:, kk_ - 1, hs_], mk[:, hs_])
            if kk_ <= 5:
                psMT = PS.next()
                for h in range(2):
                    hs_ = slice(h * 128, (h + 1) * 128)
                    P.mm(psMT[:, hs_], mk[:, hs_], mt[:, kk_ - 1, hs_])
                mk = mk_pool.next()
                P.copy("act", mk, psM[:, 0:256])
                P.copy("dve", mt[:, kk_, :], psMT[:, 0:256])
            else:
                sb_ = s_pool.next()
                P.tt("dve", sb_, psM[:, 0:256], id2.rr("p a b -> p (a b)"), add)
            yield
        if CT <= 1.5:
            return
        for kk_ in range(5, -1, -1):
            psS = PS.next()
            for h in range(2):
                hs_ = slice(h * 128, (h + 1) * 128)
                P.mm(psS[:, hs_], mt[:, kk_, hs_], sb_[:, hs_])
            s2 = s_pool.next()
            P.tt("dve", s2, psS[:, 0:256], sb_, add)
            sb_ = s2
            yield
        psU = PS.next()
        for h in range(2):
            P.mm(psU[:, h * 64:(h + 1) * 64], sb_[:, h * 128:(h + 1) * 128], x1[:, h * 64:(h + 1) * 64])
        P.mm(psU[:, 128:384], tok[:, 0, :], sb_)
        ut = ut_pool.next()
        P.copy("act", ut, psU[:, 0:128])
        wt = wt_pool.next()
        P.copy("act", wt[0:64, :], psU[0:64, 128:256])
        P.copy("act", wt[64:128, :], psU[64:128, 256:384])
        res.update(tok=tok, mr=mr, ut=ut, wt=wt, Rt=Rt)
        yield

    def unit_seq(d, c, cb, pre, pl, yt):
        tok, mr, ut, wt, Rt = pre["tok"], pre["mr"], pre["ut"], pre["wt"], pre["Rt"]
        cs = slice(c * CH, (c + 1) * CH)
        psU = PS.next()
        P.mm(psU[:, 0:128], wt, Hb[cb])
        u = u_pool.next()
        P.stt(u, psU[:, 0:128], -1.0, ut, mul, sub)
        yield
        psO = PS.next()
        P.mm(psO[:, 0:256], tok[:, 3, :], mr[:, 1, :], start=True, stop=False)
        P.mm(psO[:, 0:256], u, mr[:, 0, :], start=False, stop=False)
        P.mm(psO[:, 0:128], Hb[cb], Rt, start=False, stop=False)
        P.mm(psO[:, 128:256], Hb[cb], Rt, start=False, stop=True)
        psH = PS.next()
        P.mm(psH[:, 0:128], tok[:, 2, :], tok[:, 3, :], start=True, stop=False)
        P.mm(psH[:, 0:128], tok[:, 1, :], u, start=False, stop=True)
        P.copy("act", yt[0:64, cb, cs], psO[0:64, 0:128])
        P.copy("act", yt[64:128, cb, cs], psO[64:128, 128:256])
        th = th_pool.next()
        P.tt("dve", th, psH[:, 0:128], bdf, mul)
        P.stt(Hf[cb], Hf[cb], pl[:, cb, c:c + 1], th, mul, add)
        P.copy("act", Hb[cb], Hf[cb])
        yield

    import os
    def rr_gen(gens):
        gens = list(gens)
        while gens:
            nxt = []
            for g in gens:
                try:
                    next(g)
                    nxt.append(g)
                except StopIteration:
                    pass
            gens = nxt
            yield

    def tile_units(d, g0, R):
        ops, vb, pl, gt, bon = R["ops"], R["vb"], R["pl"], R["gt"], R["bon"]
        yt = yt_pool.next()
        corder = list(range(NCH)) if d == 0 else list(range(NCH - 1, -1, -1))
        pres = {c: [dict() for _ in range(4)] for c in corder}
        yield from rr_gen([unit_pre(d, corder[0], cb, ops, vb, pres[corder[0]][cb]) for cb in range(4)])
        for idx, c in enumerate(corder):
            gl = [unit_seq(d, c, cb, pres[c][cb], pl, yt) for cb in range(4)]
            if idx + 1 < len(corder):
                cn = corder[idx + 1]
                gl += [unit_pre(d, cn, cb, ops, vb, pres[cn][cb]) for cb in range(4)]
            yield from rr_gen(gl)
        if d == 0:
            P.dma(YFv[:, :, g0:g0 + TC], yt)
            P.dma(BONv[:, :, g0:g0 + TC], bon)
        else:
            for cb in range(4):
                y0 = f2_pool.next()
                P.dma(y0, YFv[:, cb, g0:g0 + TC])
                b0 = f2_pool.next()
                P.dma(b0, BONv[:, cb, g0:g0 + TC])
                y = yt[:, cb, :]
                P.tt("dve", y, y, y0, add)
                P.tt("dve", b0, b0, bon[:, cb, :], add)
                pm = PS.next()
                P.mm(pm[:, 0:TC], bd64, y)
                sq = f2_pool.next()
                P.act(sq, y, AF.Square)
                pq = PS.next()
                P.mm(pq[:, 0:TC], bd64, sq)
                msq = f2_pool.next()
                P.act(msq, pm[:, 0:TC], AF.Square)
                var = sq
                P.tt("dve", var, pq[:, 0:TC], msq, sub)
                P.act(var, var, AF.Sqrt, bias=epsg, scale=1.0)
                P.recip(var, var)
                P.tt("dve", y, y, pm[:, 0:TC], sub)
                P.tt("dve", y, y, var, mul)
                P.ts("dve", y, y, vec[:, V["lng"][0] + cb:V["lng"][0] + cb + 1], mul,
                     vec[:, V["lnb"][0] + cb:V["lnb"][0] + cb + 1], add)
                P.tt("dve", y, y, b0, add)
                o = o_pool.next()
                pg = PS.next()
                P.mm(pg[:, 0:TC], Glu[:, 0, cb * 128:(cb + 1) * 128], gt)
                P.tt("dve", o, pg[:, 0:TC], y, mul)
                P.dma(RWv[:, cb, g0:g0 + TC], o)
                yield

    for d in range(2):
        P.memset("pool", tzp, 0.0)
        P.memset("pool", zap, 0.0)
        for si, T in enumerate(cfg.seqs):
            for cb in range(4):
                P.memset("pool", Hf[cb], 0.0)
                P.memset("pool", Hb[cb], 0.0)
            nt = T // TC
            order = list(range(nt)) if d == 0 else list(range(nt - 1, -1, -1))
            prev = None
            for j in order + [None]:
                gens = []
                R = None
                if j is not None:
                    t0 = j * TC
                    g0 = cfg.off[si] + t0
                    R = {"g0": g0}
                    gens.append(c1(si, g0, t0, d, R))
                if prev is not None:
                    gens.append(tile_units(d, prev["g0"], prev))
                run_rr(gens)
                prev = R
    P.release(m)
def layernorm(P, PS, y, gname, vec, ones32s, epst, sqp, stp, tp, emit):
    pm = PS.next()
    pq = PS.next()
    for b in range(8):
        P.mm(pm, ones32s, y[:, b, :], start=(b == 0), stop=(b == 7))
    for b in range(8):
        sq = sqp.next()
        P.act(sq, y[:, b, :], AF.Square)
        P.mm(pq, ones32s, sq, start=(b == 0), stop=(b == 7))
    mean = stp.next()
    P.copy("act", mean, pm)
    msq = stp.next()
    P.act(msq, pm, AF.Square)
    var = stp.next()
    P.tt("dve", var, pq, msq, ALU.subtract)
    P.act(var, var, AF.Sqrt, bias=epst, scale=1.0)
    P.recip(var, var)
    g0 = VOFF[gname + "g"][0]
    b0 = VOFF[gname + "b"][0]
    for b in range(8):
        t = tp.next()
        P.tt("dve", t, y[:, b, :], mean, ALU.subtract)
        P.tt("dve", t, t, var, ALU.mult)
        P.act(t, t, AF.Identity, bias=vec[:, b0 + b:b0 + b + 1], scale=vec[:, g0 + b:g0 + b + 1])
        emit(b, t)


def phaseD1(P, cfg, PS, l, W, vec, XT, S, consts):
    m = P.mark()
    Wpa = P.sb("Wpa", [128, 4, D], BF16)
    Wpb = P.sb("Wpb", [128, 4, D], BF16)
    Wo = P.sb("Wo", [128, 8, D], BF16)
    ms_ = P.mark()
    stage = P.sbpool("stg", [128, 2048], F32, 2)
    load_w(P, Wpa, W["w_pa"], C, D, stage)
    load_w(P, Wpb, W["w_pb"], C, D, stage)
    load_w(P, Wo, W["w_o"], D, D, stage)
    P.release(ms_)
    at_pool = P.sbpool("dat", [128, 8, TT_], BF16, 2)
    g_pool = P.sbpool("dg", [128, 16, TT_], F32, 1)
    xf_pool = P.sbpool("dxf", [128, 8, TT_], F32, 2)
    mx_pool = P.sbpool("dmx", [128, 8, TT_], BF16, 1)
    y_pool = P.sbpool("dy", [128, 8, TT_], F32, 1)
    t_pool = P.sbpool("dt", [128, TT_], F32, 4)
    sq_pool = P.sbpool("dsq", [128, TT_], F32, 2)
    st_pool = P.sbpool("dst", [128, TT_], F32, 3)
    XTv = XT.rr("(kc p) t -> p kc t", p=128)
    X1v = S["X1"].rr("(kc p) t -> p kc t", p=128)
    Gv = S["G"].rr("(b p) t -> p b t", p=128)
    ATv = S["ATT"].rr("(b p) t -> p b t", p=128)
    RWv = S["RW"].rr("(b p) t -> p b t", p=128)
    for (si, j, g0, t0) in cfg.tiles():
        at = at_pool.next()
        P.dma(at[:, 0:4, :], ATv[:, :, g0:g0 + TT_])
        P.dma(at[:, 4:8, :], RWv[:, :, g0:g0 + TT_])
        g = g_pool.next()
        P.dma(g[:, 0:8, :], Gv[:, 0:8, g0:g0 + TT_])
        P.dma(g[:, 8:16, :], Gv[:, 8:16, g0:g0 + TT_])
        xf = xf_pool.next()
        P.dma(xf, XTv[:, :, g0:g0 + TT_])
        mx = mx_pool.next()
        for mb in range(8):
            pa = PS.next()
            for kc in range(4):
                P.mm(pa, Wpa[:, kc, mb * 128:(mb + 1) * 128], at[:, kc, :], start=(kc == 0), stop=(kc == 3))
            pb = PS.next()
            for kc in range(4):
                P.mm(pb, Wpb[:, kc, mb * 128:(mb + 1) * 128], at[:, 4 + kc, :], start=(kc == 0), stop=(kc == 3))
            t1 = t_pool.next()
            t2 = t_pool.next()
            P.tt("dve", t1, pa, g[:, mb, :], ALU.mult)
            P.tt("dve", t2, pb, g[:, 8 + mb, :], ALU.mult)
            P.tt("dve", mx[:, mb, :], t1, t2, ALU.add)
        y = y_pool.next()
        for mb in range(8):
            po = PS.next()
            for kc in range(8):
                P.mm(po, Wo[:, kc, mb * 128:(mb + 1) * 128], mx[:, kc, :], start=(kc == 0), stop=(kc == 7))
            P.stt(y[:, mb, :], xf[:, mb, :], ALPHA, po, ALU.mult, ALU.add)

        def emit(b, t):
            P.dma(X1v[:, b, g0:g0 + TT_], t)
        layernorm(P, PS, y, "l1", vec, consts["ones32s"], consts["eps_ln"], sq_pool, st_pool, t_pool, emit)
    P.release(m)


def phaseD2a(P, cfg, PS, l, W, vec, S, consts):
    m = P.mark()
    Wup = P.sb("Wup", [128, 8, 2 * DFF], BF16)
    Wdn = P.sb("Wdn", [128, 22, D], BF16)
    ms_ = P.mark()
    stage = P.sbpool("stg", [128, 2048], F32, 2)
    load_w(P, Wup, W["w_ffn_up"], D, 2 * DFF, stage)
    load_w(P, Wdn, W["w_ffn_down"], DFF, D, stage)
    P.release(ms_)
    xf_pool = P.sbpool("exf", [128, 8, TT_], F32, 1)
    xb_pool = P.sbpool("exb", [128, 8, TT_], BF16, 1)
    xh_pool = P.sbpool("exh", [128, 8, 2], F32, 2)
    xhb_pool = P.sbpool("exhb", [128, 8, 2], BF16, 2)
    uh_pool = P.sbpool("euh", [128, 44, 2], F32, 2)
    ue_pool = P.sbpool("eue", [128, TT_ + 2], F32, 3)
    yc_pool = P.sbpool("eyc", [128, TT_], F32, 3)
    s_pool = P.sbpool("es", [128, TT_], F32, 3)
    gb_pool = P.sbpool("egb", [128, 22, TT_], BF16, 1)
    o_pool = P.sbpool("eo", [128, TT_], F32, 2)
    X1v = S["X1"].rr("(kc p) t -> p kc t", p=128)
    Y2v = S["Y2"].rr("(kc p) t -> p kc t", p=128)
    cw = [VOFF["cw0"][0], VOFF["cw1"][0], VOFF["cw2"][0]]
    cb = VOFF["cb"][0]
    for (si, j, g0, t0) in cfg.tiles():
        T = cfg.seqs[si]
        xf = xf_pool.next()
        P.dma(xf, X1v[:, :, g0:g0 + TT_])
        xh = xh_pool.next()
        if t0 == 0:
            P.memset("pool", xh[:, :, 0:1], 0.0)
        else:
            P.dma(xh[:, :, 0:1], X1v[:, :, g0 - 1:g0])
        if t0 + TT_ >= T:
            P.memset("pool", xh[:, :, 1:2], 0.0)
        else:
            P.dma(xh[:, :, 1:2], X1v[:, :, g0 + TT_:g0 + TT_ + 1])
        xb = xb_pool.next()
        P.copy("dve", xb[:, 0:4, :], xf[:, 0:4, :])
        P.copy("act", xb[:, 4:8, :], xf[:, 4:8, :])
        xhb = xhb_pool.next()
        P.copy("pool", xhb, xh)
        psh = PS.next()
        for b in range(44):
            for kc in range(8):
                P.mm(psh[:, 2 * b:2 * b + 2], Wup[:, kc, b * 128:(b + 1) * 128], xhb[:, kc, :], start=(kc == 0), stop=(kc == 7))
        uh = uh_pool.next()
        P.copy("dve", uh.rr("p b c -> p (b c)"), psh[:, 0:88])
        gb = gb_pool.next()

        def conv_block(b):
            ps = PS.next()
            for kc in range(8):
                P.mm(ps, Wup[:, kc, b * 128:(b + 1) * 128], xb[:, kc, :], start=(kc == 0), stop=(kc == 7))
            ue = ue_pool.next()
            P.act(ue[:, 1:TT_ + 1], ps, AF.Copy)
            P.copy("pool", ue[:, 0:1], uh[:, b, 0:1])
            P.copy("pool", ue[:, TT_ + 1:TT_ + 2], uh[:, b, 1:2])
            yc = yc_pool.next()
            P.act(yc, ue[:, 1:TT_ + 1], AF.Identity, bias=vec[:, cb + b:cb + b + 1], scale=vec[:, cw[1] + b:cw[1] + b + 1])
            P.stt(yc, ue[:, 0:TT_], vec[:, cw[0] + b:cw[0] + b + 1], yc, ALU.mult, ALU.add)
            P.stt(yc, ue[:, 2:TT_ + 2], vec[:, cw[2] + b:cw[2] + b + 1], yc, ALU.mult, ALU.add)
            return yc
        for i in range(22):
            ya = conv_block(i)
            yb = conv_block(22 + i)
            s = s_pool.next()
            P.act(s, ya, AF.Square)
            P.ts("dve", s, s, 0.044715, ALU.mult, 1.0, ALU.add)
            P.tt("pool", s, s, ya, ALU.mult)
            P.act(s, s, AF.Sigmoid, scale=1.5957691216057308)
            P.tt("dve", s, s, ya, ALU.mult)
            P.tt("dve", gb[:, i, :], s, yb, ALU.mult)
        for mb in range(8):
            ps = PS.next()
            for kc in range(22):
                P.mm(ps, Wdn[:, kc, mb * 128:(mb + 1) * 128], gb[:, kc, :], start=(kc == 0), stop=(kc == 21))
            o = o_pool.next()
            P.stt(o, xf[:, mb, :], ALPHA, ps, ALU.mult, ALU.add)
            P.dma(Y2v[:, mb, g0:g0 + TT_], o)
    P.release(m)


def phaseD2b(P, cfg, PS, l, W, vec, XTn, y_out, S, PTl, consts, last):
    m = P.mark()
    Wpg = P.sb("Wpg", [128, 8, D], BF16)
    Wpp = P.sb("Wpp", [128, 2, D], BF16)
    ms_ = P.mark()
    stage = P.sbpool("stg", [128, 2048], F32, 2)
    load_w(P, Wpg, W["w_pe_gate"], D, D, stage)
    load_w(P, Wpp, W["w_pe_proj"], PD, D, stage)
    P.release(ms_)
    xf_pool = P.sbpool("fxf", [128, 8, TT_], F32, 1)
    xb_pool = P.sbpool("fxb", [128, 8, TT_], BF16, 2)
    y_pool = P.sbpool("fy", [128, 8, TT_], F32, 2)
    pt_pool = P.sbpool("fpt", [128, 2, TT_], BF16, 2)
    t_pool = P.sbpool("ft", [128, TT_], F32, 4)
    sq_pool = P.sbpool("fsq", [128, TT_], F32, 2)
    st_pool = P.sbpool("fst", [128, TT_], F32, 3)
    x2_pool = P.sbpool("fx2", [128, 8, TT_], F32, 1)
    yo_pool = P.sbpool("fyo", [128, D], F32, 2)
    X1v = S["X1"].rr("(kc p) t -> p kc t", p=128)
    Y2v = S["Y2"].rr("(kc p) t -> p kc t", p=128)
    PTv = PTl.rr("(kc p) t -> p kc t", p=128)
    if not last:
        XTv = XTn.rr("(kc p) t -> p kc t", p=128)
    k = 0
    for (si, j, g0, t0) in cfg.tiles():
        xf = xf_pool.next()
        P.dma(xf, X1v[:, :, g0:g0 + TT_])
        y = y_pool.next()
        P.dma(y, Y2v[:, :, g0:g0 + TT_])
        pt = pt_pool.next()
        P.dma(pt, PTv[:, :, g0:g0 + TT_])
        xb = xb_pool.next()
        P.copy("dve", xb[:, 0:4, :], xf[:, 0:4, :])
        P.copy("act", xb[:, 4:8, :], xf[:, 4:8, :])
        for mb in range(8):
            pg = PS.next()
            for kc in range(8):
                P.mm(pg, Wpg[:, kc, mb * 128:(mb + 1) * 128], xb[:, kc, :], start=(kc == 0), stop=(kc == 7))
            pp = PS.next()
            for kc in range(2):
                P.mm(pp, Wpp[:, kc, mb * 128:(mb + 1) * 128], pt[:, kc, :], start=(kc == 0), stop=(kc == 1))
            sg = t_pool.next()
            P.act(sg, pg, AF.Sigmoid)
            P.tt("dve", sg, pp, sg, ALU.mult)
            P.tt("dve", y[:, mb, :], y[:, mb, :], sg, ALU.add)
        if not last:
            def emit(b, t):
                P.dma(XTv[:, b, g0:g0 + TT_], t)
            layernorm(P, PS, y, "l2", vec, consts["ones32s"], consts["eps_ln"], sq_pool, st_pool, t_pool, emit)
        else:
            x2 = x2_pool.next()

            def emit(b, t):
                P.copy("act", x2[:, b, :], t)
            layernorm(P, PS, y, "l2", vec, consts["ones32s"], consts["eps_ln"], sq_pool, st_pool, t_pool, emit)
            for tb in range(4):
                yo = yo_pool.next()
                for half in range(2):
                    ps = PS.next()
                    for q in range(4):
                        kc = half * 4 + q
                        P.tr(ps[:, q * 128:(q + 1) * 128], x2[:, kc, tb * 128:(tb + 1) * 128], consts["ident"])
                    P.copy(("act", "dve")[k % 2], yo[:, half * 512:(half + 1) * 512], ps)
                    k += 1
                P.dma(y_out[g0 + tb * 128:g0 + (tb + 1) * 128, :], yo)
    P.release(m)
WKEYS0 = ["w_in", "w_krr", "wq_n", "wq_r", "wq_rr", "wk", "wv", "w_lu", "a_lu", "g_lu", "w_pa", "w_pb", "w_o",
          "w_ffn_up", "w_ffn_down", "w_pe_gate", "w_pe_proj"]
WSHAPES = {"w_in": (D, INC), "w_krr": (D, 32), "wq_n": (QL, 512), "wq_r": (QL, 256), "wq_rr": (QL, 256),
           "wk": (KVL, 512), "wv": (KVL, 512), "w_lu": (128, C), "a_lu": (128, C), "g_lu": (128, C),
           "v_ld": (D, 32), "v_lu": (32, C), "w_pa": (C, D), "w_pb": (C, D), "w_o": (D, D),
           "w_ffn_up": (D, 2 * DFF), "w_ffn_down": (DFF, D), "w_pe_gate": (D, D), "w_pe_proj": (PD, D)}


def wkeys(l):
    return WKEYS0 + (["v_ld", "v_lu"] if l > 0 else [])


def build(cfg):
    nc = bass.Bass("TRN2", target_bir_lowering=False)
    P = Prog(nc)
    NT = cfg.NT
    dbg = cfg.debug

    def din(name, shape, dt=F32):
        return TT(nc.dram_tensor(name, list(shape), dt, kind="ExternalInput").ap(), Buf(name))
    x_in = din("x", [NT, D])
    p_in = din("p", [DEPTH, NT, PD])
    cd = {"cs": din("cs", [128, cfg.Tmax]), "sn": din("sn", [128, cfg.Tmax]), "ident_d": din("ident", [128, 128]),
          "masks_d": din("masks", [128, 4, 2, 128]), "bd_d": din("bd", [128, 128]), "rmask_d": din("rmask", [128, TT_])}
    Wd = []
    vecd = []
    for l in range(DEPTH):
        vecd.append(din("vec_%d" % l, [128, NVEC]))
        Wd.append({k: din("%s_%d" % (k, l), WSHAPES[k]) for k in wkeys(l)})
    y_out = TT(nc.dram_tensor("y", [NT, D], F32, kind="ExternalOutput").ap(), Buf("y"))
    kind = "ExternalOutput" if dbg else "Internal"
    S = {}
    for name, shape, dt in [("XT0", [D, NT], F32), ("XT1", [D, NT], F32), ("PT0", [PD, NT], BF16), ("PT1", [PD, NT], BF16),
                            ("QT", [H, QK, NT], BF16), ("KT", [H, QK, NT], BF16), ("V", [NT, 512], BF16),
                            ("Z", [1920, NT], F32), ("G", [2048, NT], F32), ("ATT", [512, NT], BF16),
                            ("RW", [512, NT], BF16), ("VF", [512, NT], F32), ("YF", [512, NT], F32),
                            ("BON", [512, NT], F32), ("X1", [D, NT], F32), ("Y2", [D, NT], F32)]:
        S[name] = P.dram(name, shape, dt, kind=kind)
    PSall = Pool([P.psum("ps%d" % i, [128, 512], F32) for i in range(8)])
    ident = P.sb("ident_sb", [128, 128], F32)
    P.dma(ident, cd["ident_d"])
    cd["ident"] = ident
    ones32s = P.sb("ones32s", [128, 128], F32)
    P.memset("pool", ones32s, 1.0 / D)
    cd["ones32s"] = ones32s
    for nm, val in [("eps_rms", RMS_EPS), ("eps_ln", LN_EPS), ("eps_12", 1e-12), ("eps_gn", GN_EPS)]:
        t = P.sb(nm, [128, 1], F32)
        P.memset("pool", t, val)
        cd[nm] = t
    vecs = []
    for l in range(DEPTH):
        v = P.sb("vec%d" % l, [128, NVEC], F32)
        P.dma(v, vecd[l])
        vecs.append(v)
    ph = cfg.phases
    XTs = [S["XT0"], S["XT1"]]
    PTs = [S["PT0"], S["PT1"]]
    if "0" in ph:
        phase0(P, cfg, PSall, x_in, p_in, XTs[0], PTs, ident)
    for l in range(cfg.nlayers if hasattr(cfg, "nlayers") else DEPTH):
        XT = XTs[l % 2]
        XTn = XTs[(l + 1) % 2]
        last = (l == DEPTH - 1)
        if "A" in ph:
            phaseA(P, cfg, PSall, l, Wd[l], vecs[l], XT, S, cd)
        if "B" in ph:
            phaseB(P, cfg, PSall, l, S, cd)
        if "C" in ph:
            phaseC(P, cfg, PSall, l, Wd[l], vecs[l], XT, S, cd)
        if "D" in ph:
            phaseD1(P, cfg, PSall, l, Wd[l], vecs[l], XT, S, cd)
        if "E" in ph:
            phaseD2a(P, cfg, PSall, l, Wd[l], vecs[l], S, cd)
            phaseD2b(P, cfg, PSall, l, Wd[l], vecs[l], XTn, y_out, S, PTs[l], cd, last)
    P.finish([y_out])
    return nc, P


def host_inputs(cfg, w, x_cores, p_cores):
    c = host_consts(cfg.Tmax)
    base = {"cs": c["cs"], "sn": c["sn"], "ident": c["ident"],
            "masks": np.ascontiguousarray(np.repeat(c["masks"].reshape(128, 4, 1, 128), 2, axis=2)),
            "bd": c["bd"], "rmask": c["rmask"]}
    for l in range(DEPTH):
        hp = host_layer_params(w, l)
        base["vec_%d" % l] = hp["vec"]
        for k in wkeys(l):
            base["%s_%d" % (k, l)] = np.ascontiguousarray(hp[k], dtype=np.float32)
    maps = []
    for xc, pc in zip(x_cores, p_cores):
        m = dict(base)
        m["x"] = np.ascontiguousarray(xc, dtype=np.float32)
        m["p"] = np.ascontiguousarray(pc, dtype=np.float32)
        maps.append(m)
    return maps


def kernel(**inputs):
    w = {k: np.asarray(v) for k, v in inputs.items()}
    xp, xs, pp, ps_ = w["x_prompt"], w["x_sample"], w["p_prompt"], w["p_sample"]
    n = 8
    Bp, Tp = xp.shape[0], xp.shape[1]
    Bs, Ts = xs.shape[0], xs.shape[1]
    npc = Bp // n
    nsc = Bs // n
    cfg = Cfg([Tp] * npc + [Ts] * nsc)
    x_cores, p_cores = [], []
    for c in range(n):
        xl = [xp[c * npc + i] for i in range(npc)] + [xs[c * nsc + i] for i in range(nsc)]
        pl = [pp[:, c * npc + i] for i in range(npc)] + [ps_[:, c * nsc + i] for i in range(nsc)]
        x_cores.append(np.concatenate(xl, 0))
        p_cores.append(np.concatenate(pl, 1))
    nc, P = build(cfg)
    maps = host_inputs(cfg, w, x_cores, p_cores)
    res = run_bass_kernel_spmd(nc, maps, core_ids=list(range(n)))
    yp = np.zeros(xp.shape, np.float32)
    ys = np.zeros(xs.shape, np.float32)
    for c in range(n):
        y = np.asarray(res.results[c]["y"], np.float32)
        o = 0
        for i in range(npc):
            yp[c * npc + i] = y[o:o + Tp]
            o += Tp
        for i in range(nsc):
            ys[c * nsc + i] = y[o:o + Ts]
            o += Ts
    return (yp, ys)
```

```python
import numpy as np
import concourse.bass as bass
import concourse.mybir as mybir
from concourse.bass_utils import run_bass_kernel_spmd

F32 = mybir.dt.float32
BF16 = mybir.dt.bfloat16
AF = mybir.ActivationFunctionType
ALU = mybir.AluOpType


class Buf:
    __slots__ = ("name", "w", "r", "psum")

    def __init__(self, name):
        self.name = name
        self.psum = False
        self.w = {}
        self.r = {}


class TT:
    __slots__ = ("ap", "buf")

    def __init__(self, ap, buf):
        self.ap = ap
        self.buf = buf

    def __getitem__(self, idx):
        return TT(self.ap[idx], self.buf)

    def rr(self, pat, **kw):
        return TT(self.ap.rearrange(pat, **kw), self.buf)


class Pool:
    def __init__(self, tiles):
        self.tiles = tiles
        self.i = 0

    def next(self):
        t = self.tiles[self.i % len(self.tiles)]
        self.i += 1
        return t


class Prog:
    def __init__(self, nc, n_dma_sems=40):
        self.nc = nc
        self.E = {"pe": nc.tensor, "dve": nc.vector, "act": nc.scalar, "pool": nc.gpsimd, "sp": nc.sync}
        self.sems = []
        self.semval = []
        self.esem = {}
        for e in self.E:
            self.esem[e] = self._newsem("s_" + e)
        self.dsems = [self._newsem("d%d" % i) for i in range(n_dma_sems)]
        self.di = 0
        self.known = {e: {} for e in self.E}
        self.ninst = {e: 0 for e in self.E}
        self._stack = []

    def _newsem(self, name):
        h = self.nc.alloc_semaphore(name)
        self.sems.append(h)
        self.semval.append(0)
        return len(self.sems) - 1

    def sb(self, name, shape, dtype):
        self._uid = getattr(self, "_uid", 0) + 1
        name = "%s_u%d" % (name, self._uid)
        cm = self.nc.sbuf_tensor(name, list(shape), dtype)
        t = cm.__enter__()
        self._stack.append(cm)
        return TT(t[tuple(slice(None) for _ in shape)], Buf(name))

    def sbpool(self, name, shape, dtype, n):
        return Pool([self.sb("%s%d" % (name, i), shape, dtype) for i in range(n)])

    def psum(self, name, shape, dtype):
        cm = self.nc.psum_tensor(name, list(shape), dtype)
        t = cm.__enter__()
        self._stack.append(cm)
        b = Buf(name)
        b.psum = True
        return TT(t[tuple(slice(None) for _ in shape)], b)

    def mark(self):
        return len(self._stack)

    def barrier(self):
        for e in self.E:
            for s in range(len(self.sems)):
                if self.semval[s] > 0:
                    self._wait(e, s, self.semval[s])

    def release(self, mark):
        self.barrier()
        while len(self._stack) > mark:
            cm = self._stack.pop()
            cm.__exit__(None, None, None)

    def dram(self, name, shape, dtype, kind="Internal"):
        t = self.nc.dram_tensor(name, list(shape), dtype, kind=kind)
        return TT(t.ap(), Buf(name))

    def _wait(self, eng, sem, val):
        k = self.known[eng]
        if k.get(sem, 0) >= val:
            return
        self.E[eng].wait_ge(self.sems[sem], val)
        k[sem] = val

    def _pre(self, eng, reads, writes, acc=False):
        for t in reads:
            for s, v in t.buf.w.items():
                self._wait(eng, s, v)
            if t.buf.psum:
                for s, v in t.buf.r.items():
                    if s != self.esem[eng]:
                        self._wait(eng, s, v)
        for t in writes:
            b = t.buf
            for s, v in b.w.items():
                if eng == "pe" and s == self.esem["pe"]:
                    continue
                self._wait(eng, s, v)
            for s, v in b.r.items():
                if eng == "pe" and s == self.esem["pe"]:
                    continue
                self._wait(eng, s, v)

    def _post(self, eng, ins, reads, writes):
        s = self.esem[eng]
        self.semval[s] += 1
        v = self.semval[s]
        ins.then_inc(self.sems[s], 1)
        self.ninst[eng] += 1
        for t in reads:
            t.buf.r[s] = v
        for t in writes:
            t.buf.w[s] = v
            t.buf.r = {}

    @staticmethod
    def _ap(x):
        return x.ap if isinstance(x, TT) else x

    def _tts(self, *xs):
        return [x for x in xs if isinstance(x, TT)]

    def mm(self, out, lhsT, rhs, start=True, stop=True):
        self._pre("pe", [lhsT, rhs], [out])
        ins = self.nc.tensor.matmul(out.ap, lhsT.ap, rhs.ap, start=start, stop=stop)
        self._post("pe", ins, [lhsT, rhs], [out])

    def tr(self, out, in_, ident):
        self._pre("pe", [in_, ident], [out])
        ins = self.nc.tensor.transpose(out.ap, in_.ap, ident.ap)
        self._post("pe", ins, [in_, ident], [out])

    def act(self, out, in_, func=None, bias=None, scale=None):
        func = func if func is not None else AF.Copy
        rd = self._tts(in_, bias, scale)
        self._pre("act", rd, [out])
        kw = {}
        if bias is not None:
            kw["bias"] = self._ap(bias)
        if scale is not None:
            kw["scale"] = self._ap(scale)
        ins = self.nc.scalar.activation(out.ap, in_.ap, func, **kw)
        self._post("act", ins, rd, [out])

    def tt(self, eng, out, a, b, op):
        self._pre(eng, [a, b], [out])
        ins = self.E[eng].tensor_tensor(out.ap, a.ap, b.ap, op)
        self._post(eng, ins, [a, b], [out])

    def ts(self, eng, out, a, s1, op0, s2=None, op1=None):
        rd = self._tts(a, s1, s2)
        self._pre(eng, rd, [out])
        if op1 is None:
            ins = self.E[eng].tensor_scalar(out.ap, a.ap, self._ap(s1), None, op0)
        else:
            ins = self.E[eng].tensor_scalar(out.ap, a.ap, self._ap(s1), self._ap(s2), op0, op1)
        self._post(eng, ins, rd, [out])

    def stt(self, out, a, s, b, op0, op1):
        rd = self._tts(a, s, b)
        self._pre("dve", rd, [out])
        ins = self.nc.vector.scalar_tensor_tensor(out.ap, a.ap, self._ap(s), b.ap, op0, op1)
        self._post("dve", ins, rd, [out])

    def copy(self, eng, out, in_):
        if eng == "act":
            return self.act(out, in_, AF.Copy)
        self._pre(eng, [in_], [out])
        ins = self.E[eng].tensor_copy(out.ap, in_.ap)
        self._post(eng, ins, [in_], [out])

    def memset(self, eng, out, val):
        self._pre(eng, [], [out])
        ins = self.E[eng].memset(out.ap, val)
        self._post(eng, ins, [], [out])

    def recip(self, out, in_):
        self._pre("dve", [in_], [out])
        ins = self.nc.vector.reciprocal(out.ap, in_.ap)
        self._post("dve", ins, [in_], [out])

    def scan(self, out, d0, d1, init, op0, op1):
        rd = self._tts(d0, d1, init)
        self._pre("dve", rd, [out])
        ins = self.nc.vector.tensor_tensor_scan(out.ap, d0.ap, d1.ap, self._ap(init), op0, op1)
        self._post("dve", ins, rd, [out])

    def dma(self, out, in_, q="sp"):
        self._pre(q, [in_], [out])
        s = self.dsems[self.di % len(self.dsems)]
        self.di += 1
        self._wait(q, s, self.semval[s])
        self.semval[s] += 16
        v = self.semval[s]
        self.E[q].dma_start(out=out.ap, in_=in_.ap, allow_slow_non_contiguous=True).then_inc(self.sems[s], 16)
        in_.buf.r[s] = v
        out.buf.w[s] = v
        out.buf.r = {}

    def finish(self, outs):
        for s in self.dsems:
            if self.semval[s] > 0:
                self._wait("sp", s, self.semval[s])
        self.release(0)
D = 1024
DEPTH = 2
H = 8
NOPE, ROPE, VD = 64, 32, 64
QK = 96
QL, KVL = 768, 256
C = 512
DFF = 2816
PD = 256
ALPHA = (2 * DEPTH) ** 0.25
LN_EPS = 1e-5
RMS_EPS = 1e-6
GN_EPS = 64e-5
DECAY_SCALE = 0.606531
OFF_CKV, OFF_KR, OFF_RW = 768, 1024, 1056
OFF_GA = OFF_RW + 1920
OFF_GB = OFF_GA + 1024
INC = OFF_GB + 1024
TT_ = 512
CH = 128

VOFF = {}
_o = 0
for _n, _w in [("qg", 6), ("kvg", 2), ("mu", 15), ("w0", 8), ("a0", 8), ("kk", 4), ("ka", 4), ("rk", 4),
               ("v0", 4), ("lng", 4), ("lnb", 4), ("l1g", 8), ("l1b", 8), ("cw0", 44), ("cw1", 44),
               ("cw2", 44), ("cb", 44), ("l2g", 8), ("l2b", 8)]:
    VOFF[_n] = (_o, _w)
    _o += _w
NVEC = _o


def host_consts(Tmax):
    c = {}
    c["ident"] = np.eye(128, dtype=np.float32)
    pos = np.arange(Tmax, dtype=np.float32)
    inv = (np.float32(10000.0) ** (-np.arange(0, ROPE, 2, dtype=np.float32) / np.float32(ROPE))).astype(np.float32)
    ang = (pos[:, None] * inv[None, :]).astype(np.float32)
    cos = np.cos(ang).astype(np.float32).T
    sin = np.sin(ang).astype(np.float32).T
    c["cs"] = np.ascontiguousarray(np.tile(np.concatenate([cos, cos], 0), (4, 1)))
    c["sn"] = np.ascontiguousarray(np.tile(np.concatenate([sin, sin], 0), (4, 1)))
    s = np.arange(128)[:, None]
    t = np.arange(128)[None, :]
    m = np.zeros((128, 4, 128), np.float32)
    m[:, 0] = (s < t)
    m[:, 1] = (s <= t)
    m[:, 2] = (s > t)
    m[:, 3] = (s >= t)
    c["masks"] = m.reshape(128, 512)
    bd = np.zeros((128, 128), np.float32)
    bd[:64, :64] = 1
    bd[64:, 64:] = 1
    c["bd"] = bd
    rm = np.ones((128, TT_), np.float32)
    rm[:, ::CH] = 0
    c["rmask"] = rm
    return c


def blk(v, n):
    return np.ascontiguousarray(np.asarray(v, np.float32).reshape(n, 128).T)


def host_layer_params(w, l):
    o = {}
    vec = np.zeros((128, NVEC), np.float32)

    def put(name, arr):
        a, n = VOFF[name]
        assert arr.shape == (128, n), (name, arr.shape)
        vec[:, a:a + n] = arr
    put("qg", blk(w["q_norm_g"][l], 6))
    put("kvg", blk(w["kv_norm_g"][l], 2))
    put("mu", blk(w["tshift_mu"][l], 15))
    put("w0", blk(w["w0"][l].reshape(-1), 8))
    put("a0", blk(w["a0"][l].reshape(-1), 8))
    put("kk", blk(w["k_k"][l], 4))
    put("ka", blk(w["k_a"][l], 4))
    put("rk", blk(w["r_k"][l], 4))
    if l > 0:
        put("v0", blk(w["v0"][l - 1], 4))
    put("lng", blk(w["lnx_g"][l], 4))
    put("lnb", blk(w["lnx_b"][l], 4))
    put("l1g", blk(w["ln1_g"][l], 8))
    put("l1b", blk(w["ln1_b"][l], 8))
    for i in range(3):
        put("cw%d" % i, blk(w["conv_w"][l, i], 44))
    put("cb", blk(w["conv_b"][l], 44))
    put("l2g", blk(w["ln2_g"][l], 8))
    put("l2b", blk(w["ln2_b"][l], 8))
    o["vec"] = vec
    win = np.asarray(w["w_in"][l], np.float32)
    o["w_in"] = win
    o["w_krr"] = np.ascontiguousarray(np.concatenate([win[:, OFF_KR + 16:OFF_KR + 32], win[:, OFF_KR:OFF_KR + 16]], 1))
    wq = np.asarray(w["w_uq"][l], np.float32).reshape(QL, H, QK)
    o["wq_n"] = np.ascontiguousarray(wq[:, :, :NOPE].reshape(QL, H * NOPE))
    o["wq_r"] = np.ascontiguousarray(wq[:, :, NOPE:].reshape(QL, H * ROPE))
    o["wq_rr"] = np.ascontiguousarray(np.concatenate([wq[:, :, NOPE + 16:], wq[:, :, NOPE:NOPE + 16]], 2).reshape(QL, H * ROPE))
    wkv = np.asarray(w["w_ukv"][l], np.float32).reshape(KVL, H, NOPE + VD)
    o["wk"] = np.ascontiguousarray(wkv[:, :, :NOPE].reshape(KVL, H * NOPE))
    o["wv"] = np.ascontiguousarray(wkv[:, :, NOPE:].reshape(KVL, H * VD))
    o["w_lu"] = np.ascontiguousarray(np.asarray(w["w_lora_up"][l], np.float32).reshape(128, C))
    o["a_lu"] = np.ascontiguousarray(np.asarray(w["a_lora_up"][l], np.float32).reshape(128, C))
    o["g_lu"] = np.asarray(w["g_lora_up"][l], np.float32)
    if l > 0:
        o["v_ld"] = np.asarray(w["v_lora_down"][l - 1], np.float32)
        o["v_lu"] = np.asarray(w["v_lora_up"][l - 1], np.float32)
    for k in ["w_pa", "w_pb", "w_o", "w_ffn_up", "w_ffn_down", "w_pe_gate", "w_pe_proj"]:
        o[k] = np.asarray(w[k][l], np.float32)
    return o


class Cfg:
    def __init__(self, seqs, debug=False, phases="0ABCDE"):
        self.seqs = list(seqs)
        self.NT = sum(seqs)
        self.off = [sum(seqs[:i]) for i in range(len(seqs))]
        self.Tmax = max(seqs)
        self.debug = debug
        self.phases = phases

    def tiles(self):
        for si, T in enumerate(self.seqs):
            for j in range(T // TT_):
                yield si, j, self.off[si] + j * TT_, j * TT_
def load_w(P, dst, src, K, N, stage, engs=("dve", "pool"), scale_vec=None, neg=None):
    KC = (K + 127) // 128
    i = 0
    for kc in range(KC):
        rows = min(128, K - kc * 128)
        for c0 in range(0, N, 2048):
            cw = min(2048, N - c0)
            st = stage.next()
            P.dma(st[0:rows, 0:cw], src[kc * 128:kc * 128 + rows, c0:c0 + cw])
            eng = engs[i % len(engs)]
            i += 1
            if scale_vec is None:
                P.copy(eng, dst[0:rows, kc, c0:c0 + cw], st[0:rows, 0:cw])
            else:
                P.ts(eng, dst[0:rows, kc, c0:c0 + cw], st[0:rows, 0:cw], scale_vec[0:rows, kc:kc + 1], ALU.mult)
    if neg is not None:
        for (a, b) in neg:
            P.ts("pool", dst[:, :, a:b], dst[:, :, a:b], -1.0, ALU.mult)


def phase0(P, cfg, PS, x_in, p_in, XT, PTs, ident):
    m = P.mark()
    xin_pool = P.sbpool("p0x", [128, 4, 1024], F32, 2)
    xf_pool = P.sbpool("p0f", [128, 8, 512], F32, 2)
    pin_pool = P.sbpool("p0p", [128, 4, 256], F32, 2)
    pb_pool = P.sbpool("p0b", [128, 2, 512], BF16, 2)
    XTv = XT.rr("(kc p) t -> p kc t", p=128)
    k = 0
    for (si, j, g0, t0) in cfg.tiles():
        xin = xin_pool.next()
        P.dma(xin, x_in[g0:g0 + TT_, :].rr("(tb p) f -> p tb f", p=128))
        xf = xf_pool.next()
        for kc in range(8):
            ps = PS.next()
            for tb in range(4):
                P.tr(ps[:, tb * 128:(tb + 1) * 128], xin[:, tb, kc * 128:(kc + 1) * 128], ident)
            P.copy(("act", "dve")[k % 2], xf[:, kc, :], ps)
            k += 1
        P.dma(XTv[:, :, g0:g0 + TT_], xf)
        for l in range(DEPTH):
            pin = pin_pool.next()
            P.dma(pin, p_in[l, g0:g0 + TT_, :].rr("(tb p) f -> p tb f", p=128))
            pb = pb_pool.next()
            for kc in range(2):
                ps = PS.next()
                for tb in range(4):
                    P.tr(ps[:, tb * 128:(tb + 1) * 128], pin[:, tb, kc * 128:(kc + 1) * 128], ident)
                P.copy(("act", "dve")[k % 2], pb[:, kc, :], ps)
                k += 1
            P.dma(PTs[l].rr("(kc p) t -> p kc t", p=128)[:, :, g0:g0 + TT_], pb)
    P.release(m)


def phaseA(P, cfg, PS, l, W, vec, XT, S, consts):
    m = P.mark()
    Win = P.sb("Win", [128, 8, INC], BF16)
    Wkrr = P.sb("Wkrr", [128, 8, 128], BF16)
    P.memset("pool", Wkrr, 0.0)
    Wqn = P.sb("Wqn", [128, 6, 512], BF16)
    Wqr = P.sb("Wqr", [128, 6, 256], BF16)
    Wqrr = P.sb("Wqrr", [128, 6, 256], BF16)
    Wk = P.sb("Wk", [128, 2, 512], BF16)
    Wv = P.sb("Wv", [128, 2, 512], BF16)
    ones = P.sb("onesb", [128, 128], BF16)
    P.memset("pool", ones, 1.0)
    ms_ = P.mark()
    stage = P.sbpool("stg", [128, 2048], F32, 2)
    qg = vec[:, VOFF["qg"][0]:VOFF["qg"][0] + 6]
    kvg = vec[:, VOFF["kvg"][0]:VOFF["kvg"][0] + 2]
    load_w(P, Win, W["w_in"], D, INC, stage)
    load_w(P, Wkrr[:, :, 0:32], W["w_krr"], D, 32, stage)
    P.ts("pool", Wkrr[:, :, 0:16], Wkrr[:, :, 0:16], -1.0, ALU.mult)
    load_w(P, Wqn, W["wq_n"], QL, 512, stage, scale_vec=qg)
    load_w(P, Wqr, W["wq_r"], QL, 256, stage, scale_vec=qg)
    load_w(P, Wqrr, W["wq_rr"], QL, 256, stage, scale_vec=qg)
    P.ts("pool", Wqrr.rr("p k (h r) -> p k h r", r=32)[:, :, :, 0:16], Wqrr.rr("p k (h r) -> p k h r", r=32)[:, :, :, 0:16], -1.0, ALU.mult)
    load_w(P, Wk, W["wk"], KVL, 512, stage, scale_vec=kvg)
    load_w(P, Wv, W["wv"], KVL, 512, stage, scale_vec=kvg)
    P.release(ms_)

    xf_pool = P.sbpool("axf", [128, 8, TT_], F32, 1)
    xb_pool = P.sbpool("axb", [128, 8, TT_], BF16, 2)
    cs_pool = P.sbpool("acs", [128, 2, TT_], F32, 2)
    cq_pool = P.sbpool("acq", [128, 8, TT_], F32, 1)
    sq_pool = P.sbpool("asq", [128, TT_], BF16, 3)
    cn_pool = P.sbpool("acn", [128, 8, TT_], BF16, 1)
    rs_pool = P.sbpool("ars", [128, 2, TT_], F32, 1)
    zo_pool = P.sbpool("azo", [128, TT_], F32, 4)
    qo_pool = P.sbpool("aqo", [128, TT_], BF16, 6)
    t1_pool = P.sbpool("at1", [128, TT_], F32, 3)
    vo_pool = P.sbpool("avo", [128, 4, 512], BF16, 1)
    XTv = XT.rr("(kc p) t -> p kc t", p=128)
    Zv = S["Z"].rr("(b p) t -> p b t", p=128)
    Gv = S["G"].rr("(b p) t -> p b t", p=128)
    tiles = list(cfg.tiles())
    PSacc = Pool(PS.tiles[0:2])
    PS = Pool(PS.tiles[2:8])
    import os
    AT = float(os.environ.get("AT", "99"))

    def load(i):
        si, j, g0, t0 = tiles[i]
        xf = xf_pool.next()
        P.dma(xf, XTv[:, :, g0:g0 + TT_])
        cs = cs_pool.next()
        P.dma(cs[:, 0, :], consts["cs"][:, t0:t0 + TT_])
        P.dma(cs[:, 1, :], consts["sn"][:, t0:t0 + TT_])
        return xf, cs
    if AT <= 0:
        P.release(m)
        return
    nxt = load(0)
    ev = 0
    for i, (si, j, g0, t0) in enumerate(tiles):
        xf, cs = nxt
        xb = xb_pool.next()
        P.copy("act", xb[:, 0:4, :], xf[:, 0:4, :])
        P.copy("dve", xb[:, 4:8, :], xf[:, 4:8, :])
        if i + 1 < len(tiles):
            nxt = load(i + 1)
        if cfg.debug and i == 0 and l == 0:
            dbg1 = P.dram("dbg_xb", [128, 8, TT_], BF16, kind="ExternalOutput")
            P.dma(dbg1, xb)
            for ii, cc in enumerate([0, 1024, 2048, 4096]):
                dbg2 = P.dram("dbg_win%d" % ii, [128, 8, 512], BF16, kind="ExternalOutput")
                P.dma(dbg2, Win[:, :, cc:cc + 512])
            dbg3 = P.dram("dbg_xf", [128, 8, TT_], F32, kind="ExternalOutput")
            P.dma(dbg3, xf)

        def proj(c0, mw, Wt=Win, KC=8, rhs=xb):
            ps = PS.next()
            for kc in range(KC):
                P.mm(ps[0:mw, :], Wt[:, kc, c0:c0 + mw], rhs[:, kc, :], start=(kc == 0), stop=(kc == KC - 1))
            return ps
        if AT <= 0.5:
            continue
        cq = cq_pool.next()
        cn = cn_pool.next()
        rs = rs_pool.next()
        ssq = [PSacc.next(), PSacc.next()]
        sqs = []
        for b in range(8):
            ps = proj(b * 128, 128)
            P.copy("dve", cq[:, b, :], ps)
            sq = sq_pool.next()
            P.act(sq, ps, AF.Square)
            sqs.append(sq)
            which = 0 if b < 6 else 1
            first = b in (0, 6)
            last = b in (5, 7)
            P.mm(ssq[which], ones, sq, start=first, stop=last)
        if cfg.debug and i == 0 and l == 0:
            dbg4 = P.dram("dbg_cq", [128, 8, TT_], F32, kind="ExternalOutput")
            P.dma(dbg4, cq)
        if AT <= 0.6:
            continue
        for which, (n, b0, b1) in enumerate([(QL, 0, 6), (KVL, 6, 8)]):
            P.act(rs[:, which, :], ssq[which], AF.Sqrt, bias=consts["eps_rms"], scale=1.0 / n)
            if AT <= 0.7:
                continue
            P.recip(rs[:, which, :], rs[:, which, :])
            if AT <= 0.8:
                continue
            for b in range(b0, b1):
                P.tt("dve", cn[:, b, :], cq[:, b, :], rs[:, which, :], ALU.mult)
        if AT <= 0.4:
            continue
        for b in range(15):
            ps = proj(OFF_RW + b * 128, 128)
            zo = zo_pool.next()
            P.copy(("act", "dve")[ev % 2], zo, ps)
            ev += 1
            P.dma(Zv[:, b, g0:g0 + TT_], zo)
            if cfg.debug and i == 0 and l == 0 and b == 0:
                dz = P.sb("dbgz", [128, TT_], F32)
                P.copy("dve", dz, ps)
                P.dma(P.dram("dbg_z0", [128, TT_], F32, kind="ExternalOutput"), dz)
                P.dma(P.dram("dbg_z1", [128, TT_], F32, kind="ExternalOutput"), zo)
        if AT <= 0.45:
            continue
        for b in range(16):
            ps = proj(OFF_GA + b * 128, 128)
            zo = zo_pool.next()
            P.act(zo, ps, AF.Sigmoid)
            P.dma(Gv[:, b, g0:g0 + TT_], zo)
        if AT <= 1:
            continue
        ps = proj(OFF_KR, 128)
        ps2 = proj(0, 128, Wt=Wkrr)
        t1 = t1_pool.next()
        t2 = t1_pool.next()
        P.tt("dve", t1[0:32, :], ps[0:32, :], cs[0:32, 0, :], ALU.mult)
        P.tt("dve", t2[0:32, :], ps2[0:32, :], cs[0:32, 1, :], ALU.mult)
        kr = qo_pool.next()
        P.tt("dve", kr[0:32, :], t1[0:32, :], t2[0:32, :], ALU.add)
        for h in range(H):
            P.dma(S["KT"][h, 64:96, g0:g0 + TT_], kr[0:32, :])
        if AT <= 4:
            continue
        scale = QK ** -0.5
        for b in range(4):
            ps = proj(b * 128, 128, Wt=Wqn, KC=6, rhs=cn)
            qo = qo_pool.next()
            P.act(qo, ps, AF.Copy, scale=scale)
            for hh in range(2):
                P.dma(S["QT"][2 * b + hh, 0:64, g0:g0 + TT_], qo[hh * 64:(hh + 1) * 64, :])
        for b in range(2):
            ps = proj(b * 128, 128, Wt=Wqr, KC=6, rhs=cn)
            ps2 = proj(b * 128, 128, Wt=Wqrr, KC=6, rhs=cn)
            t1 = t1_pool.next()
            t2 = t1_pool.next()
            P.tt("dve", t1, ps, cs[:, 0, :], ALU.mult)
            P.tt("dve", t2, ps2, cs[:, 1, :], ALU.mult)
            qo = qo_pool.next()
            P.tt("dve", t1, t1, t2, ALU.add)
            P.act(qo, t1, AF.Copy, scale=scale)
            for hh in range(4):
                P.dma(S["QT"][4 * b + hh, 64:96, g0:g0 + TT_], qo[hh * 32:(hh + 1) * 32, :])
        if AT <= 5:
            continue
        for b in range(4):
            ps = proj(b * 128, 128, Wt=Wk, KC=2, rhs=cn[:, 6:8, :])
            qo = qo_pool.next()
            P.copy(("act", "dve")[b % 2], qo, ps)
            for hh in range(2):
                P.dma(S["KT"][2 * b + hh, 0:64, g0:g0 + TT_], qo[hh * 64:(hh + 1) * 64, :])
        if AT <= 6:
            continue
        vo = vo_pool.next()
        for tb in range(4):
            ps = PS.next()
            for kc in range(2):
                P.mm(ps, cn[:, 6 + kc, tb * 128:(tb + 1) * 128], Wv[:, kc, :], start=(kc == 0), stop=(kc == 1))
            P.copy(("act", "dve")[tb % 2], vo[:, tb, :], ps)
        P.dma(S["V"][g0:g0 + TT_, :].rr("(tb p) c -> p tb c", p=128), vo)
    P.release(m)
def phaseB(P, cfg, PSall, l, S, consts):
    m = P.mark()
    PSs = Pool(PSall.tiles[0:5])
    PSo = Pool(PSall.tiles[5:7])
    PSb = Pool(PSall.tiles[7:8])
    Tm = cfg.Tmax
    kt_pool = P.sbpool("bkt", [96, Tm], BF16, 2)
    vh_pool = P.sbpool("bvh", [128, Tm // 128, 65], BF16, 2)
    for t in vh_pool.tiles:
        P.memset("pool", t[:, :, 64:65], 1.0)
    q_pool = P.sbpool("bq", [96, TT_], BF16, 3)
    pt_pool = P.sbpool("bpt", [128, TT_], BF16, 4)
    lrow_pool = P.sbpool("blr", [65, TT_], F32, 2)
    rec_pool = P.sbpool("brc", [64, TT_], F32, 2)
    ao_pool = P.sbpool("bao", [64, TT_], BF16, 3)
    ones32 = P.sb("bones", [65, 64], F32)
    P.memset("pool", ones32, 1.0)
    for si, T in enumerate(cfg.seqs):
        off = cfg.off[si]
        nk = T // 128
        for h in range(H):
            kt = kt_pool.next()
            vh = vh_pool.next()
            P.dma(kt[:, 0:T], S["KT"][h, :, off:off + T])
            for c0 in range(0, nk, 8):
                P.dma(vh[:, c0:c0 + 8, 0:64],
                      S["V"][off + c0 * 128:off + (c0 + 8) * 128, h * 64:(h + 1) * 64].rr("(c p) v -> p c v", p=128))
            for qt in range(T // TT_):
                g0 = off + qt * TT_
                q = q_pool.next()
                P.dma(q, S["QT"][h, :, g0:g0 + TT_])
                pso = PSo.next()
                pss = {}
                LA = 2
                for kc in range(min(LA, nk)):
                    pss[kc] = PSs.next()
                    P.mm(pss[kc], kt[:, kc * 128:(kc + 1) * 128], q)
                for kc in range(nk):
                    if kc + LA < nk:
                        pss[kc + LA] = PSs.next()
                        P.mm(pss[kc + LA], kt[:, (kc + LA) * 128:(kc + LA + 1) * 128], q)
                    pt = pt_pool.next()
                    P.act(pt, pss.pop(kc), AF.Exp)
                    P.mm(pso[0:65, :], vh[:, kc, :], pt, start=(kc == 0), stop=(kc == nk - 1))
                lrow = lrow_pool.next()
                P.copy("dve", lrow[64:65, :], pso[64:65, :])
                psb = PSb.next()
                P.mm(psb[0:64, :], ones32[64:65, :], lrow[64:65, :])
                rec = rec_pool.next()
                P.recip(rec, psb[0:64, :])
                ao = ao_pool.next()
                P.tt("dve", ao, pso[0:64, :], rec, ALU.mult)
                P.dma(S["ATT"][h * 64:(h + 1) * 64, g0:g0 + TT_], ao)
    P.release(m)
def run_rr(gens):
    gens = list(gens)
    while gens:
        nxt = []
        for g in gens:
            try:
                next(g)
                nxt.append(g)
            except StopIteration:
                pass
        gens = nxt


def phaseC(P, cfg, PS, l, W, vec, XT, S, consts):
    import os
    CT = float(os.environ.get("CT", "99"))
    TC = 256
    NCH = TC // CH
    m = P.mark()
    Wlu = P.sb("Wlu", [128, 1, C], BF16)
    Alu = P.sb("Alu", [128, 1, C], BF16)
    Glu = P.sb("Glu", [128, 1, C], BF16)
    if l > 0:
        Vld = P.sb("Vld", [128, 8, 128], BF16)
        Vlu = P.sb("Vlu", [128, 1, C], BF16)
    ms_ = P.mark()
    stage = P.sbpool("stg", [128, 2048], F32, 2)
    load_w(P, Wlu, W["w_lu"], 128, C, stage)
    load_w(P, Alu, W["a_lu"], 128, C, stage)
    load_w(P, Glu, W["g_lu"], 128, C, stage)
    if l > 0:
        P.memset("pool", Vld, 0.0)
        P.memset("pool", Vlu, 0.0)
        load_w(P, Vld[:, :, 0:32], W["v_ld"], D, 32, stage)
        load_w(P, Vlu, W["v_lu"], 32, C, stage)
    P.release(ms_)
    masks = P.sb("cmask", [128, 4, 2, 128], F32)
    P.dma(masks, consts["masks_d"])
    bdf = P.sb("cbdf", [128, 128], F32)
    P.dma(bdf, consts["bd_d"])
    bdb = P.sb("cbdb", [128, 128], BF16)
    P.copy("pool", bdb, bdf)
    bd64 = P.sb("cbd64", [128, 128], F32)
    P.ts("dve", bd64, bdf, 1.0 / 64, ALU.mult)
    id2 = P.sb("cid2", [128, 2, 128], F32)
    P.copy("pool", id2[:, 0, :], consts["ident"])
    P.copy("pool", id2[:, 1, :], consts["ident"])
    idb = P.sb("cidb", [128, 128], BF16)
    P.copy("pool", idb, consts["ident"])
    rmask = P.sb("crm", [128, TC], F32)
    P.dma(rmask, consts["rmask_d"][:, 0:TC])
    mo, _ = VOFF["mu"]
    om = P.sb("com", [128, 15], F32)
    hm = P.sb("chm", [128, 15], F32)
    P.ts("dve", om, vec[:, mo:mo + 15], -1.0, ALU.mult, 1.0, ALU.add)
    P.ts("dve", hm, vec[:, mo:mo + 15], 0.5, ALU.mult)
    eps12 = consts["eps_12"]
    epsg = consts["eps_gn"]

    zt_pool = P.sbpool("czt", [128, 3, TC + 2], F32, 2)
    zs_pool = P.sbpool("czs", [128, 15, TC], F32, 1)
    f_pool = P.sbpool("cf", [128, TC], F32, 4)
    fcb = [P.sbpool("cfc%d" % i, [128, TC], F32, 11) for i in range(4)]
    nt_pool = P.sbpool("cnt", [128, 4], F32, 8)
    bcb = [P.sbpool("cbc%d" % i, [128, TC], BF16, 2) for i in range(4)]
    ops_pool = P.sbpool("cops", [128, 4, 6, TC], BF16, 2)
    vb_pool = P.sbpool("cvb", [128, 4, TC], BF16, 2)
    pl_pool = P.sbpool("cpl", [128, 4, 4], F32, 3)
    yt_pool = P.sbpool("cyt", [128, 4, TC], F32, 2)
    sg_pool = P.sbpool("csg", [128, TC], BF16, 2)
    bon_pool = P.sbpool("cbon", [128, 4, TC], F32, 2)
    if l > 0:
        xh_pool = P.sbpool("cxh", [128, 2, TC], F32, 1)
        xb_pool = P.sbpool("cxb", [128, 8, TC], BF16, 1)
        vf_pool = P.sbpool("cvf", [128, 4, TC], F32, 1)
        xd_pool = P.sbpool("cxd", [128, TC], BF16, 1)
    tok_pool = P.sbpool("utok", [128, 4, 128], BF16, 8)
    mt_pool = P.sbpool("umt", [128, 6, 256], BF16, 4)
    mk_pool = P.sbpool("umk", [128, 256], BF16, 5)
    s_pool = P.sbpool("us", [128, 256], BF16, 5)
    nk_pool = P.sbpool("unk", [128, 256], BF16, 4)
    mr_pool = P.sbpool("umr", [128, 2, 256], BF16, 8)
    x1_pool = P.sbpool("ux1", [128, 128], BF16, 4)
    ut_pool = P.sbpool("uut", [128, 128], F32, 8)
    wt_pool = P.sbpool("uwt", [128, 128], BF16, 8)
    u_pool = P.sbpool("uu", [128, 128], BF16, 4)
    th_pool = P.sbpool("uth", [128, 128], F32, 4)
    pad_pools = []
    for nm in ("upB", "upA", "upR"):
        pp_ = P.sbpool(nm, [128, 2, 128], BF16, 4)
        for t_ in pp_.tiles:
            P.memset("pool", t_, 0.0)
        pad_pools.append(pp_)
    tzp = P.sb("ctzp", [128, TC], BF16)
    zap = P.sb("czap", [128, TC], BF16)
    f2_pool = P.sbpool("cf2", [128, TC], F32, 4)
    o_pool = P.sbpool("co", [128, TC], BF16, 2)
    Hf = [P.sb("Hf%d" % i, [128, 128], F32) for i in range(4)]
    Hb = [P.sb("Hb%d" % i, [128, 128], BF16) for i in range(4)]

    print("phaseC layer", l, "sbuf remaining", P.nc.sbuf_bytes_remaining)
    Zv = S["Z"].rr("(b p) t -> p b t", p=128)
    XTv = XT.rr("(kc p) t -> p kc t", p=128)
    YFv = S["YF"].rr("(b p) t -> p b t", p=128)
    BONv = S["BON"].rr("(b p) t -> p b t", p=128)
    VFv = S["VF"].rr("(b p) t -> p b t", p=128)
    RWv = S["RW"].rr("(b p) t -> p b t", p=128)
    V = VOFF
    mul, add, sub = ALU.mult, ALU.add, ALU.subtract

    def c1(si, g0, t0, d, res):
        T = cfg.seqs[si]
        zs = zs_pool.next()
        for gb in range(5):
            zt = zt_pool.next()
            P.dma(zt[:, :, 1:TC + 1], Zv[:, 3 * gb:3 * gb + 3, g0:g0 + TC])
            if t0 == 0:
                P.memset("pool", zt[:, :, 0:1], 0.0)
            else:
                P.dma(zt[:, :, 0:1], Zv[:, 3 * gb:3 * gb + 3, g0 - 1:g0])
            if t0 + TC >= T:
                P.memset("pool", zt[:, :, TC + 1:TC + 2], 0.0)
            else:
                P.dma(zt[:, :, TC + 1:TC + 2], Zv[:, 3 * gb:3 * gb + 3, g0 + TC:g0 + TC + 1])
            for q in range(3):
                b = 3 * gb + q
                t = f_pool.next()
                P.tt("dve", t, zt[:, q, 0:TC], zt[:, q, 2:TC + 2], add)
                P.act(zs[:, b, :], zt[:, q, 1:TC + 1], AF.Identity, scale=om[:, b:b + 1])
                P.stt(zs[:, b, :], t, hm[:, b:b + 1], zs[:, b, :], mul, add)
            yield
        hs = slice(64 * d, 64 * d + 64)
        tz = tzp
        P.act(tz[hs, :], zs[hs, 12, :], AF.Tanh)
        zab = zap
        P.copy("act", zab[hs, :], zs[hs, 13, :])
        if l > 0:
            xb = xb_pool.next()
            for hh in range(4):
                xh = xh_pool.next()
                P.dma(xh, XTv[:, 2 * hh:2 * hh + 2, g0:g0 + TC])
                P.copy(("dve", "act")[hh % 2], xb[:, 2 * hh:2 * hh + 2, :], xh)
            ps = PS.next()[:, 0:TC]
            for kc in range(8):
                P.mm(ps, Vld[:, kc, :], xb[:, kc, :], start=(kc == 0), stop=(kc == 7))
            xd = xd_pool.next()
            P.copy("act", xd, ps)
            vf = vf_pool.next()
            P.dma(vf, VFv[:, :, g0:g0 + TC])
            for cb in range(4):
                ps = PS.next()[:, 0:TC]
                P.mm(ps, Vlu[:, 0, cb * 128:(cb + 1) * 128], xd)
                vm = f_pool.next()
                P.act(vm, ps, AF.Sigmoid, bias=vec[:, V["v0"][0] + cb:V["v0"][0] + cb + 1])
                t = f_pool.next()
                P.tt("dve", t, vf[:, cb, :], zs[:, 8 + cb, :], sub)
                P.tt("dve", t, t, vm, mul)
                P.tt("dve", zs[:, 8 + cb, :], zs[:, 8 + cb, :], t, add)
        elif d == 0:
            P.dma(VFv[:, :, g0:g0 + TC], zs[:, 8:12, :])
        vb = vb_pool.next()
        P.copy("act", vb, zs[:, 8:12, :])
        yield
        ops = ops_pool.next()
        pl = pl_pool.next()
        gt = None
        bon = bon_pool.next()
        if d == 1:
            gt = sg_pool.next()
            P.act(gt, zs[:, 14, :], AF.Sigmoid)
        def chain(cb, f_pool, b_pool):
            r = zs[:, cb, :]
            k = zs[:, 4 + cb, :]
            v = zs[:, 8 + cb, :]
            col = lambda n: vec[:, V[n][0] + cb:V[n][0] + cb + 1]
            cold = lambda n: vec[:, V[n][0] + 4 * d + cb:V[n][0] + 4 * d + cb + 1]
            ps = PS.next()[:, 0:TC]
            P.mm(ps, Wlu[:, 0, cb * 128:(cb + 1) * 128], tz)
            lw = f_pool.next()
            P.act(lw, ps, AF.Sigmoid, bias=cold("w0"))
            yield
            P.act(lw, lw, AF.Identity, scale=-DECAY_SCALE)
            ps = PS.next()[:, 0:TC]
            P.mm(ps, Alu[:, 0, cb * 128:(cb + 1) * 128], zab)
            a = f_pool.next()
            P.act(a, ps, AF.Sigmoid, bias=cold("a0"))
            yield
            kkr = f_pool.next()
            P.act(kkr, k, AF.Identity, scale=col("kk"))
            sq = b_pool.next()
            P.act(sq, kkr, AF.Square)
            yield
            ps = PS.next()[:, 0:TC]
            P.mm(ps, bdb, sq)
            rs = f_pool.next()
            P.act(rs, ps, AF.Sqrt, bias=eps12, scale=1.0)
            yield
            P.recip(rs, rs)
            kk = kkr
            P.tt("dve", kk, kkr, rs, mul)
            kd = f_pool.next()
            P.ts("dve", kd, a, -1.0, add, col("ka"), mul)
            P.stt(kd, kd, 1.0, k, add, mul)
            yield
            kka = rs
            P.tt("dve", kka, kk, a, mul)
            yield
            t = f_pool.next()
            P.stt(t, r, col("rk"), kd, mul, mul)
            tb16 = b_pool.next()
            P.copy("act", tb16, t)
            yield
            ps = PS.next()[:, 0:TC]
            P.mm(ps, bdb, tb16)
            P.tt("dve", bon[:, cb, :], ps, v, mul)
            yield
            cum = f_pool.next()
            P.scan(cum, rmask, lw, 0.0, mul, add)
            yield
            E = lw
            P.tt("dve", E, cum, lw, sub)
            tot = cum.rr("p (c q) -> p c q", q=CH)[:, :, CH - 1]
            P.act(pl[:, cb, 0:NCH], tot, AF.Exp)
            ntot = nt_pool.next()
            P.ts("dve", ntot[:, 0:NCH], tot, -1.0, mul)
            yield
            pincl = f_pool.next()
            pexcl = f_pool.next()
            pinv = f_pool.next()
            pinv2 = a
            pinv2 = f_pool.next()
            if d == 0:
                P.act(pincl, cum, AF.Exp)
                P.act(pexcl, E, AF.Exp)
                P.act(pinv, cum, AF.Exp, scale=-1.0)
                for c in range(NCH):
                    cs = slice(c * CH, (c + 1) * CH)
                    P.act(pinv2[:, cs], cum[:, cs], AF.Exp, bias=cum[:, c * CH + CH - 1:c * CH + CH], scale=-1.0)
            else:
                P.act(pinv2, E, AF.Exp)
                for c in range(NCH):
                    cs = slice(c * CH, (c + 1) * CH)
                    tc_ = cum[:, c * CH + CH - 1:c * CH + CH]
                    P.act(pincl[:, cs], E[:, cs], AF.Exp, bias=tc_, scale=-1.0)
                    P.act(pexcl[:, cs], cum[:, cs], AF.Exp, bias=tc_, scale=-1.0)
                    P.act(pinv[:, cs], E[:, cs], AF.Exp, bias=ntot[:, c:c + 1], scale=1.0)
            yield
            P.tt("dve", ops[:, cb, 0, :], kk, pexcl, mul)
            P.tt("dve", ops[:, cb, 1, :], kka, pinv, mul)
            P.tt("dve", ops[:, cb, 2, :], kd, pinv, mul)
            P.tt("dve", ops[:, cb, 3, :], r, pincl, mul)
            P.tt("dve", ops[:, cb, 4, :], kka, pinv2, mul)
            P.tt("dve", ops[:, cb, 5, :], kd, pinv2, mul)
            yield
        yield from rr_gen([chain(cb, fcb[cb], bcb[cb]) for cb in range(4)])
        res.update(zs=zs, vb=vb, ops=ops, pl=pl, gt=gt, bon=bon)

    def unit_pre(d, c, cb, ops, vb, res):
        ms, msT, mi = (0, 2, 1) if d == 0 else (2, 0, 3)
        cs = slice(c * CH, (c + 1) * CH)
        Bt, At, Kt, Rt, At2, Kt2 = [ops[:, cb, i, cs] for i in range(6)]
        hp = [slice(0, 64), slice(64, 128)]
        m2 = lambda i: masks[:, i, :, :].rr("p a b -> p (a b)")
        pst = PS.next()
        pstb = TT(pst.ap.bitcast(BF16), pst.buf)
        for i, src in enumerate([Bt, At2, Kt2, vb[:, cb, cs]]):
            P.tr(pstb[:, i * 128:(i + 1) * 128], src, idb)
        tok = tok_pool.next()
        P.copy("act", tok.rr("p a b -> p (a b)"), pstb[:, 0:512])
        yield
        if CT <= 1.1:
            return
        pB, pA, pR = [pp_.next() for pp_ in pad_pools]
        for h in range(2):
            P.copy("act", pB[hp[h], h, :], Bt[hp[h], :])
            P.copy("dve", pA[hp[h], h, :], At[hp[h], :])
            P.copy(("act", "dve")[h], pR[hp[h], h, :], Rt[hp[h], :])
        f2 = lambda t_: t_.rr("p a b -> p (a b)")
        psN = PS.next()
        psT = PS.next()
        P.mm(psN[:, 0:256], At, f2(pB))
        P.mm(psT[:, 0:256], Bt, f2(pA))
        mt = mt_pool.next()
        mk = mk_pool.next()
        P.stt(mk, psN[:, 0:256], -1.0, m2(ms), mul, mul)
        P.stt(mt[:, 0, :], psT[:, 0:256], -1.0, m2(msT), mul, mul)
        yield
        if CT <= 1.2:
            return
        psK = PS.next()
        psA = PS.next()
        P.mm(psK[:, 0:256], Kt, f2(pB))
        P.mm(psA[:, 0:256], At, f2(pR))
        P.mm(psA[:, 256:512], Kt, f2(pR))
        nk = nk_pool.next()
        P.tt("dve", nk, psK[:, 0:256], m2(ms), mul)
        mr = mr_pool.next()
        P.tt("dve", mr[:, 0, :], psA[:, 0:256], m2(mi), mul)
        P.tt("dve", mr[:, 1, :], psA[:, 256:512], m2(mi), mul)
        yield
        if CT <= 1.3:
            return
        psX = PS.next()
        for h in range(2):
            P.mm(psX[:, h * 64:(h + 1) * 64], nk[:, h * 128:(h + 1) * 128], tok[:, 3, h * 64:(h + 1) * 64])
        x1 = x1_pool.next()
        P.copy("act", x1, psX[:, 0:128])
        yield
        if CT <= 1.4:
            return
        sb_ = None
        for kk_ in range(1, 7):
            psM = PS.next()
            for h in range(2):
                hs_ = slice(h * 128, (h + 1) * 128)
                P.mm(psM[:, hs_], mt[:, kk_ - 1, hs_], mk[:, hs_])
            if kk_ <= 5:
                psMT = PS.next()
                for h in range(2):
                    hs_ = slice(h * 128, (h + 1) * 128)
                    P.mm(psMT[:, hs_], mk[:, hs_], mt[:, kk_ - 1, hs_])
                mk = mk_pool.next()
                P.copy("act", mk, psM[:, 0:256])
                P.copy("dve", mt[:, kk_, :], psMT[:, 0:256])
            else:
                sb_ = s_pool.next()
                P.tt("dve", sb_, psM[:, 0:256], id2.rr("p a b -> p (a b)"), add)
            yield
        if CT <= 1.5:
            return
        for kk_ in range(5, -1, -1):
            psS = PS.next()
            for h in range(2):
                hs_ = slice(h * 128, (h + 1) * 128)
                P.mm(psS[:, hs_], mt[:, kk_, hs_], sb_[:, hs_])
            s2 = s_pool.next()
            P.tt("dve", s2, psS[:, 0:256], sb_, add)
            sb_ = s2
            yield
        psU = PS.next()
        for h in range(2):
            P.mm(psU[:, h * 64:(h + 1) * 64], sb_[:, h * 128:(h + 1) * 128], x1[:, h * 64:(h + 1) * 64])
        P.mm(psU[:, 128:384], tok[:, 0, :], sb_)
        ut = ut_pool.next()
        P.copy("act", ut, psU[:, 0:128])
        wt = wt_pool.next()
        P.copy("act", wt[0:64, :], psU[0:64, 128:256])
        P.copy("act", wt[64:128, :], psU[64:128, 256:384])
        res.update(tok=tok, mr=mr, ut=ut, wt=wt, Rt=Rt)
        yield

    def unit_seq(d, c, cb, pre, pl, yt):
        tok, mr, ut, wt, Rt = pre["tok"], pre["mr"], pre["ut"], pre["wt"], pre["Rt"]
        cs = slice(c * CH, (c + 1) * CH)
        psU = PS.next()
        P.mm(psU[:, 0:128], wt, Hb[cb])
        u = u_pool.next()
        P.stt(u, psU[:, 0:128], -1.0, ut, mul, sub)
        yield
        psO = PS.next()
        P.mm(psO[:, 0:256], tok[:, 3, :], mr[:, 1, :], start=True, stop=False)
        P.mm(psO[:, 0:256], u, mr[:, 0, :], start=False, stop=False)
        P.mm(psO[:, 0:128], Hb[cb], Rt, start=False, stop=False)
        P.mm(psO[:, 128:256], Hb[cb], Rt, start=False, stop=True)
        psH = PS.next()
        P.mm(psH[:, 0:128], tok[:, 2, :], tok[:, 3, :], start=True, stop=False)
        P.mm(psH[:, 0:128], tok[:, 1, :], u, start=False, stop=True)
        P.copy("act", yt[0:64, cb, cs], psO[0:64, 0:128])
        P.copy("act", yt[64:128, cb, cs], psO[64:128, 128:256])
        th = th_pool.next()
        P.tt("dve", th, psH[:, 0:128], bdf, mul)
        P.stt(Hf[cb], Hf[cb], pl[:, cb, c:c + 1], th, mul, add)
        P.copy("act", Hb[cb], Hf[cb])
        yield

    import os
    def rr_gen(gens):
        gens = list(gens)
        while gens:
            nxt = []
            for g in gens:
                try:
                    next(g)
                    nxt.append(g)
                except StopIteration:
                    pass
            gens = nxt
            yield

    def tile_units(d, g0, R):
        ops, vb, pl, gt, bon = R["ops"], R["vb"], R["pl"], R["gt"], R["bon"]
        yt = yt_pool.next()
        corder = list(range(NCH)) if d == 0 else list(range(NCH - 1, -1, -1))
        pres = {c: [dict() for _ in range(4)] for c in corder}
        yield from rr_gen([unit_pre(d, corder[0], cb, ops, vb, pres[corder[0]][cb]) for cb in range(4)])
        for idx, c in enumerate(corder):
            gl = [unit_seq(d, c, cb, pres[c][cb], pl, yt) for cb in range(4)]
            if idx + 1 < len(corder):
                cn = corder[idx + 1]
                gl += [unit_pre(d, cn, cb, ops, vb, pres[cn][cb]) for cb in range(4)]
            yield from rr_gen(gl)
        if d == 0:
            P.dma(YFv[:, :, g0:g0 + TC], yt)
            P.dma(BONv[:, :, g0:g0 + TC], bon)
        else:
            for cb in range(4):
                y0 = f2_pool.next()
                P.dma(y0, YFv[:, cb, g0:g0 + TC])
                b0 = f2_pool.next()
                P.dma(b0, BONv[:, cb, g0:g0 + TC])
                y = yt[:, cb, :]
                P.tt("dve", y, y, y0, add)
                P.tt("dve", b0, b0, bon[:, cb, :], add)
                pm = PS.next()
                P.mm(pm[:, 0:TC], bd64, y)
                sq = f2_pool.next()
                P.act(sq, y, AF.Square)
                pq = PS.next()
                P.mm(pq[:, 0:TC], bd64, sq)
                msq = f2_pool.next()
                P.act(msq, pm[:, 0:TC], AF.Square)
                var = sq
                P.tt("dve", var, pq[:, 0:TC], msq, sub)
                P.act(var, var, AF.Sqrt, bias=epsg, scale=1.0)
                P.recip(var, var)
                P.tt("dve", y, y, pm[:, 0:TC], sub)
                P.tt("dve", y, y, var, mul)
                P.ts("dve", y, y, vec[:, V["lng"][0] + cb:V["lng"][0] + cb + 1], mul,
                     vec[:, V["lnb"][0] + cb:V["lnb"][0] + cb + 1], add)
                P.tt("dve", y, y, b0, add)
                o = o_pool.next()
                pg = PS.next()
                P.mm(pg[:, 0:TC], Glu[:, 0, cb * 128:(cb + 1) * 128], gt)
                P.tt("dve", o, pg[:, 0:TC], y, mul)
                P.dma(RWv[:, cb, g0:g0 + TC], o)
                yield

    for d in range(2):
        P.memset("pool", tzp, 0.0)
        P.memset("pool", zap, 0.0)
        for si, T in enumerate(cfg.seqs):
            for cb in range(4):
                P.memset("pool", Hf[cb], 0.0)
                P.memset("pool", Hb[cb], 0.0)
            nt = T // TC
            order = list(range(nt)) if d == 0 else list(range(nt - 1, -1, -1))
            prev = None
            for j in order + [None]:
                gens = []
                R = None
                if j is not None:
                    t0 = j * TC
                    g0 = cfg.off[si] + t0
                    R = {"g0": g0}
                    gens.append(c1(si, g0, t0, d, R))
                if prev is not None:
                    gens.append(tile_units(d, prev["g0"], prev))
                run_rr(gens)
                prev = R
    P.release(m)
def layernorm(P, PS, y, gname, vec, ones32s, epst, sqp, stp, tp, emit):
    pm = PS.next()
    pq = PS.next()
    for b in range(8):
        P.mm(pm, ones32s, y[:, b, :], start=(b == 0), stop=(b == 7))
    for b in range(8):
        sq = sqp.next()
        P.act(sq, y[:, b, :], AF.Square)
        P.mm(pq, ones32s, sq, start=(b == 0), stop=(b == 7))
    mean = stp.next()
    P.copy("act", mean, pm)
    msq = stp.next()
    P.act(msq, pm, AF.Square)
    var = stp.next()
    P.tt("dve", var, pq, msq, ALU.subtract)
    P.act(var, var, AF.Sqrt, bias=epst, scale=1.0)
    P.recip(var, var)
    g0 = VOFF[gname + "g"][0]
    b0 = VOFF[gname + "b"][0]
    for b in range(8):
        t = tp.next()
        P.tt("dve", t, y[:, b, :], mean, ALU.subtract)
        P.tt("dve", t, t, var, ALU.mult)
        P.act(t, t, AF.Identity, bias=vec[:, b0 + b:b0 + b + 1], scale=vec[:, g0 + b:g0 + b + 1])
        emit(b, t)


def phaseD1(P, cfg, PS, l, W, vec, XT, S, consts):
    m = P.mark()
    Wpa = P.sb("Wpa", [128, 4, D], BF16)
    Wpb = P.sb("Wpb", [128, 4, D], BF16)
    Wo = P.sb("Wo", [128, 8, D], BF16)
    ms_ = P.mark()
    stage = P.sbpool("stg", [128, 2048], F32, 2)
    load_w(P, Wpa, W["w_pa"], C, D, stage)
    load_w(P, Wpb, W["w_pb"], C, D, stage)
    load_w(P, Wo, W["w_o"], D, D, stage)
    P.release(ms_)
    at_pool = P.sbpool("dat", [128, 8, TT_], BF16, 2)
    g_pool = P.sbpool("dg", [128, 16, TT_], F32, 1)
    xf_pool = P.sbpool("dxf", [128, 8, TT_], F32, 2)
    mx_pool = P.sbpool("dmx", [128, 8, TT_], BF16, 1)
    y_pool = P.sbpool("dy", [128, 8, TT_], F32, 1)
    t_pool = P.sbpool("dt", [128, TT_], F32, 4)
    sq_pool = P.sbpool("dsq", [128, TT_], F32, 2)
    st_pool = P.sbpool("dst", [128, TT_], F32, 3)
    XTv = XT.rr("(kc p) t -> p kc t", p=128)
    X1v = S["X1"].rr("(kc p) t -> p kc t", p=128)
    Gv = S["G"].rr("(b p) t -> p b t", p=128)
    ATv = S["ATT"].rr("(b p) t -> p b t", p=128)
    RWv = S["RW"].rr("(b p) t -> p b t", p=128)
    for (si, j, g0, t0) in cfg.tiles():
        at = at_pool.next()
        P.dma(at[:, 0:4, :], ATv[:, :, g0:g0 + TT_])
        P.dma(at[:, 4:8, :], RWv[:, :, g0:g0 + TT_])
        g = g_pool.next()
        P.dma(g[:, 0:8, :], Gv[:, 0:8, g0:g0 + TT_])
        P.dma(g[:, 8:16, :], Gv[:, 8:16, g0:g0 + TT_])
        xf = xf_pool.next()
        P.dma(xf, XTv[:, :, g0:g0 + TT_])
        mx = mx_pool.next()
        for mb in range(8):
            pa = PS.next()
            for kc in range(4):
                P.mm(pa, Wpa[:, kc, mb * 128:(mb + 1) * 128], at[:, kc, :], start=(kc == 0), stop=(kc == 3))
            pb = PS.next()
            for kc in range(4):
                P.mm(pb, Wpb[:, kc, mb * 128:(mb + 1) * 128], at[:, 4 + kc, :], start=(kc == 0), stop=(kc == 3))
            t1 = t_pool.next()
            t2 = t_pool.next()
            P.tt("dve", t1, pa, g[:, mb, :], ALU.mult)
            P.tt("dve", t2, pb, g[:, 8 + mb, :], ALU.mult)
            P.tt("dve", mx[:, mb, :], t1, t2, ALU.add)
        y = y_pool.next()
        for mb in range(8):
            po = PS.next()
            for kc in range(8):
                P.mm(po, Wo[:, kc, mb * 128:(mb + 1) * 128], mx[:, kc, :], start=(kc == 0), stop=(kc == 7))
            P.stt(y[:, mb, :], xf[:, mb, :], ALPHA, po, ALU.mult, ALU.add)

        def emit(b, t):
            P.dma(X1v[:, b, g0:g0 + TT_], t)
        layernorm(P, PS, y, "l1", vec, consts["ones32s"], consts["eps_ln"], sq_pool, st_pool, t_pool, emit)
    P.release(m)


def phaseD2a(P, cfg, PS, l, W, vec, S, consts):
    m = P.mark()
    Wup = P.sb("Wup", [128, 8, 2 * DFF], BF16)
    Wdn = P.sb("Wdn", [128, 22, D], BF16)
    ms_ = P.mark()
    stage = P.sbpool("stg", [128, 2048], F32, 2)
    load_w(P, Wup, W["w_ffn_up"], D, 2 * DFF, stage)
    load_w(P, Wdn, W["w_ffn_down"], DFF, D, stage)
    P.release(ms_)
    xf_pool = P.sbpool("exf", [128, 8, TT_], F32, 1)
    xb_pool = P.sbpool("exb", [128, 8, TT_], BF16, 1)
    xh_pool = P.sbpool("exh", [128, 8, 2], F32, 2)
    xhb_pool = P.sbpool("exhb", [128, 8, 2], BF16, 2)
    uh_pool = P.sbpool("euh", [128, 44, 2], F32, 2)
    ue_pool = P.sbpool("eue", [128, TT_ + 2], F32, 3)
    yc_pool = P.sbpool("eyc", [128, TT_], F32, 3)
    s_pool = P.sbpool("es", [128, TT_], F32, 3)
    gb_pool = P.sbpool("egb", [128, 22, TT_], BF16, 1)
    o_pool = P.sbpool("eo", [128, TT_], F32, 2)
    X1v = S["X1"].rr("(kc p) t -> p kc t", p=128)
    Y2v = S["Y2"].rr("(kc p) t -> p kc t", p=128)
    cw = [VOFF["cw0"][0], VOFF["cw1"][0], VOFF["cw2"][0]]
    cb = VOFF["cb"][0]
    for (si, j, g0, t0) in cfg.tiles():
        T = cfg.seqs[si]
        xf = xf_pool.next()
        P.dma(xf, X1v[:, :, g0:g0 + TT_])
        xh = xh_pool.next()
        if t0 == 0:
            P.memset("pool", xh[:, :, 0:1], 0.0)
        else:
            P.dma(xh[:, :, 0:1], X1v[:, :, g0 - 1:g0])
        if t0 + TT_ >= T:
            P.memset("pool", xh[:, :, 1:2], 0.0)
        else:
            P.dma(xh[:, :, 1:2], X1v[:, :, g0 + TT_:g0 + TT_ + 1])
        xb = xb_pool.next()
        P.copy("dve", xb[:, 0:4, :], xf[:, 0:4, :])
        P.copy("act", xb[:, 4:8, :], xf[:, 4:8, :])
        xhb = xhb_pool.next()
        P.copy("pool", xhb, xh)
        psh = PS.next()
        for b in range(44):
            for kc in range(8):
                P.mm(psh[:, 2 * b:2 * b + 2], Wup[:, kc, b * 128:(b + 1) * 128], xhb[:, kc, :], start=(kc == 0), stop=(kc == 7))
        uh = uh_pool.next()
        P.copy("dve", uh.rr("p b c -> p (b c)"), psh[:, 0:88])
        gb = gb_pool.next()

        def conv_block(b):
            ps = PS.next()
            for kc in range(8):
                P.mm(ps, Wup[:, kc, b * 128:(b + 1) * 128], xb[:, kc, :], start=(kc == 0), stop=(kc == 7))
            ue = ue_pool.next()
            P.act(ue[:, 1:TT_ + 1], ps, AF.Copy)
            P.copy("pool", ue[:, 0:1], uh[:, b, 0:1])
            P.copy("pool", ue[:, TT_ + 1:TT_ + 2], uh[:, b, 1:2])
            yc = yc_pool.next()
            P.act(yc, ue[:, 1:TT_ + 1], AF.Identity, bias=vec[:, cb + b:cb + b + 1], scale=vec[:, cw[1] + b:cw[1] + b + 1])
            P.stt(yc, ue[:, 0:TT_], vec[:, cw[0] + b:cw[0] + b + 1], yc, ALU.mult, ALU.add)
            P.stt(yc, ue[:, 2:TT_ + 2], vec[:, cw[2] + b:cw[2] + b + 1], yc, ALU.mult, ALU.add)
            return yc
        for i in range(22):
            ya = conv_block(i)
            yb = conv_block(22 + i)
            s = s_pool.next()
            P.act(s, ya, AF.Square)
            P.ts("dve", s, s, 0.044715, ALU.mult, 1.0, ALU.add)
            P.tt("pool", s, s, ya, ALU.mult)
            P.act(s, s, AF.Sigmoid, scale=1.5957691216057308)
            P.tt("dve", s, s, ya, ALU.mult)
            P.tt("dve", gb[:, i, :], s, yb, ALU.mult)
        for mb in range(8):
            ps = PS.next()
            for kc in range(22):
                P.mm(ps, Wdn[:, kc, mb * 128:(mb + 1) * 128], gb[:, kc, :], start=(kc == 0), stop=(kc == 21))
            o = o_pool.next()
            P.stt(o, xf[:, mb, :], ALPHA, ps, ALU.mult, ALU.add)
            P.dma(Y2v[:, mb, g0:g0 + TT_], o)
    P.release(m)


def phaseD2b(P, cfg, PS, l, W, vec, XTn, y_out, S, PTl, consts, last):
    m = P.mark()
    Wpg = P.sb("Wpg", [128, 8, D], BF16)
    Wpp = P.sb("Wpp", [128, 2, D], BF16)
    ms_ = P.mark()
    stage = P.sbpool("stg", [128, 2048], F32, 2)
    load_w(P, Wpg, W["w_pe_gate"], D, D, stage)
    load_w(P, Wpp, W["w_pe_proj"], PD, D, stage)
    P.release(ms_)
    xf_pool = P.sbpool("fxf", [128, 8, TT_], F32, 1)
    xb_pool = P.sbpool("fxb", [128, 8, TT_], BF16, 2)
    y_pool = P.sbpool("fy", [128, 8, TT_], F32, 2)
    pt_pool = P.sbpool("fpt", [128, 2, TT_], BF16, 2)
    t_pool = P.sbpool("ft", [128, TT_], F32, 4)
    sq_pool = P.sbpool("fsq", [128, TT_], F32, 2)
    st_pool = P.sbpool("fst", [128, TT_], F32, 3)
    x2_pool = P.sbpool("fx2", [128, 8, TT_], F32, 1)
    yo_pool = P.sbpool("fyo", [128, D], F32, 2)
    X1v = S["X1"].rr("(kc p) t -> p kc t", p=128)
    Y2v = S["Y2"].rr("(kc p) t -> p kc t", p=128)
    PTv = PTl.rr("(kc p) t -> p kc t", p=128)
    if not last:
        XTv = XTn.rr("(kc p) t -> p kc t", p=128)
    k = 0
    for (si, j, g0, t0) in cfg.tiles():
        xf = xf_pool.next()
        P.dma(xf, X1v[:, :, g0:g0 + TT_])
        y = y_pool.next()
        P.dma(y, Y2v[:, :, g0:g0 + TT_])
        pt = pt_pool.next()
        P.dma(pt, PTv[:, :, g0:g0 + TT_])
        xb = xb_pool.next()
        P.copy("dve", xb[:, 0:4, :], xf[:, 0:4, :])
        P.copy("act", xb[:, 4:8, :], xf[:, 4:8, :])
        for mb in range(8):
            pg = PS.next()
            for kc in range(8):
                P.mm(pg, Wpg[:, kc, mb * 128:(mb + 1) * 128], xb[:, kc, :], start=(kc == 0), stop=(kc == 7))
            pp = PS.next()
            for kc in range(2):
                P.mm(pp, Wpp[:, kc, mb * 128:(mb + 1) * 128], pt[:, kc, :], start=(kc == 0), stop=(kc == 1))
            sg = t_pool.next()
            P.act(sg, pg, AF.Sigmoid)
            P.tt("dve", sg, pp, sg, ALU.mult)
            P.tt("dve", y[:, mb, :], y[:, mb, :], sg, ALU.add)
        if not last:
            def emit(b, t):
                P.dma(XTv[:, b, g0:g0 + TT_], t)
            layernorm(P, PS, y, "l2", vec, consts["ones32s"], consts["eps_ln"], sq_pool, st_pool, t_pool, emit)
        else:
            x2 = x2_pool.next()

            def emit(b, t):
                P.copy("act", x2[:, b, :], t)
            layernorm(P, PS, y, "l2", vec, consts["ones32s"], consts["eps_ln"], sq_pool, st_pool, t_pool, emit)
            for tb in range(4):
                yo = yo_pool.next()
                for half in range(2):
                    ps = PS.next()
                    for q in range(4):
                        kc = half * 4 + q
                        P.tr(ps[:, q * 128:(q + 1) * 128], x2[:, kc, tb * 128:(tb + 1) * 128], consts["ident"])
                    P.copy(("act", "dve")[k % 2], yo[:, half * 512:(half + 1) * 512], ps)
                    k += 1
                P.dma(y_out[g0 + tb * 128:g0 + (tb + 1) * 128, :], yo)
    P.release(m)
WKEYS0 = ["w_in", "w_krr", "wq_n", "wq_r", "wq_rr", "wk", "wv", "w_lu", "a_lu", "g_lu", "w_pa", "w_pb", "w_o",
          "w_ffn_up", "w_ffn_down", "w_pe_gate", "w_pe_proj"]
WSHAPES = {"w_in": (D, INC), "w_krr": (D, 32), "wq_n": (QL, 512), "wq_r": (QL, 256), "wq_rr": (QL, 256),
           "wk": (KVL, 512), "wv": (KVL, 512), "w_lu": (128, C), "a_lu": (128, C), "g_lu": (128, C),
           "v_ld": (D, 32), "v_lu": (32, C), "w_pa": (C, D), "w_pb": (C, D), "w_o": (D, D),
           "w_ffn_up": (D, 2 * DFF), "w_ffn_down": (DFF, D), "w_pe_gate": (D, D), "w_pe_proj": (PD, D)}


def wkeys(l):
    return WKEYS0 + (["v_ld", "v_lu"] if l > 0 else [])


def build(cfg):
    nc = bass.Bass("TRN2", target_bir_lowering=False)
    P = Prog(nc)
    NT = cfg.NT
    dbg = cfg.debug

    def din(name, shape, dt=F32):
        return TT(nc.dram_tensor(name, list(shape), dt, kind="ExternalInput").ap(), Buf(name))
    x_in = din("x", [NT, D])
    p_in = din("p", [DEPTH, NT, PD])
    cd = {"cs": din("cs", [128, cfg.Tmax]), "sn": din("sn", [128, cfg.Tmax]), "ident_d": din("ident", [128, 128]),
          "masks_d": din("masks", [128, 4, 2, 128]), "bd_d": din("bd", [128, 128]), "rmask_d": din("rmask", [128, TT_])}
    Wd = []
    vecd = []
    for l in range(DEPTH):
        vecd.append(din("vec_%d" % l, [128, NVEC]))
        Wd.append({k: din("%s_%d" % (k, l), WSHAPES[k]) for k in wkeys(l)})
    y_out = TT(nc.dram_tensor("y", [NT, D], F32, kind="ExternalOutput").ap(), Buf("y"))
    kind = "ExternalOutput" if dbg else "Internal"
    S = {}
    for name, shape, dt in [("XT0", [D, NT], F32), ("XT1", [D, NT], F32), ("PT0", [PD, NT], BF16), ("PT1", [PD, NT], BF16),
                            ("QT", [H, QK, NT], BF16), ("KT", [H, QK, NT], BF16), ("V", [NT, 512], BF16),
                            ("Z", [1920, NT], F32), ("G", [2048, NT], F32), ("ATT", [512, NT], BF16),
                            ("RW", [512, NT], BF16), ("VF", [512, NT], F32), ("YF", [512, NT], F32),
                            ("BON", [512, NT], F32), ("X1", [D, NT], F32), ("Y2", [D, NT], F32)]:
        S[name] = P.dram(name, shape, dt, kind=kind)
    PSall = Pool([P.psum("ps%d" % i, [128, 512], F32) for i in range(8)])
    ident = P.sb("ident_sb", [128, 128], F32)
    P.dma(ident, cd["ident_d"])
    cd["ident"] = ident
    ones32s = P.sb("ones32s", [128, 128], F32)
    P.memset("pool", ones32s, 1.0 / D)
    cd["ones32s"] = ones32s
    for nm, val in [("eps_rms", RMS_EPS), ("eps_ln", LN_EPS), ("eps_12", 1e-12), ("eps_gn", GN_EPS)]:
        t = P.sb(nm, [128, 1], F32)
        P.memset("pool", t, val)
        cd[nm] = t
    vecs = []
    for l in range(DEPTH):
        v = P.sb("vec%d" % l, [128, NVEC], F32)
        P.dma(v, vecd[l])
        vecs.append(v)
    ph = cfg.phases
    XTs = [S["XT0"], S["XT1"]]
    PTs = [S["PT0"], S["PT1"]]
    if "0" in ph:
        phase0(P, cfg, PSall, x_in, p_in, XTs[0], PTs, ident)
    for l in range(cfg.nlayers if hasattr(cfg, "nlayers") else DEPTH):
        XT = XTs[l % 2]
        XTn = XTs[(l + 1) % 2]
        last = (l == DEPTH - 1)
        if "A" in ph:
            phaseA(P, cfg, PSall, l, Wd[l], vecs[l], XT, S, cd)
        if "B" in ph:
            phaseB(P, cfg, PSall, l, S, cd)
        if "C" in ph:
            phaseC(P, cfg, PSall, l, Wd[l], vecs[l], XT, S, cd)
        if "D" in ph:
            phaseD1(P, cfg, PSall, l, Wd[l], vecs[l], XT, S, cd)
        if "E" in ph:
            phaseD2a(P, cfg, PSall, l, Wd[l], vecs[l], S, cd)
            phaseD2b(P, cfg, PSall, l, Wd[l], vecs[l], XTn, y_out, S, PTs[l], cd, last)
    P.finish([y_out])
    return nc, P


def host_inputs(cfg, w, x_cores, p_cores):
    c = host_consts(cfg.Tmax)
    base = {"cs": c["cs"], "sn": c["sn"], "ident": c["ident"],
            "masks": np.ascontiguousarray(np.repeat(c["masks"].reshape(128, 4, 1, 128), 2, axis=2)),
            "bd": c["bd"], "rmask": c["rmask"]}
    for l in range(DEPTH):
        hp = host_layer_params(w, l)
        base["vec_%d" % l] = hp["vec"]
        for k in wkeys(l):
            base["%s_%d" % (k, l)] = np.ascontiguousarray(hp[k], dtype=np.float32)
    maps = []
    for xc, pc in zip(x_cores, p_cores):
        m = dict(base)
        m["x"] = np.ascontiguousarray(xc, dtype=np.float32)
        m["p"] = np.ascontiguousarray(pc, dtype=np.float32)
        maps.append(m)
    return maps


def kernel(**inputs):
    w = {k: np.asarray(v) for k, v in inputs.items()}
    xp, xs, pp, ps_ = w["x_prompt"], w["x_sample"], w["p_prompt"], w["p_sample"]
    n = 8
    Bp, Tp = xp.shape[0], xp.shape[1]
    Bs, Ts = xs.shape[0], xs.shape[1]
    npc = Bp // n
    nsc = Bs // n
    cfg = Cfg([Tp] * npc + [Ts] * nsc)
    x_cores, p_cores = [], []
    for c in range(n):
        xl = [xp[c * npc + i] for i in range(npc)] + [xs[c * nsc + i] for i in range(nsc)]
        pl = [pp[:, c * npc + i] for i in range(npc)] + [ps_[:, c * nsc + i] for i in range(nsc)]
        x_cores.append(np.concatenate(xl, 0))
        p_cores.append(np.concatenate(pl, 1))
    nc, P = build(cfg)
    maps = host_inputs(cfg, w, x_cores, p_cores)
    res = run_bass_kernel_spmd(nc, maps, core_ids=list(range(n)))
    yp = np.zeros(xp.shape, np.float32)
    ys = np.zeros(xs.shape, np.float32)
    for c in range(n):
        y = np.asarray(res.results[c]["y"], np.float32)
        o = 0
        for i in range(npc):
            yp[c * npc + i] = y[o:o + Tp]
            o += Tp
        for i in range(nsc):
            ys[c * nsc + i] = y[o:o + Ts]
            o += Ts
    return (yp, ys)
```

```python
import numpy as np
import concourse.bass as bass
import concourse.mybir as mybir
from concourse.bass_utils import run_bass_kernel_spmd

F32 = mybir.dt.float32
BF16 = mybir.dt.bfloat16
AF = mybir.ActivationFunctionType
ALU = mybir.AluOpType


class Buf:
    __slots__ = ("name", "w", "r", "psum")

    def __init__(self, name):
        self.name = name
        self.psum = False
        self.w = {}
        self.r = {}


class TT:
    __slots__ = ("ap", "buf")

    def __init__(self, ap, buf):
        self.ap = ap
        self.buf = buf

    def __getitem__(self, idx):
        return TT(self.ap[idx], self.buf)

    def rr(self, pat, **kw):
        return TT(self.ap.rearrange(pat, **kw), self.buf)


class Pool:
    def __init__(self, tiles):
        self.tiles = tiles
        self.i = 0

    def next(self):
        t = self.tiles[self.i % len(self.tiles)]
        self.i += 1
        return t


class Prog:
    def __init__(self, nc, n_dma_sems=40):
        self.nc = nc
        self.E = {"pe": nc.tensor, "dve": nc.vector, "act": nc.scalar, "pool": nc.gpsimd, "sp": nc.sync}
        self.sems = []
        self.semval = []
        self.esem = {}
        for e in self.E:
            self.esem[e] = self._newsem("s_" + e)
        self.dsems = [self._newsem("d%d" % i) for i in range(n_dma_sems)]
        self.di = 0
        self.known = {e: {} for e in self.E}
        self.ninst = {e: 0 for e in self.E}
        self._stack = []

    def _newsem(self, name):
        h = self.nc.alloc_semaphore(name)
        self.sems.append(h)
        self.semval.append(0)
        return len(self.sems) - 1

    def sb(self, name, shape, dtype):
        self._uid = getattr(self, "_uid", 0) + 1
        name = "%s_u%d" % (name, self._uid)
        cm = self.nc.sbuf_tensor(name, list(shape), dtype)
        t = cm.__enter__()
        self._stack.append(cm)
        return TT(t[tuple(slice(None) for _ in shape)], Buf(name))

    def sbpool(self, name, shape, dtype, n):
        return Pool([self.sb("%s%d" % (name, i), shape, dtype) for i in range(n)])

    def psum(self, name, shape, dtype):
        cm = self.nc.psum_tensor(name, list(shape), dtype)
        t = cm.__enter__()
        self._stack.append(cm)
        b = Buf(name)
        b.psum = True
        return TT(t[tuple(slice(None) for _ in shape)], b)

    def mark(self):
        return len(self._stack)

    def barrier(self):
        for e in self.E:
            for s in range(len(self.sems)):
                if self.semval[s] > 0:
                    self._wait(e, s, self.semval[s])

    def release(self, mark):
        self.barrier()
        while len(self._stack) > mark:
            cm = self._stack.pop()
            cm.__exit__(None, None, None)

    def dram(self, name, shape, dtype, kind="Internal"):
        t = self.nc.dram_tensor(name, list(shape), dtype, kind=kind)
        return TT(t.ap(), Buf(name))

    def _wait(self, eng, sem, val):
        k = self.known[eng]
        if k.get(sem, 0) >= val:
            return
        self.E[eng].wait_ge(self.sems[sem], val)
        k[sem] = val

    def _pre(self, eng, reads, writes, acc=False):
        for t in reads:
            for s, v in t.buf.w.items():
                self._wait(eng, s, v)
            if t.buf.psum:
                for s, v in t.buf.r.items():
                    if s != self.esem[eng]:
                        self._wait(eng, s, v)
        for t in writes:
            b = t.buf
            for s, v in b.w.items():
                if eng == "pe" and s == self.esem["pe"]:
                    continue
                self._wait(eng, s, v)
            for s, v in b.r.items():
                if eng == "pe" and s == self.esem["pe"]:
                    continue
                self._wait(eng, s, v)

    def _post(self, eng, ins, reads, writes):
        s = self.esem[eng]
        self.semval[s] += 1
        v = self.semval[s]
        ins.then_inc(self.sems[s], 1)
        self.ninst[eng] += 1
        for t in reads:
            t.buf.r[s] = v
        for t in writes:
            t.buf.w[s] = v
            t.buf.r = {}

    @staticmethod
    def _ap(x):
        return x.ap if isinstance(x, TT) else x

    def _tts(self, *xs):
        return [x for x in xs if isinstance(x, TT)]

    def mm(self, out, lhsT, rhs, start=True, stop=True):
        self._pre("pe", [lhsT, rhs], [out])
        ins = self.nc.tensor.matmul(out.ap, lhsT.ap, rhs.ap, start=start, stop=stop)
        self._post("pe", ins, [lhsT, rhs], [out])

    def tr(self, out, in_, ident):
        self._pre("pe", [in_, ident], [out])
        ins = self.nc.tensor.transpose(out.ap, in_.ap, ident.ap)
        self._post("pe", ins, [in_, ident], [out])

    def act(self, out, in_, func=None, bias=None, scale=None):
        func = func if func is not None else AF.Copy
        rd = self._tts(in_, bias, scale)
        self._pre("act", rd, [out])
        kw = {}
        if bias is not None:
            kw["bias"] = self._ap(bias)
        if scale is not None:
            kw["scale"] = self._ap(scale)
        ins = self.nc.scalar.activation(out.ap, in_.ap, func, **kw)
        self._post("act", ins, rd, [out])

    def tt(self, eng, out, a, b, op):
        self._pre(eng, [a, b], [out])
        ins = self.E[eng].tensor_tensor(out.ap, a.ap, b.ap, op)
        self._post(eng, ins, [a, b], [out])

    def ts(self, eng, out, a, s1, op0, s2=None, op1=None):
        rd = self._tts(a, s1, s2)
        self._pre(eng, rd, [out])
        if op1 is None:
            ins = self.E[eng].tensor_scalar(out.ap, a.ap, self._ap(s1), None, op0)
        else:
            ins = self.E[eng].tensor_scalar(out.ap, a.ap, self._ap(s1), self._ap(s2), op0, op1)
        self._post(eng, ins, rd, [out])

    def stt(self, out, a, s, b, op0, op1):
        rd = self._tts(a, s, b)
        self._pre("dve", rd, [out])
        ins = self.nc.vector.scalar_tensor_tensor(out.ap, a.ap, self._ap(s), b.ap, op0, op1)
        self._post("dve", ins, rd, [out])

    def copy(self, eng, out, in_):
        if eng == "act":
            return self.act(out, in_, AF.Copy)
        self._pre(eng, [in_], [out])
        ins = self.E[eng].tensor_copy(out.ap, in_.ap)
        self._post(eng, ins, [in_], [out])

    def memset(self, eng, out, val):
        self._pre(eng, [], [out])
        ins = self.E[eng].memset(out.ap, val)
        self._post(eng, ins, [], [out])

    def recip(self, out, in_):
        self._pre("dve", [in_], [out])
        ins = self.nc.vector.reciprocal(out.ap, in_.ap)
        self._post("dve", ins, [in_], [out])

    def scan(self, out, d0, d1, init, op0, op1):
        rd = self._tts(d0, d1, init)
        self._pre("dve", rd, [out])
        ins = self.nc.vector.tensor_tensor_scan(out.ap, d0.ap, d1.ap, self._ap(init), op0, op1)
        self._post("dve", ins, rd, [out])

    def dma(self, out, in_, q="sp"):
        self._pre(q, [in_], [out])
        s = self.dsems[self.di % len(self.dsems)]
        self.di += 1
        self._wait(q, s, self.semval[s])
        self.semval[s] += 16
        v = self.semval[s]
        self.E[q].dma_start(out=out.ap, in_=in_.ap, allow_slow_non_contiguous=True).then_inc(self.sems[s], 16)
        in_.buf.r[s] = v
        out.buf.w[s] = v
        out.buf.r = {}

    def finish(self, outs):
        for s in self.dsems:
            if self.semval[s] > 0:
                self._wait("sp", s, self.semval[s])
        self.release(0)
D = 1024
DEPTH = 2
H = 8
NOPE, ROPE, VD = 64, 32, 64
QK = 96
QL, KVL = 768, 256
C = 512
DFF = 2816
PD = 256
ALPHA = (2 * DEPTH) ** 0.25
LN_EPS = 1e-5
RMS_EPS = 1e-6
GN_EPS = 64e-5
DECAY_SCALE = 0.606531
OFF_CKV, OFF_KR, OFF_RW = 768, 1024, 1056
OFF_GA = OFF_RW + 1920
OFF_GB = OFF_GA + 1024
INC = OFF_GB + 1024
TT_ = 512
CH = 128

VOFF = {}
_o = 0
for _n, _w in [("qg", 6), ("kvg", 2), ("mu", 15), ("w0", 8), ("a0", 8), ("kk", 4), ("ka", 4), ("rk", 4),
               ("v0", 4), ("lng", 4), ("lnb", 4), ("l1g", 8), ("l1b", 8), ("cw0", 44), ("cw1", 44),
               ("cw2", 44), ("cb", 44), ("l2g", 8), ("l2b", 8)]:
    VOFF[_n] = (_o, _w)
    _o += _w
NVEC = _o


def host_consts(Tmax):
    c = {}
    c["ident"] = np.eye(128, dtype=np.float32)
    pos = np.arange(Tmax, dtype=np.float32)
    inv = (np.float32(10000.0) ** (-np.arange(0, ROPE, 2, dtype=np.float32) / np.float32(ROPE))).astype(np.float32)
    ang = (pos[:, None] * inv[None, :]).astype(np.float32)
    cos = np.cos(ang).astype(np.float32).T
    sin = np.sin(ang).astype(np.float32).T
    c["cs"] = np.ascontiguousarray(np.tile(np.concatenate([cos, cos], 0), (4, 1)))
    c["sn"] = np.ascontiguousarray(np.tile(np.concatenate([sin, sin], 0), (4, 1)))
    s = np.arange(128)[:, None]
    t = np.arange(128)[None, :]
    m = np.zeros((128, 4, 128), np.float32)
    m[:, 0] = (s < t)
    m[:, 1] = (s <= t)
    m[:, 2] = (s > t)
    m[:, 3] = (s >= t)
    c["masks"] = m.reshape(128, 512)
    bd = np.zeros((128, 128), np.float32)
    bd[:64, :64] = 1
    bd[64:, 64:] = 1
    c["bd"] = bd
    rm = np.ones((128, TT_), np.float32)
    rm[:, ::CH] = 0
    c["rmask"] = rm
    return c


def blk(v, n):
    return np.ascontiguousarray(np.asarray(v, np.float32).reshape(n, 128).T)


def host_layer_params(w, l):
    o = {}
    vec = np.zeros((128, NVEC), np.float32)

    def put(name, arr):
        a, n = VOFF[name]
        assert arr.shape == (128, n), (name, arr.shape)
        vec[:, a:a + n] = arr
    put("qg", blk(w["q_norm_g"][l], 6))
    put("kvg", blk(w["kv_norm_g"][l], 2))
    put("mu", blk(w["tshift_mu"][l], 15))
    put("w0", blk(w["w0"][l].reshape(-1), 8))
    put("a0", blk(w["a0"][l].reshape(-1), 8))
    put("kk", blk(w["k_k"][l], 4))
    put("ka", blk(w["k_a"][l], 4))
    put("rk", blk(w["r_k"][l], 4))
    if l > 0:
        put("v0", blk(w["v0"][l - 1], 4))
    put("lng", blk(w["lnx_g"][l], 4))
    put("lnb", blk(w["lnx_b"][l], 4))
    put("l1g", blk(w["ln1_g"][l], 8))
    put("l1b", blk(w["ln1_b"][l], 8))
    for i in range(3):
        put("cw%d" % i, blk(w["conv_w"][l, i], 44))
    put("cb", blk(w["conv_b"][l], 44))
    put("l2g", blk(w["ln2_g"][l], 8))
    put("l2b", blk(w["ln2_b"][l], 8))
    o["vec"] = vec
    win = np.asarray(w["w_in"][l], np.float32)
    o["w_in"] = win
    o["w_krr"] = np.ascontiguousarray(np.concatenate([win[:, OFF_KR + 16:OFF_KR + 32], win[:, OFF_KR:OFF_KR + 16]], 1))
    wq = np.asarray(w["w_uq"][l], np.float32).reshape(QL, H, QK)
    o["wq_n"] = np.ascontiguousarray(wq[:, :, :NOPE].reshape(QL, H * NOPE))
    o["wq_r"] = np.ascontiguousarray(wq[:, :, NOPE:].reshape(QL, H * ROPE))
    o["wq_rr"] = np.ascontiguousarray(np.concatenate([wq[:, :, NOPE + 16:], wq[:, :, NOPE:NOPE + 16]], 2).reshape(QL, H * ROPE))
    wkv = np.asarray(w["w_ukv"][l], np.float32).reshape(KVL, H, NOPE + VD)
    o["wk"] = np.ascontiguousarray(wkv[:, :, :NOPE].reshape(KVL, H * NOPE))
    o["wv"] = np.ascontiguousarray(wkv[:, :, NOPE:].reshape(KVL, H * VD))
    o["w_lu"] = np.ascontiguousarray(np.asarray(w["w_lora_up"][l], np.float32).reshape(128, C))
    o["a_lu"] = np.ascontiguousarray(np.asarray(w["a_lora_up"][l], np.float32).reshape(128, C))
    o["g_lu"] = np.asarray(w["g_lora_up"][l], np.float32)
    if l > 0:
        o["v_ld"] = np.asarray(w["v_lora_down"][l - 1], np.float32)
        o["v_lu"] = np.asarray(w["v_lora_up"][l - 1], np.float32)
    for k in ["w_pa", "w_pb", "w_o", "w_ffn_up", "w_ffn_down", "w_pe_gate", "w_pe_proj"]:
        o[k] = np.asarray(w[k][l], np.float32)
    return o


class Cfg:
    def __init__(self, seqs, debug=False, phases="0ABCDE"):
        self.seqs = list(seqs)
        self.NT = sum(seqs)
        self.off = [sum(seqs[:i]) for i in range(len(seqs))]
        self.Tmax = max(seqs)
        self.debug = debug
        self.phases = phases

    def tiles(self):
        for si, T in enumerate(self.seqs):
            for j in range(T // TT_):
                yield si, j, self.off[si] + j * TT_, j * TT_
def load_w(P, dst, src, K, N, stage, engs=("dve", "pool"), scale_vec=None, neg=None):
    KC = (K + 127) // 128
    i = 0
    for kc in range(KC):
        rows = min(128, K - kc * 128)
        for c0 in range(0, N, 2048):
            cw = min(2048, N - c0)
            st = stage.next()
            P.dma(st[0:rows, 0:cw], src[kc * 128:kc * 128 + rows, c0:c0 + cw])
            eng = engs[i % len(engs)]
            i += 1
            if scale_vec is None:
                P.copy(eng, dst[0:rows, kc, c0:c0 + cw], st[0:rows, 0:cw])
            else:
                P.ts(eng, dst[0:rows, kc, c0:c0 + cw], st[0:rows, 0:cw], scale_vec[0:rows, kc:kc + 1], ALU.mult)
    if neg is not None:
        for (a, b) in neg:
            P.ts("pool", dst[:, :, a:b], dst[:, :, a:b], -1.0, ALU.mult)


def phase0(P, cfg, PS, x_in, p_in, XT, PTs, ident):
    m = P.mark()
    xin_pool = P.sbpool("p0x", [128, 4, 1024], F32, 2)
    xf_pool = P.sbpool("p0f", [128, 8, 512], F32, 2)
    pin_pool = P.sbpool("p0p", [128, 4, 256], F32, 2)
    pb_pool = P.sbpool("p0b", [128, 2, 512], BF16, 2)
    XTv = XT.rr("(kc p) t -> p kc t", p=128)
    k = 0
    for (si, j, g0, t0) in cfg.tiles():
        xin = xin_pool.next()
        P.dma(xin, x_in[g0:g0 + TT_, :].rr("(tb p) f -> p tb f", p=128))
        xf = xf_pool.next()
        for kc in range(8):
            ps = PS.next()
            for tb in range(4):
                P.tr(ps[:, tb * 128:(tb + 1) * 128], xin[:, tb, kc * 128:(kc + 1) * 128], ident)
            P.copy(("act", "dve")[k % 2], xf[:, kc, :], ps)
            k += 1
        P.dma(XTv[:, :, g0:g0 + TT_], xf)
        for l in range(DEPTH):
            pin = pin_pool.next()
            P.dma(pin, p_in[l, g0:g0 + TT_, :].rr("(tb p) f -> p tb f", p=128))
            pb = pb_pool.next()
            for kc in range(2):
                ps = PS.next()
                for tb in range(4):
                    P.tr(ps[:, tb * 128:(tb + 1) * 128], pin[:, tb, kc * 128:(kc + 1) * 128], ident)
                P.copy(("act", "dve")[k % 2], pb[:, kc, :], ps)
                k += 1
            P.dma(PTs[l].rr("(kc p) t -> p kc t", p=128)[:, :, g0:g0 + TT_], pb)
    P.release(m)


def phaseA(P, cfg, PS, l, W, vec, XT, S, consts):
    m = P.mark()
    Win = P.sb("Win", [128, 8, INC], BF16)
    Wkrr = P.sb("Wkrr", [128, 8, 128], BF16)
    P.memset("pool", Wkrr, 0.0)
    Wqn = P.sb("Wqn", [128, 6, 512], BF16)
    Wqr = P.sb("Wqr", [128, 6, 256], BF16)
    Wqrr = P.sb("Wqrr", [128, 6, 256], BF16)
    Wk = P.sb("Wk", [128, 2, 512], BF16)
    Wv = P.sb("Wv", [128, 2, 512], BF16)
    ones = P.sb("onesb", [128, 128], BF16)
    P.memset("pool", ones, 1.0)
    ms_ = P.mark()
    stage = P.sbpool("stg", [128, 2048], F32, 2)
    qg = vec[:, VOFF["qg"][0]:VOFF["qg"][0] + 6]
    kvg = vec[:, VOFF["kvg"][0]:VOFF["kvg"][0] + 2]
    load_w(P, Win, W["w_in"], D, INC, stage)
    load_w(P, Wkrr[:, :, 0:32], W["w_krr"], D, 32, stage)
    P.ts("pool", Wkrr[:, :, 0:16], Wkrr[:, :, 0:16], -1.0, ALU.mult)
    load_w(P, Wqn, W["wq_n"], QL, 512, stage, scale_vec=qg)
    load_w(P, Wqr, W["wq_r"], QL, 256, stage, scale_vec=qg)
    load_w(P, Wqrr, W["wq_rr"], QL, 256, stage, scale_vec=qg)
    P.ts("pool", Wqrr.rr("p k (h r) -> p k h r", r=32)[:, :, :, 0:16], Wqrr.rr("p k (h r) -> p k h r", r=32)[:, :, :, 0:16], -1.0, ALU.mult)
    load_w(P, Wk, W["wk"], KVL, 512, stage, scale_vec=kvg)
    load_w(P, Wv, W["wv"], KVL, 512, stage, scale_vec=kvg)
    P.release(ms_)

    xf_pool = P.sbpool("axf", [128, 8, TT_], F32, 1)
    xb_pool = P.sbpool("axb", [128, 8, TT_], BF16, 2)
    cs_pool = P.sbpool("acs", [128, 2, TT_], F32, 2)
    cq_pool = P.sbpool("acq", [128, 8, TT_], F32, 1)
    sq_pool = P.sbpool("asq", [128, TT_], BF16, 3)
    cn_pool = P.sbpool("acn", [128, 8, TT_], BF16, 1)
    rs_pool = P.sbpool("ars", [128, 2, TT_], F32, 1)
    zo_pool = P.sbpool("azo", [128, TT_], F32, 4)
    qo_pool = P.sbpool("aqo", [128, TT_], BF16, 6)
    t1_pool = P.sbpool("at1", [128, TT_], F32, 3)
    vo_pool = P.sbpool("avo", [128, 4, 512], BF16, 1)
    XTv = XT.rr("(kc p) t -> p kc t", p=128)
    Zv = S["Z"].rr("(b p) t -> p b t", p=128)
    Gv = S["G"].rr("(b p) t -> p b t", p=128)
    tiles = list(cfg.tiles())
    PSacc = Pool(PS.tiles[0:2])
    PS = Pool(PS.tiles[2:8])
    import os
    AT = float(os.environ.get("AT", "99"))

    def load(i):
        si, j, g0, t0 = tiles[i]
        xf = xf_pool.next()
        P.dma(xf, XTv[:, :, g0:g0 + TT_])
        cs = cs_pool.next()
        P.dma(cs[:, 0, :], consts["cs"][:, t0:t0 + TT_])
        P.dma(cs[:, 1, :], consts["sn"][:, t0:t0 + TT_])
        return xf, cs
    if AT <= 0:
        P.release(m)
        return
    nxt = load(0)
    ev = 0
    for i, (si, j, g0, t0) in enumerate(tiles):
        xf, cs = nxt
        xb = xb_pool.next()
        P.copy("act", xb[:, 0:4, :], xf[:, 0:4, :])
        P.copy("dve", xb[:, 4:8, :], xf[:, 4:8, :])
        if i + 1 < len(tiles):
            nxt = load(i + 1)
        if cfg.debug and i == 0 and l == 0:
            dbg1 = P.dram("dbg_xb", [128, 8, TT_], BF16, kind="ExternalOutput")
            P.dma(dbg1, xb)
            for ii, cc in enumerate([0, 1024, 2048, 4096]):
                dbg2 = P.dram("dbg_win%d" % ii, [128, 8, 512], BF16, kind="ExternalOutput")
                P.dma(dbg2, Win[:, :, cc:cc + 512])
            dbg3 = P.dram("dbg_xf", [128, 8, TT_], F32, kind="ExternalOutput")
            P.dma(dbg3, xf)

        def proj(c0, mw, Wt=Win, KC=8, rhs=xb):
            ps = PS.next()
            for kc in range(KC):
                P.mm(ps[0:mw, :], Wt[:, kc, c0:c0 + mw], rhs[:, kc, :], start=(kc == 0), stop=(kc == KC - 1))
            return ps
        if AT <= 0.5:
            continue
        cq = cq_pool.next()
        cn = cn_pool.next()
        rs = rs_pool.next()
        ssq = [PSacc.next(), PSacc.next()]
        sqs = []
        for b in range(8):
            ps = proj(b * 128, 128)
            P.copy("dve", cq[:, b, :], ps)
            sq = sq_pool.next()
            P.act(sq, ps, AF.Square)
            sqs.append(sq)
            which = 0 if b < 6 else 1
            first = b in (0, 6)
            last = b in (5, 7)
            P.mm(ssq[which], ones, sq, start=first, stop=last)
        if cfg.debug and i == 0 and l == 0:
            dbg4 = P.dram("dbg_cq", [128, 8, TT_], F32, kind="ExternalOutput")
            P.dma(dbg4, cq)
        if AT <= 0.6:
            continue
        for which, (n, b0, b1) in enumerate([(QL, 0, 6), (KVL, 6, 8)]):
            P.act(rs[:, which, :], ssq[which], AF.Sqrt, bias=consts["eps_rms"], scale=1.0 / n)
            if AT <= 0.7:
                continue
            P.recip(rs[:, which, :], rs[:, which, :])
            if AT <= 0.8:
                continue
            for b in range(b0, b1):
                P.tt("dve", cn[:, b, :], cq[:, b, :], rs[:, which, :], ALU.mult)
        if AT <= 0.4:
            continue
        for b in range(15):
            ps = proj(OFF_RW + b * 128, 128)
            zo = zo_pool.next()
            P.copy(("act", "dve")[ev % 2], zo, ps)
            ev += 1
            P.dma(Zv[:, b, g0:g0 + TT_], zo)
            if cfg.debug and i == 0 and l == 0 and b == 0:
                dz = P.sb("dbgz", [128, TT_], F32)
                P.copy("dve", dz, ps)
                P.dma(P.dram("dbg_z0", [128, TT_], F32, kind="ExternalOutput"), dz)
                P.dma(P.dram("dbg_z1", [128, TT_], F32, kind="ExternalOutput"), zo)
        if AT <= 0.45:
            continue
        for b in range(16):
            ps = proj(OFF_GA + b * 128, 128)
            zo = zo_pool.next()
            P.act(zo, ps, AF.Sigmoid)
            P.dma(Gv[:, b, g0:g0 + TT_], zo)
        if AT <= 1:
            continue
        ps = proj(OFF_KR, 128)
        ps2 = proj(0, 128, Wt=Wkrr)
        t1 = t1_pool.next()
        t2 = t1_pool.next()
        P.tt("dve", t1[0:32, :], ps[0:32, :], cs[0:32, 0, :], ALU.mult)
        P.tt("dve", t2[0:32, :], ps2[0:32, :], cs[0:32, 1, :], ALU.mult)
        kr = qo_pool.next()
        P.tt("dve", kr[0:32, :], t1[0:32, :], t2[0:32, :], ALU.add)
        for h in range(H):
            P.dma(S["KT"][h, 64:96, g0:g0 + TT_], kr[0:32, :])
        if AT <= 4:
            continue
        scale = QK ** -0.5
        for b in range(4):
            ps = proj(b * 128, 128, Wt=Wqn, KC=6, rhs=cn)
            qo = qo_pool.next()
            P.act(qo, ps, AF.Copy, scale=scale)
            for hh in range(2):
                P.dma(S["QT"][2 * b + hh, 0:64, g0:g0 + TT_], qo[hh * 64:(hh + 1) * 64, :])
        for b in range(2):
            ps = proj(b * 128, 128, Wt=Wqr, KC=6, rhs=cn)
            ps2 = proj(b * 128, 128, Wt=Wqrr, KC=6, rhs=cn)
            t1 = t1_pool.next()
            t2 = t1_pool.next()
            P.tt("dve", t1, ps, cs[:, 0, :], ALU.mult)
            P.tt("dve", t2, ps2, cs[:, 1, :], ALU.mult)
            qo = qo_pool.next()
            P.tt("dve", t1, t1, t2, ALU.add)
            P.act(qo, t1, AF.Copy, scale=scale)
            for hh in range(4):
                P.dma(S["QT"][4 * b + hh, 64:96, g0:g0 + TT_], qo[hh * 32:(hh + 1) * 32, :])
        if AT <= 5:
            continue
        for b in range(4):
            ps = proj(b * 128, 128, Wt=Wk, KC=2, rhs=cn[:, 6:8, :])
            qo = qo_pool.next()
            P.copy(("act", "dve")[b % 2], qo, ps)
            for hh in range(2):
                P.dma(S["KT"][2 * b + hh, 0:64, g0:g0 + TT_], qo[hh * 64:(hh + 1) * 64, :])
        if AT <= 6:
            continue
        vo = vo_pool.next()
        for tb in range(4):
            ps = PS.next()
            for kc in range(2):
                P.mm(ps, cn[:, 6 + kc, tb * 128:(tb + 1) * 128], Wv[:, kc, :], start=(kc == 0), stop=(kc == 1))
            P.copy(("act", "dve")[tb % 2], vo[:, tb, :], ps)
        P.dma(S["V"][g0:g0 + TT_, :].rr("(tb p) c -> p tb c", p=128), vo)
    P.release(m)
def phaseB(P, cfg, PSall, l, S, consts):
    m = P.mark()
    PSs = Pool(PSall.tiles[0:5])
    PSo = Pool(PSall.tiles[5:7])
    PSb = Pool(PSall.tiles[7:8])
    Tm = cfg.Tmax
    kt_pool = P.sbpool("bkt", [96, Tm], BF16, 2)
    vh_pool = P.sbpool("bvh", [128, Tm // 128, 65], BF16, 2)
    for t in vh_pool.tiles:
        P.memset("pool", t[:, :, 64:65], 1.0)
    q_pool = P.sbpool("bq", [96, TT_], BF16, 3)
    pt_pool = P.sbpool("bpt", [128, TT_], BF16, 4)
    lrow_pool = P.sbpool("blr", [65, TT_], F32, 2)
    rec_pool = P.sbpool("brc", [64, TT_], F32, 2)
    ao_pool = P.sbpool("bao", [64, TT_], BF16, 3)
    ones32 = P.sb("bones", [65, 64], F32)
    P.memset("pool", ones32, 1.0)
    for si, T in enumerate(cfg.seqs):
        off = cfg.off[si]
        nk = T // 128
        for h in range(H):
            kt = kt_pool.next()
            vh = vh_pool.next()
            P.dma(kt[:, 0:T], S["KT"][h, :, off:off + T])
            for c0 in range(0, nk, 8):
                P.dma(vh[:, c0:c0 + 8, 0:64],
                      S["V"][off + c0 * 128:off + (c0 + 8) * 128, h * 64:(h + 1) * 64].rr("(c p) v -> p c v", p=128))
            for qt in range(T // TT_):
                g0 = off + qt * TT_
                q = q_pool.next()
                P.dma(q, S["QT"][h, :, g0:g0 + TT_])
                pso = PSo.next()
                pss = {}
                LA = 2
                for kc in range(min(LA, nk)):
                    pss[kc] = PSs.next()
                    P.mm(pss[kc], kt[:, kc * 128:(kc + 1) * 128], q)
                for kc in range(nk):
                    if kc + LA < nk:
                        pss[kc + LA] = PSs.next()
                        P.mm(pss[kc + LA], kt[:, (kc + LA) * 128:(kc + LA + 1) * 128], q)
                    pt = pt_pool.next()
                    P.act(pt, pss.pop(kc), AF.Exp)
                    P.mm(pso[0:65, :], vh[:, kc, :], pt, start=(kc == 0), stop=(kc == nk - 1))
                lrow = lrow_pool.next()
                P.copy("dve", lrow[64:65, :], pso[64:65, :])
                psb = PSb.next()
                P.mm(psb[0:64, :], ones32[64:65, :], lrow[64:65, :])
                rec = rec_pool.next()
                P.recip(rec, psb[0:64, :])
                ao = ao_pool.next()
                P.tt("dve", ao, pso[0:64, :], rec, ALU.mult)
                P.dma(S["ATT"][h * 64:(h + 1) * 64, g0:g0 + TT_], ao)
    P.release(m)
def run_rr(gens):
    gens = list(gens)
    while gens:
        nxt = []
        for g in gens:
            try:
                next(g)
                nxt.append(g)
            except StopIteration:
                pass
        gens = nxt


def phaseC(P, cfg, PS, l, W, vec, XT, S, consts):
    import os
    CT = float(os.environ.get("CT", "99"))
    TC = 256
    NCH = TC // CH
    m = P.mark()
    Wlu = P.sb("Wlu", [128, 1, C], BF16)
    Alu = P.sb("Alu", [128, 1, C], BF16)
    Glu = P.sb("Glu", [128, 1, C], BF16)
    if l > 0:
        Vld = P.sb("Vld", [128, 8, 128], BF16)
        Vlu = P.sb("Vlu", [128, 1, C], BF16)
    ms_ = P.mark()
    stage = P.sbpool("stg", [128, 2048], F32, 2)
    load_w(P, Wlu, W["w_lu"], 128, C, stage)
    load_w(P, Alu, W["a_lu"], 128, C, stage)
    load_w(P, Glu, W["g_lu"], 128, C, stage)
    if l > 0:
        P.memset("pool", Vld, 0.0)
        P.memset("pool", Vlu, 0.0)
        load_w(P, Vld[:, :, 0:32], W["v_ld"], D, 32, stage)
        load_w(P, Vlu, W["v_lu"], 32, C, stage)
    P.release(ms_)
    masks = P.sb("cmask", [128, 4, 2, 128], F32)
    P.dma(masks, consts["masks_d"])
    bdf = P.sb("cbdf", [128, 128], F32)
    P.dma(bdf, consts["bd_d"])
    bdb = P.sb("cbdb", [128, 128], BF16)
    P.copy("pool", bdb, bdf)
    bd64 = P.sb("cbd64", [128, 128], F32)
    P.ts("dve", bd64, bdf, 1.0 / 64, ALU.mult)
    id2 = P.sb("cid2", [128, 2, 128], F32)
    P.copy("pool", id2[:, 0, :], consts["ident"])
    P.copy("pool", id2[:, 1, :], consts["ident"])
    idb = P.sb("cidb", [128, 128], BF16)
    P.copy("pool", idb, consts["ident"])
    rmask = P.sb("crm", [128, TC], F32)
    P.dma(rmask, consts["rmask_d"][:, 0:TC])
    mo, _ = VOFF["mu"]
    om = P.sb("com", [128, 15], F32)
    hm = P.sb("chm", [128, 15], F32)
    P.ts("dve", om, vec[:, mo:mo + 15], -1.0, ALU.mult, 1.0, ALU.add)
    P.ts("dve", hm, vec[:, mo:mo + 15], 0.5, ALU.mult)
    eps12 = consts["eps_12"]
    epsg = consts["eps_gn"]

    zt_pool = P.sbpool("czt", [128, 3, TC + 2], F32, 2)
    zs_pool = P.sbpool("czs", [128, 15, TC], F32, 1)
    f_pool = P.sbpool("cf", [128, TC], F32, 4)
    fcb = [P.sbpool("cfc%d" % i, [128, TC], F32, 11) for i in range(4)]
    nt_pool = P.sbpool("cnt", [128, 4], F32, 8)
    bcb = [P.sbpool("cbc%d" % i, [128, TC], BF16, 2) for i in range(4)]
    ops_pool = P.sbpool("cops", [128, 4, 6, TC], BF16, 2)
    vb_pool = P.sbpool("cvb", [128, 4, TC], BF16, 2)
    pl_pool = P.sbpool("cpl", [128, 4, 4], F32, 3)
    yt_pool = P.sbpool("cyt", [128, 4, TC], F32, 2)
    sg_pool = P.sbpool("csg", [128, TC], BF16, 2)
    bon_pool = P.sbpool("cbon", [128, 4, TC], F32, 2)
    if l > 0:
        xh_pool = P.sbpool("cxh", [128, 2, TC], F32, 1)
        xb_pool = P.sbpool("cxb", [128, 8, TC], BF16, 1)
        vf_pool = P.sbpool("cvf", [128, 4, TC], F32, 1)
        xd_pool = P.sbpool("cxd", [128, TC], BF16, 1)
    tok_pool = P.sbpool("utok", [128, 4, 128], BF16, 8)
    mt_pool = P.sbpool("umt", [128, 6, 256], BF16, 4)
    mk_pool = P.sbpool("umk", [128, 256], BF16, 5)
    s_pool = P.sbpool("us", [128, 256], BF16, 5)
    nk_pool = P.sbpool("unk", [128, 256], BF16, 4)
    mr_pool = P.sbpool("umr", [128, 2, 256], BF16, 8)
    x1_pool = P.sbpool("ux1", [128, 128], BF16, 4)
    ut_pool = P.sbpool("uut", [128, 128], F32, 8)
    wt_pool = P.sbpool("uwt", [128, 128], BF16, 8)
    u_pool = P.sbpool("uu", [128, 128], BF16, 4)
    th_pool = P.sbpool("uth", [128, 128], F32, 4)
    pad_pools = []
    for nm in ("upB", "upA", "upR"):
        pp_ = P.sbpool(nm, [128, 2, 128], BF16, 4)
        for t_ in pp_.tiles:
            P.memset("pool", t_, 0.0)
        pad_pools.append(pp_)
    tzp = P.sb("ctzp", [128, TC], BF16)
    zap = P.sb("czap", [128, TC], BF16)
    f2_pool = P.sbpool("cf2", [128, TC], F32, 4)
    o_pool = P.sbpool("co", [128, TC], BF16, 2)
    Hf = [P.sb("Hf%d" % i, [128, 128], F32) for i in range(4)]
    Hb = [P.sb("Hb%d" % i, [128, 128], BF16) for i in range(4)]

    print("phaseC layer", l, "sbuf remaining", P.nc.sbuf_bytes_remaining)
    Zv = S["Z"].rr("(b p) t -> p b t", p=128)
    XTv = XT.rr("(kc p) t -> p kc t", p=128)
    YFv = S["YF"].rr("(b p) t -> p b t", p=128)
    BONv = S["BON"].rr("(b p) t -> p b t", p=128)
    VFv = S["VF"].rr("(b p) t -> p b t", p=128)
    RWv = S["RW"].rr("(b p) t -> p b t", p=128)
    V = VOFF
    mul, add, sub = ALU.mult, ALU.add, ALU.subtract

    def c1(si, g0, t0, d, res):
        T = cfg.seqs[si]
        zs = zs_pool.next()
        for gb in range(5):
            zt = zt_pool.next()
            P.dma(zt[:, :, 1:TC + 1], Zv[:, 3 * gb:3 * gb + 3, g0:g0 + TC])
            if t0 == 0:
                P.memset("pool", zt[:, :, 0:1], 0.0)
            else:
                P.dma(zt[:, :, 0:1], Zv[:, 3 * gb:3 * gb + 3, g0 - 1:g0])
            if t0 + TC >= T:
                P.memset("pool", zt[:, :, TC + 1:TC + 2], 0.0)
            else:
                P.dma(zt[:, :, TC + 1:TC + 2], Zv[:, 3 * gb:3 * gb + 3, g0 + TC:g0 + TC + 1])
            for q in range(3):
                b = 3 * gb + q
                t = f_pool.next()
                P.tt("dve", t, zt[:, q, 0:TC], zt[:, q, 2:TC + 2], add)
                P.act(zs[:, b, :], zt[:, q, 1:TC + 1], AF.Identity, scale=om[:, b:b + 1])
                P.stt(zs[:, b, :], t, hm[:, b:b + 1], zs[:, b, :], mul, add)
            yield
        hs = slice(64 * d, 64 * d + 64)
        tz = tzp
        P.act(tz[hs, :], zs[hs, 12, :], AF.Tanh)
        zab = zap
        P.copy("act", zab[hs, :], zs[hs, 13, :])
        if l > 0:
            xb = xb_pool.next()
            for hh in range(4):
                xh = xh_pool.next()
                P.dma(xh, XTv[:, 2 * hh:2 * hh + 2, g0:g0 + TC])
                P.copy(("dve", "act")[hh % 2], xb[:, 2 * hh:2 * hh + 2, :], xh)
            ps = PS.next()[:, 0:TC]
            for kc in range(8):
                P.mm(ps, Vld[:, kc, :], xb[:, kc, :], start=(kc == 0), stop=(kc == 7))
            xd = xd_pool.next()
            P.copy("act", xd, ps)
            vf = vf_pool.next()
            P.dma(vf, VFv[:, :, g0:g0 + TC])
            for cb in range(4):
                ps = PS.next()[:, 0:TC]
                P.mm(ps, Vlu[:, 0, cb * 128:(cb + 1) * 128], xd)
                vm = f_pool.next()
                P.act(vm, ps, AF.Sigmoid, bias=vec[:, V["v0"][0] + cb:V["v0"][0] + cb + 1])
                t = f_pool.next()
                P.tt("dve", t, vf[:, cb, :], zs[:, 8 + cb, :], sub)
                P.tt("dve", t, t, vm, mul)
                P.tt("dve", zs[:, 8 + cb, :], zs[:, 8 + cb, :], t, add)
        elif d == 0:
            P.dma(VFv[:, :, g0:g0 + TC], zs[:, 8:12, :])
        vb = vb_pool.next()
        P.copy("act", vb, zs[:, 8:12, :])
        yield
        ops = ops_pool.next()
        pl = pl_pool.next()
        gt = None
        bon = bon_pool.next()
        if d == 1:
            gt = sg_pool.next()
            P.act(gt, zs[:, 14, :], AF.Sigmoid)
        def chain(cb, f_pool, b_pool):
            r = zs[:, cb, :]
            k = zs[:, 4 + cb, :]
            v = zs[:, 8 + cb, :]
            col = lambda n: vec[:, V[n][0] + cb:V[n][0] + cb + 1]
            cold = lambda n: vec[:, V[n][0] + 4 * d + cb:V[n][0] + 4 * d + cb + 1]
            ps = PS.next()[:, 0:TC]
            P.mm(ps, Wlu[:, 0, cb * 128:(cb + 1) * 128], tz)
            lw = f_pool.next()
            P.act(lw, ps, AF.Sigmoid, bias=cold("w0"))
            yield
            P.act(lw, lw, AF.Identity, scale=-DECAY_SCALE)
            ps = PS.next()[:, 0:TC]
            P.mm(ps, Alu[:, 0, cb * 128:(cb + 1) * 128], zab)
            a = f_pool.next()
            P.act(a, ps, AF.Sigmoid, bias=cold("a0"))
            yield
            kkr = f_pool.next()
            P.act(kkr, k, AF.Identity, scale=col("kk"))
            sq = b_pool.next()
            P.act(sq, kkr, AF.Square)
            yield
            ps = PS.next()[:, 0:TC]
            P.mm(ps, bdb, sq)
            rs = f_pool.next()
            P.act(rs, ps, AF.Sqrt, bias=eps12, scale=1.0)
            yield
            P.recip(rs, rs)
            kk = kkr
            P.tt("dve", kk, kkr, rs, mul)
            kd = f_pool.next()
            P.ts("dve", kd, a, -1.0, add, col("ka"), mul)
            P.stt(kd, kd, 1.0, k, add, mul)
            yield
            kka = rs
            P.tt("dve", kka, kk, a, mul)
            yield
            t = f_pool.next()
            P.stt(t, r, col("rk"), kd, mul, mul)
            tb16 = b_pool.next()
            P.copy("act", tb16, t)
            yield
            ps = PS.next()[:, 0:TC]
            P.mm(ps, bdb, tb16)
            P.tt("dve", bon[:, cb, :], ps, v, mul)
            yield
            cum = f_pool.next()
            P.scan(cum, rmask, lw, 0.0, mul, add)
            yield
            E = lw
            P.tt("dve", E, cum, lw, sub)
            tot = cum.rr("p (c q) -> p c q", q=CH)[:, :, CH - 1]
            P.act(pl[:, cb, 0:NCH], tot, AF.Exp)
            ntot = nt_pool.next()
            P.ts("dve", ntot[:, 0:NCH], tot, -1.0, mul)
            yield
            pincl = f_pool.next()
            pexcl = f_pool.next()
            pinv = f_pool.next()
            pinv2 = a
            pinv2 = f_pool.next()
            if d == 0:
                P.act(pincl, cum, AF.Exp)
                P.act(pexcl, E, AF.Exp)
                P.act(pinv, cum, AF.Exp, scale=-1.0)
                for c in range(NCH):
                    cs = slice(c * CH, (c + 1) * CH)
                    P.act(pinv2[:, cs], cum[:, cs], AF.Exp, bias=cum[:, c * CH + CH - 1:c * CH + CH], scale=-1.0)
            else:
                P.act(pinv2, E, AF.Exp)
                for c in range(NCH):
                    cs = slice(c * CH, (c + 1) * CH)
                    tc_ = cum[:, c * CH + CH - 1:c * CH + CH]
                    P.act(pincl[:, cs], E[:, cs], AF.Exp, bias=tc_, scale=-1.0)
                    P.act(pexcl[:, cs], cum[:, cs], AF.Exp, bias=tc_, scale=-1.0)
                    P.act(pinv[:, cs], E[:, cs], AF.Exp, bias=ntot[:, c:c + 1], scale=1.0)
            yield
            P.tt("dve", ops[:, cb, 0, :], kk, pexcl, mul)
            P.tt("dve", ops[:, cb, 1, :], kka, pinv, mul)
            P.tt("dve", ops[:, cb, 2, :], kd, pinv, mul)
            P.tt("dve", ops[:, cb, 3, :], r, pincl, mul)
            P.tt("dve", ops[:, cb, 4, :], kka, pinv2, mul)
            P.tt("dve", ops[:, cb, 5, :], kd, pinv2, mul)
            yield
        yield from rr_gen([chain(cb, fcb[cb], bcb[cb]) for cb in range(4)])
        res.update(zs=zs, vb=vb, ops=ops, pl=pl, gt=gt, bon=bon)

    def unit_pre(d, c, cb, ops, vb, res):
        ms, msT, mi = (0, 2, 1) if d == 0 else (2, 0, 3)
        cs = slice(c * CH, (c + 1) * CH)
        Bt, At, Kt, Rt, At2, Kt2 = [ops[:, cb, i, cs] for i in range(6)]
        hp = [slice(0, 64), slice(64, 128)]
        m2 = lambda i: masks[:, i, :, :].rr("p a b -> p (a b)")
        pst = PS.next()
        pstb = TT(pst.ap.bitcast(BF16), pst.buf)
        for i, src in enumerate([Bt, At2, Kt2, vb[:, cb, cs]]):
            P.tr(pstb[:, i * 128:(i + 1) * 128], src, idb)
        tok = tok_pool.next()
        P.copy("act", tok.rr("p a b -> p (a b)"), pstb[:, 0:512])
        yield
        if CT <= 1.1:
            return
        pB, pA, pR = [pp_.next() for pp_ in pad_pools]
        for h in range(2):
            P.copy("act", pB[hp[h], h, :], Bt[hp[h], :])
            P.copy("dve", pA[hp[h], h, :], At[hp[h], :])
            P.copy(("act", "dve")[h], pR[hp[h], h, :], Rt[hp[h], :])
        f2 = lambda t_: t_.rr("p a b -> p (a b)")
        psN = PS.next()
        psT = PS.next()
        P.mm(psN[:, 0:256], At, f2(pB))
        P.mm(psT[:, 0:256], Bt, f2(pA))
        mt = mt_pool.next()
        mk = mk_pool.next()
        P.stt(mk, psN[:, 0:256], -1.0, m2(ms), mul, mul)
        P.stt(mt[:, 0, :], psT[:, 0:256], -1.0, m2(msT), mul, mul)
        yield
        if CT <= 1.2:
            return
        psK = PS.next()
        psA = PS.next()
        P.mm(psK[:, 0:256], Kt, f2(pB))
        P.mm(psA[:, 0:256], At, f2(pR))
        P.mm(psA[:, 256:512], Kt, f2(pR))
        nk = nk_pool.next()
        P.tt("dve", nk, psK[:, 0:256], m2(ms), mul)
        mr = mr_pool.next()
        P.tt("dve", mr[:, 0, :], psA[:, 0:256], m2(mi), mul)
        P.tt("dve", mr[:, 1, :], psA[:, 256:512], m2(mi), mul)
        yield
        if CT <= 1.3:
            return
        psX = PS.next()
        for h in range(2):
            P.mm(psX[:, h * 64:(h + 1) * 64], nk[:, h * 128:(h + 1) * 128], tok[:, 3, h * 64:(h + 1) * 64])
        x1 = x1_pool.next()
        P.copy("act", x1, psX[:, 0:128])
        yield
        if CT <= 1.4:
            return
        sb_ = None
        for kk_ in range(1, 7):
            psM = PS.next()
            for h in range(2):
                hs_ = slice(h * 128, (h + 1) * 128)
                P.mm(psM[:, hs_], mt[:, kk_ - 1, hs_], mk[:, hs_])
            if kk_ <= 5:
                psMT = PS.next()
                for h in range(2):
                    hs_ = slice(h * 128, (h + 1) * 128)
                    P.mm(psMT[:, hs_], mk[:, hs_], mt[:, kk_ - 1, hs_])
                mk = mk_pool.next()
                P.copy("act", mk, psM[:, 0:256])
                P.copy("dve", mt[:, kk_, :], psMT[:, 0:256])
            else:
                sb_ = s_pool.next()
                P.tt("dve", sb_, psM[:, 0:256], id2.rr("p a b -> p (a b)"), add)
            yield
        if CT <= 1.5:
            return
        for kk_ in range(5, -1, -1):
            psS = PS.next()
            for h in range(2):
                hs_ = slice(h * 128, (h + 1) * 128)
                P.mm(psS[:, hs_], mt[:, kk_, hs_], sb_[:, hs_])
            s2 = s_pool.next()
            P.tt("dve", s2, psS[:, 0:256], sb_, add)
            sb_ = s2
            yield
        psU = PS.next()
        for h in range(2):
            P.mm(psU[:, h * 64:(h + 1) * 64], sb_[:, h * 128:(h + 1) * 128], x1[:, h * 64:(h + 1) * 64])
        P.mm(psU[:, 128:384], tok[:, 0, :], sb_)
        ut = ut_pool.next()
        P.copy("act", ut, psU[:, 0:128])
        wt = wt_pool.next()
        P.copy("act", wt[0:64, :], psU[0:64, 128:256])
        P.copy("act", wt[64:128, :], psU[64:128, 256:384])
        res.update(tok=tok, mr=mr, ut=ut, wt=wt, Rt=Rt)
        yield

    def unit_seq(d, c, cb, pre, pl, yt):
        tok, mr, ut, wt, Rt = pre["tok"], pre["mr"], pre["ut"], pre["wt"], pre["Rt"]
        cs = slice(c * CH, (c + 1) * CH)
        psU = PS.next()
        P.mm(psU[:, 0:128], wt, Hb[cb])
        u = u_pool.next()
        P.stt(u, psU[:, 0:128], -1.0, ut, mul, sub)
        yield
        psO = PS.next()
        P.mm(psO[:, 0:256], tok[:, 3, :], mr[:, 1, :], start=True, stop=False)
        P.mm(psO[:, 0:256], u, mr[:, 0, :], start=False, stop=False)
        P.mm(psO[:, 0:128], Hb[cb], Rt, start=False, stop=False)
        P.mm(psO[:, 128:256], Hb[cb], Rt, start=False, stop=True)
        psH = PS.next()
        P.mm(psH[:, 0:128], tok[:, 2, :], tok[:, 3, :], start=True, stop=False)
        P.mm(psH[:, 0:128], tok[:, 1, :], u, start=False, stop=True)
        P.copy("act", yt[0:64, cb, cs], psO[0:64, 0:128])
        P.copy("act", yt[64:128, cb, cs], psO[64:128, 128:256])
        th = th_pool.next()
        P.tt("dve", th, psH[:, 0:128], bdf, mul)
        P.stt(Hf[cb], Hf[cb], pl[:, cb, c:c + 1], th, mul, add)
        P.copy("act", Hb[cb], Hf[cb])
        yield

    import os
    def rr_gen(gens):
        gens = list(gens)
        while gens:
            nxt = []
            for g in gens:
                try:
                    next(g)
                    nxt.append(g)
                except StopIteration:
                    pass
            gens = nxt
            yield

    def tile_units(d, g0, R):
        ops, vb, pl, gt, bon = R["ops"], R["vb"], R["pl"], R["gt"], R["bon"]
        yt = yt_pool.next()
        corder = list(range(NCH)) if d == 0 else list(range(NCH - 1, -1, -1))
        pres = {c: [dict() for _ in range(4)] for c in corder}
        yield from rr_gen([unit_pre(d, corder[0], cb, ops, vb, pres[corder[0]][cb]) for cb in range(4)])
        for idx, c in enumerate(corder):
            gl = [unit_seq(d, c, cb, pres[c][cb], pl, yt) for cb in range(4)]
            if idx + 1 < len(corder):
                cn = corder[idx + 1]
                gl += [unit_pre(d, cn, cb, ops, vb, pres[cn][cb]) for cb in range(4)]
            yield from rr_gen(gl)
        if d == 0:
            P.dma(YFv[:, :, g0:g0 + TC], yt)
            P.dma(BONv[:, :, g0:g0 + TC], bon)
        else:
            for cb in range(4):
                y0 = f2_pool.next()
                P.dma(y0, YFv[:, cb, g0:g0 + TC])
                b0 = f2_pool.next()
                P.dma(b0, BONv[:, cb, g0:g0 + TC])
                y = yt[:, cb, :]
                P.tt("dve", y, y, y0, add)
                P.tt("dve", b0, b0, bon[:, cb, :], add)
                pm = PS.next()
                P.mm(pm[:, 0:TC], bd64, y)
                sq = f2_pool.next()
                P.act(sq, y, AF.Square)
                pq = PS.next()
                P.mm(pq[:, 0:TC], bd64, sq)
                msq = f2_pool.next()
                P.act(msq, pm[:, 0:TC], AF.Square)
                var = sq
                P.tt("dve", var, pq[:, 0:TC], msq, sub)
                P.act(var, var, AF.Sqrt, bias=epsg, scale=1.0)
                P.recip(var, var)
                P.tt("dve", y, y, pm[:, 0:TC], sub)
                P.tt("dve", y, y, var, mul)
                P.ts("dve", y, y, vec[:, V["lng"][0] + cb:V["lng"][0] + cb + 1], mul,
                     vec[:, V["lnb"][0] + cb:V["lnb"][0] + cb + 1], add)
                P.tt("dve", y, y, b0, add)
                o = o_pool.next()
                pg = PS.next()
                P.mm(pg[:, 0:TC], Glu[:, 0, cb * 128:(cb + 1) * 128], gt)
                P.tt("dve", o, pg[:, 0:TC], y, mul)
                P.dma(RWv[:, cb, g0:g0 + TC], o)
                yield

    for d in range(2):
        P.memset("pool", tzp, 0.0)
        P.memset("pool", zap, 0.0)
        for si, T in enumerate(cfg.seqs):
            for cb in range(4):
                P.memset("pool", Hf[cb], 0.0)
                P.memset("pool", Hb[cb], 0.0)
            nt = T // TC
            order = list(range(nt)) if d == 0 else list(range(nt - 1, -1, -1))
            prev = None
            for j in order + [None]:
                gens = []
                R = None
                if j is not None:
                    t0 = j * TC
                    g0 = cfg.off[si] + t0
                    R = {"g0": g0}
                    gens.append(c1(si, g0, t0, d, R))
                if prev is not None:
                    gens.append(tile_units(d, prev["g0"], prev))
                run_rr(gens)
                prev = R
    P.release(m)
def layernorm(P, PS, y, gname, vec, ones32s, epst, sqp, stp, tp, emit):
    pm = PS.next()
    pq = PS.next()
    for b in range(8):
        P.mm(pm, ones32s, y[:, b, :], start=(b == 0), stop=(b == 7))
    for b in range(8):
        sq = sqp.next()
        P.act(sq, y[:, b, :], AF.Square)
        P.mm(pq, ones32s, sq, start=(b == 0), stop=(b == 7))
    mean = stp.next()
    P.copy("act", mean, pm)
    msq = stp.next()
    P.act(msq, pm, AF.Square)
    var = stp.next()
    P.tt("dve", var, pq, msq, ALU.subtract)
    P.act(var, var, AF.Sqrt, bias=epst, scale=1.0)
    P.recip(var, var)
    g0 = VOFF[gname + "g"][0]
    b0 = VOFF[gname + "b"][0]
    for b in range(8):
        t = tp.next()
        P.tt("dve", t, y[:, b, :], mean, ALU.subtract)
        P.tt("dve", t, t, var, ALU.mult)
        P.act(t, t, AF.Identity, bias=vec[:, b0 + b:b0 + b + 1], scale=vec[:, g0 + b:g0 + b + 1])
        emit(b, t)


def phaseD1(P, cfg, PS, l, W, vec, XT, S, consts):
    m = P.mark()
    Wpa = P.sb("Wpa", [128, 4, D], BF16)
    Wpb = P.sb("Wpb", [128, 4, D], BF16)
    Wo = P.sb("Wo", [128, 8, D], BF16)
    ms_ = P.mark()
    stage = P.sbpool("stg", [128, 2048], F32, 2)
    load_w(P, Wpa, W["w_pa"], C, D, stage)
    load_w(P, Wpb, W["w_pb"], C, D, stage)
    load_w(P, Wo, W["w_o"], D, D, stage)
    P.release(ms_)
    at_pool = P.sbpool("dat", [128, 8, TT_], BF16, 2)
    g_pool = P.sbpool("dg", [128, 16, TT_], F32, 1)
    xf_pool = P.sbpool("dxf", [128, 8, TT_], F32, 2)
    mx_pool = P.sbpool("dmx", [128, 8, TT_], BF16, 1)
    y_pool = P.sbpool("dy", [128, 8, TT_], F32, 2)
    t_pool = P.sbpool("dt", [128, TT_], F32, 4)
    sq_pool = P.sbpool("dsq", [128, TT_], F32, 2)
    st_pool = P.sbpool("dst", [128, TT_], F32, 3)
    XTv = XT.rr("(kc p) t -> p kc t", p=128)
    X1v = S["X1"].rr("(kc p) t -> p kc t", p=128)
    Gv = S["G"].rr("(b p) t -> p b t", p=128)
    ATv = S["ATT"].rr("(b p) t -> p b t", p=128)
    RWv = S["RW"].rr("(b p) t -> p b t", p=128)
    pend1 = None

    def ln_stage(y, g0):
        def emit(b, t):
            P.dma(X1v[:, b, g0:g0 + TT_], t)
        layernorm(P, PS, y, "l1", vec, consts["ones32s"], consts["eps_ln"], sq_pool, st_pool, t_pool, emit)
    for (si, j, g0, t0) in cfg.tiles():
        at = at_pool.next()
        P.dma(at[:, 0:4, :], ATv[:, :, g0:g0 + TT_])
        P.dma(at[:, 4:8, :], RWv[:, :, g0:g0 + TT_])
        g = g_pool.next()
        P.dma(g[:, 0:8, :], Gv[:, 0:8, g0:g0 + TT_])
        P.dma(g[:, 8:16, :], Gv[:, 8:16, g0:g0 + TT_])
        xf = xf_pool.next()
        P.dma(xf, XTv[:, :, g0:g0 + TT_])
        mx = mx_pool.next()
        for mb in range(8):
            pa = PS.next()
            for kc in range(4):
                P.mm(pa, Wpa[:, kc, mb * 128:(mb + 1) * 128], at[:, kc, :], start=(kc == 0), stop=(kc == 3))
            pb = PS.next()
            for kc in range(4):
                P.mm(pb, Wpb[:, kc, mb * 128:(mb + 1) * 128], at[:, 4 + kc, :], start=(kc == 0), stop=(kc == 3))
            t1 = t_pool.next()
            t2 = t_pool.next()
            P.tt("dve", t1, pa, g[:, mb, :], ALU.mult)
            P.tt("dve", t2, pb, g[:, 8 + mb, :], ALU.mult)
            P.tt("dve", mx[:, mb, :], t1, t2, ALU.add)
        y = y_pool.next()
        for mb in range(8):
            po = PS.next()
            for kc in range(8):
                P.mm(po, Wo[:, kc, mb * 128:(mb + 1) * 128], mx[:, kc, :], start=(kc == 0), stop=(kc == 7))
            P.stt(y[:, mb, :], xf[:, mb, :], ALPHA, po, ALU.mult, ALU.add)
        if pend1 is not None:
            ln_stage(*pend1)
        pend1 = (y, g0)
    if pend1 is not None:
        ln_stage(*pend1)
    P.release(m)


def phaseD2a(P, cfg, PS, l, W, vec, S, consts):
    m = P.mark()
    Wup = P.sb("Wup", [128, 8, 2 * DFF], BF16)
    Wdn = P.sb("Wdn", [128, 22, D], BF16)
    ms_ = P.mark()
    stage = P.sbpool("stg", [128, 2048], F32, 2)
    load_w(P, Wup, W["w_ffn_up"], D, 2 * DFF, stage)
    load_w(P, Wdn, W["w_ffn_down"], DFF, D, stage)
    P.release(ms_)
    xf_pool = P.sbpool("exf", [128, 8, TT_], F32, 1)
    xb_pool = P.sbpool("exb", [128, 8, TT_], BF16, 1)
    xh_pool = P.sbpool("exh", [128, 8, 2], F32, 2)
    xhb_pool = P.sbpool("exhb", [128, 8, 2], BF16, 2)
    uh_pool = P.sbpool("euh", [128, 44, 2], F32, 2)
    ue_pool = P.sbpool("eue", [128, TT_ + 2], F32, 3)
    yc_pool = P.sbpool("eyc", [128, TT_], F32, 3)
    s_pool = P.sbpool("es", [128, TT_], F32, 3)
    gb_pool = P.sbpool("egb", [128, 22, TT_], BF16, 1)
    o_pool = P.sbpool("eo", [128, TT_], F32, 2)
    X1v = S["X1"].rr("(kc p) t -> p kc t", p=128)
    Y2v = S["Y2"].rr("(kc p) t -> p kc t", p=128)
    cw = [VOFF["cw0"][0], VOFF["cw1"][0], VOFF["cw2"][0]]
    cb = VOFF["cb"][0]
    for (si, j, g0, t0) in cfg.tiles():
        T = cfg.seqs[si]
        xf = xf_pool.next()
        P.dma(xf, X1v[:, :, g0:g0 + TT_])
        xh = xh_pool.next()
        if t0 == 0:
            P.memset("pool", xh[:, :, 0:1], 0.0)
        else:
            P.dma(xh[:, :, 0:1], X1v[:, :, g0 - 1:g0])
        if t0 + TT_ >= T:
            P.memset("pool", xh[:, :, 1:2], 0.0)
        else:
            P.dma(xh[:, :, 1:2], X1v[:, :, g0 + TT_:g0 + TT_ + 1])
        xb = xb_pool.next()
        P.copy("dve", xb[:, 0:4, :], xf[:, 0:4, :])
        P.copy("act", xb[:, 4:8, :], xf[:, 4:8, :])
        xhb = xhb_pool.next()
        P.copy("pool", xhb, xh)
        psh = PS.next()
        for b in range(44):
            for kc in range(8):
                P.mm(psh[:, 2 * b:2 * b + 2], Wup[:, kc, b * 128:(b + 1) * 128], xhb[:, kc, :], start=(kc == 0), stop=(kc == 7))
        uh = uh_pool.next()
        P.copy("dve", uh.rr("p b c -> p (b c)"), psh[:, 0:88])
        gb = gb_pool.next()

        def conv_block(b):
            ps = PS.next()
            for kc in range(8):
                P.mm(ps, Wup[:, kc, b * 128:(b + 1) * 128], xb[:, kc, :], start=(kc == 0), stop=(kc == 7))
            ue = ue_pool.next()
            P.act(ue[:, 1:TT_ + 1], ps, AF.Copy)
            P.copy("pool", ue[:, 0:1], uh[:, b, 0:1])
            P.copy("pool", ue[:, TT_ + 1:TT_ + 2], uh[:, b, 1:2])
            yc = yc_pool.next()
            P.act(yc, ue[:, 1:TT_ + 1], AF.Identity, bias=vec[:, cb + b:cb + b + 1], scale=vec[:, cw[1] + b:cw[1] + b + 1])
            P.stt(yc, ue[:, 0:TT_], vec[:, cw[0] + b:cw[0] + b + 1], yc, ALU.mult, ALU.add)
            P.stt(yc, ue[:, 2:TT_ + 2], vec[:, cw[2] + b:cw[2] + b + 1], yc, ALU.mult, ALU.add)
            return yc
        for i in range(22):
            ya = conv_block(i)
            yb = conv_block(22 + i)
            s = s_pool.next()
            P.act(s, ya, AF.Square)
            P.ts("dve", s, s, 0.044715, ALU.mult, 1.0, ALU.add)
            P.tt("pool", s, s, ya, ALU.mult)
            P.act(s, s, AF.Sigmoid, scale=1.5957691216057308)
            P.tt("dve", s, s, ya, ALU.mult)
            P.tt("dve", gb[:, i, :], s, yb, ALU.mult)
        for mb in range(8):
            ps = PS.next()
            for kc in range(22):
                P.mm(ps, Wdn[:, kc, mb * 128:(mb + 1) * 128], gb[:, kc, :], start=(kc == 0), stop=(kc == 21))
            o = o_pool.next()
            P.stt(o, xf[:, mb, :], ALPHA, ps, ALU.mult, ALU.add)
            P.dma(Y2v[:, mb, g0:g0 + TT_], o)
    P.release(m)


def phaseD2b(P, cfg, PS, l, W, vec, XTn, y_out, S, PTl, consts, last):
    m = P.mark()
    Wpg = P.sb("Wpg", [128, 8, D], BF16)
    Wpp = P.sb("Wpp", [128, 2, D], BF16)
    ms_ = P.mark()
    stage = P.sbpool("stg", [128, 2048], F32, 2)
    load_w(P, Wpg, W["w_pe_gate"], D, D, stage)
    load_w(P, Wpp, W["w_pe_proj"], PD, D, stage)
    P.release(ms_)
    xf_pool = P.sbpool("fxf", [128, 8, TT_], F32, 1)
    xb_pool = P.sbpool("fxb", [128, 8, TT_], BF16, 2)
    y_pool = P.sbpool("fy", [128, 8, TT_], F32, 2)
    pt_pool = P.sbpool("fpt", [128, 2, TT_], BF16, 2)
    t_pool = P.sbpool("ft", [128, TT_], F32, 4)
    sq_pool = P.sbpool("fsq", [128, TT_], F32, 2)
    st_pool = P.sbpool("fst", [128, TT_], F32, 3)
    x2_pool = P.sbpool("fx2", [128, 8, TT_], F32, 1)
    yo_pool = P.sbpool("fyo", [128, D], F32, 2)
    X1v = S["X1"].rr("(kc p) t -> p kc t", p=128)
    Y2v = S["Y2"].rr("(kc p) t -> p kc t", p=128)
    PTv = PTl.rr("(kc p) t -> p kc t", p=128)
    if not last:
        XTv = XTn.rr("(kc p) t -> p kc t", p=128)
    kcnt = [0]
    pend2 = None

    def ln2_stage(y, g0):
        if not last:
            def emit(b, t):
                P.dma(XTv[:, b, g0:g0 + TT_], t)
            layernorm(P, PS, y, "l2", vec, consts["ones32s"], consts["eps_ln"], sq_pool, st_pool, t_pool, emit)
        else:
            x2 = x2_pool.next()

            def emit(b, t):
                P.copy("act", x2[:, b, :], t)
            layernorm(P, PS, y, "l2", vec, consts["ones32s"], consts["eps_ln"], sq_pool, st_pool, t_pool, emit)
            for tb in range(4):
                yo = yo_pool.next()
                for half in range(2):
                    ps = PS.next()
                    for q in range(4):
                        kc = half * 4 + q
                        P.tr(ps[:, q * 128:(q + 1) * 128], x2[:, kc, tb * 128:(tb + 1) * 128], consts["ident"])
                    P.copy(("act", "dve")[kcnt[0] % 2], yo[:, half * 512:(half + 1) * 512], ps)
                    kcnt[0] += 1
                P.dma(y_out[g0 + tb * 128:g0 + (tb + 1) * 128, :], yo)
    for (si, j, g0, t0) in cfg.tiles():
        xf = xf_pool.next()
        P.dma(xf, X1v[:, :, g0:g0 + TT_])
        y = y_pool.next()
        P.dma(y, Y2v[:, :, g0:g0 + TT_])
        pt = pt_pool.next()
        P.dma(pt, PTv[:, :, g0:g0 + TT_])
        xb = xb_pool.next()
        P.copy("dve", xb[:, 0:4, :], xf[:, 0:4, :])
        P.copy("act", xb[:, 4:8, :], xf[:, 4:8, :])
        for mb in range(8):
            pg = PS.next()
            for kc in range(8):
                P.mm(pg, Wpg[:, kc, mb * 128:(mb + 1) * 128], xb[:, kc, :], start=(kc == 0), stop=(kc == 7))
            pp = PS.next()
            for kc in range(2):
                P.mm(pp, Wpp[:, kc, mb * 128:(mb + 1) * 128], pt[:, kc, :], start=(kc == 0), stop=(kc == 1))
            sg = t_pool.next()
            P.act(sg, pg, AF.Sigmoid)
            P.tt("dve", sg, pp, sg, ALU.mult)
            P.tt("dve", y[:, mb, :], y[:, mb, :], sg, ALU.add)
        if pend2 is not None:
            ln2_stage(*pend2)
        pend2 = (y, g0)
    if pend2 is not None:
        ln2_stage(*pend2)
    P.release(m)
WKEYS0 = ["w_in", "w_krr", "wq_n", "wq_r", "wq_rr", "wk", "wv", "w_lu", "a_lu", "g_lu", "w_pa", "w_pb", "w_o",
          "w_ffn_up", "w_ffn_down", "w_pe_gate", "w_pe_proj"]
WSHAPES = {"w_in": (D, INC), "w_krr": (D, 32), "wq_n": (QL, 512), "wq_r": (QL, 256), "wq_rr": (QL, 256),
           "wk": (KVL, 512), "wv": (KVL, 512), "w_lu": (128, C), "a_lu": (128, C), "g_lu": (128, C),
           "v_ld": (D, 32), "v_lu": (32, C), "w_pa": (C, D), "w_pb": (C, D), "w_o": (D, D),
           "w_ffn_up": (D, 2 * DFF), "w_ffn_down": (DFF, D), "w_pe_gate": (D, D), "w_pe_proj": (PD, D)}


def wkeys(l):
    return WKEYS0 + (["v_ld", "v_lu"] if l > 0 else [])


def build(cfg):
    nc = bass.Bass("TRN2", target_bir_lowering=False)
    P = Prog(nc)
    NT = cfg.NT
    dbg = cfg.debug

    def din(name, shape, dt=F32):
        return TT(nc.dram_tensor(name, list(shape), dt, kind="ExternalInput").ap(), Buf(name))
    x_in = din("x", [NT, D])
    p_in = din("p", [DEPTH, NT, PD])
    cd = {"cs": din("cs", [128, cfg.Tmax]), "sn": din("sn", [128, cfg.Tmax]), "ident_d": din("ident", [128, 128]),
          "masks_d": din("masks", [128, 4, 2, 128]), "bd_d": din("bd", [128, 128]), "rmask_d": din("rmask", [128, TT_])}
    Wd = []
    vecd = []
    for l in range(DEPTH):
        vecd.append(din("vec_%d" % l, [128, NVEC]))
        Wd.append({k: din("%s_%d" % (k, l), WSHAPES[k]) for k in wkeys(l)})
    y_out = TT(nc.dram_tensor("y", [NT, D], F32, kind="ExternalOutput").ap(), Buf("y"))
    kind = "ExternalOutput" if dbg else "Internal"
    S = {}
    for name, shape, dt in [("XT0", [D, NT], F32), ("XT1", [D, NT], F32), ("PT0", [PD, NT], BF16), ("PT1", [PD, NT], BF16),
                            ("QT", [H, QK, NT], BF16), ("KT", [H, QK, NT], BF16), ("V", [NT, 512], BF16),
                            ("Z", [1920, NT], F32), ("G", [2048, NT], F32), ("ATT", [512, NT], BF16),
                            ("RW", [512, NT], BF16), ("VF", [512, NT], F32), ("YF", [512, NT], F32),
                            ("BON", [512, NT], F32), ("X1", [D, NT], F32), ("Y2", [D, NT], F32)]:
        S[name] = P.dram(name, shape, dt, kind=kind)
    PSall = Pool([P.psum("ps%d" % i, [128, 512], F32) for i in range(8)])
    ident = P.sb("ident_sb", [128, 128], F32)
    P.dma(ident, cd["ident_d"])
    cd["ident"] = ident
    ones32s = P.sb("ones32s", [128, 128], F32)
    P.memset("pool", ones32s, 1.0 / D)
    cd["ones32s"] = ones32s
    for nm, val in [("eps_rms", RMS_EPS), ("eps_ln", LN_EPS), ("eps_12", 1e-12), ("eps_gn", GN_EPS)]:
        t = P.sb(nm, [128, 1], F32)
        P.memset("pool", t, val)
        cd[nm] = t
    vecs = []
    for l in range(DEPTH):
        v = P.sb("vec%d" % l, [128, NVEC], F32)
        P.dma(v, vecd[l])
        vecs.append(v)
    ph = cfg.phases
    XTs = [S["XT0"], S["XT1"]]
    PTs = [S["PT0"], S["PT1"]]
    if "0" in ph:
        phase0(P, cfg, PSall, x_in, p_in, XTs[0], PTs, ident)
    for l in range(cfg.nlayers if hasattr(cfg, "nlayers") else DEPTH):
        XT = XTs[l % 2]
        XTn = XTs[(l + 1) % 2]
        last = (l == DEPTH - 1)
        if "A" in ph:
            phaseA(P, cfg, PSall, l, Wd[l], vecs[l], XT, S, cd)
        if "B" in ph:
            phaseB(P, cfg, PSall, l, S, cd)
        if "C" in ph:
            phaseC(P, cfg, PSall, l, Wd[l], vecs[l], XT, S, cd)
        if "D" in ph:
            phaseD1(P, cfg, PSall, l, Wd[l], vecs[l], XT, S, cd)
        if "E" in ph:
            phaseD2a(P, cfg, PSall, l, Wd[l], vecs[l], S, cd)
            phaseD2b(P, cfg, PSall, l, Wd[l], vecs[l], XTn, y_out, S, PTs[l], cd, last)
    P.finish([y_out])
    return nc, P


def host_inputs(cfg, w, x_cores, p_cores):
    c = host_consts(cfg.Tmax)
    base = {"cs": c["cs"], "sn": c["sn"], "ident": c["ident"],
            "masks": np.ascontiguousarray(np.repeat(c["masks"].reshape(128, 4, 1, 128), 2, axis=2)),
            "bd": c["bd"], "rmask": c["rmask"]}
    for l in range(DEPTH):
        hp = host_layer_params(w, l)
        base["vec_%d" % l] = hp["vec"]
        for k in wkeys(l):
            base["%s_%d" % (k, l)] = np.ascontiguousarray(hp[k], dtype=np.float32)
    maps = []
    for xc, pc in zip(x_cores, p_cores):
        m = dict(base)
        m["x"] = np.ascontiguousarray(xc, dtype=np.float32)
        m["p"] = np.ascontiguousarray(pc, dtype=np.float32)
        maps.append(m)
    return maps


def kernel(**inputs):
    w = {k: np.asarray(v) for k, v in inputs.items()}
    xp, xs, pp, ps_ = w["x_prompt"], w["x_sample"], w["p_prompt"], w["p_sample"]
    n = 8
    Bp, Tp = xp.shape[0], xp.shape[1]
    Bs, Ts = xs.shape[0], xs.shape[1]
    npc = Bp // n
    nsc = Bs // n
    cfg = Cfg([Tp] * npc + [Ts] * nsc)
    x_cores, p_cores = [], []
    for c in range(n):
        xl = [xp[c * npc + i] for i in range(npc)] + [xs[c * nsc + i] for i in range(nsc)]
        pl = [pp[:, c * npc + i] for i in range(npc)] + [ps_[:, c * nsc + i] for i in range(nsc)]
        x_cores.append(np.concatenate(xl, 0))
        p_cores.append(np.concatenate(pl, 1))
    nc, P = build(cfg)
    maps = host_inputs(cfg, w, x_cores, p_cores)
    res = run_bass_kernel_spmd(nc, maps, core_ids=list(range(n)))
    yp = np.zeros(xp.shape, np.float32)
    ys = np.zeros(xs.shape, np.float32)
    for c in range(n):
        y = np.asarray(res.results[c]["y"], np.float32)
        o = 0
        for i in range(npc):
            yp[c * npc + i] = y[o:o + Tp]
            o += Tp
        for i in range(nsc):
            ys[c * nsc + i] = y[o:o + Ts]
            o += Ts
    return (yp, ys)
```
